# Optimizing a Trainium2 kernel written in Bass

```python
import math
import jax, jax.numpy as jnp
from jax import lax
import numpy as np

D_MODEL = 1024
BATCH = 8
SEQ = 2048
DEPTH = 2
DEC_BATCH = 128
DEC_SEQ = 1
PAST_LEN = 16384
PAGE_SIZE = 128

D_MIX = D_MODEL
BRANCH_W = D_MIX // 4
RET_HEADS = 4
RET_HEAD_DIM = BRANCH_W // RET_HEADS
HGRN_HEADS = 4
HGRN_HEAD_DIM = BRANCH_W // HGRN_HEADS
S5_CH_PER_GROUP = 16
S5_GROUPS = BRANCH_W // S5_CH_PER_GROUP
S5_STATE = 64
M2_HEAD_DIM = 64
M2_HEADS = BRANCH_W // M2_HEAD_DIM
M2_GROUPS = 2
M2_STATE = 64
M2_CONV = 4
M2_CONV_CH = BRANCH_W + 2 * M2_GROUPS * M2_STATE
CHUNK = 64
ROPE_BASE = 10000.0
EPS = 1e-6
EXP_CLIP = 60.0
PROJ_SIZES = (BRANCH_W, BRANCH_W, BRANCH_W, BRANCH_W,
              BRANCH_W, BRANCH_W, BRANCH_W, BRANCH_W,
              BRANCH_W, BRANCH_W,
              BRANCH_W, M2_CONV_CH, M2_HEADS)
PROJ_TOTAL = sum(PROJ_SIZES)

kernel_name = "hybrid_ret_hgrn2_s5_ssd_step"


def rms_norm(x, w):
    xf = x.astype(jnp.float32)
    y = xf * lax.rsqrt(jnp.mean(xf * xf, axis=-1, keepdims=True) + EPS)
    return (y * w.astype(jnp.float32)).astype(x.dtype)


def head_rms(o, w):
    o = o * lax.rsqrt(jnp.mean(o * o, axis=-1, keepdims=True) + EPS)
    return o.reshape(o.shape[:2] + (-1,)) * w.astype(jnp.float32)


def rotary(x, pos):
    half = x.shape[-1] // 2
    inv = 1.0 / (ROPE_BASE ** (jnp.arange(half, dtype=jnp.float32) / half))
    ang = pos[:, None] * inv[None, :]
    cos = jnp.cos(ang)[None, :, None, :]
    sin = jnp.sin(ang)[None, :, None, :]
    x1, x2 = x[..., :half], x[..., half:]
    return jnp.concatenate([x1 * cos - x2 * sin, x1 * sin + x2 * cos], axis=-1)


def chunked_linear_recurrence(q, k, v, log_a, h0, chunk):
    bsz, L, H, K = q.shape
    V = v.shape[-1]
    C = min(chunk, L)
    n = -(-L // C)
    pad = n * C - L
    per_channel = log_a.ndim == 4

    def prep(t):
        t = jnp.pad(t, [(0, 0), (0, pad)] + [(0, 0)] * (t.ndim - 2))
        return jnp.moveaxis(t.reshape((bsz, n, C) + t.shape[2:]), 1, 0)

    causal = jnp.tril(jnp.ones((C, C), dtype=bool))

    def masked_exp(diff, mask):
        return jnp.where(mask, jnp.exp(jnp.where(mask, diff, 0.0)), 0.0)

    def step(h, blk):
        qc, kc, vc, la = blk
        cum = jnp.cumsum(la, axis=1)
        total = cum[:, -1]
        diff = cum[:, :, None] - cum[:, None, :]
        if per_channel:
            o_inter = jnp.einsum('bchk,bhkv->bchv', qc * jnp.exp(cum), h)
            decay = masked_exp(diff, causal[None, :, :, None, None])
            scores = jnp.einsum('bthk,bshk,btshk->bhts', qc, kc, decay)
            k_end = kc * jnp.exp(total[:, None] - cum)
            h_new = jnp.exp(total)[..., None] * h + jnp.einsum('bshk,bshv->bhkv', k_end, vc)
        else:
            o_inter = jnp.einsum('bchk,bhkv->bchv', qc, h) * jnp.exp(cum)[..., None]
            decay = masked_exp(diff, causal[None, :, :, None])
            scores = jnp.einsum('bthk,bshk->bhts', qc, kc) * jnp.moveaxis(decay, 3, 1)
            k_end = kc * jnp.exp(total[:, None] - cum)[..., None]
            h_new = jnp.exp(total)[..., None, None] * h + jnp.einsum('bshk,bshv->bhkv', k_end, vc)
        o = o_inter + jnp.einsum('bhts,bshv->bthv', scores, vc)
        return h_new, o

    h_last, o = lax.scan(step, h0, (prep(q), prep(k), prep(v), prep(log_a)))
    o = jnp.moveaxis(o, 0, 1).reshape(bsz, n * C, H, V)[:, :L]
    return o, h_last


def s5_scan(u, h0_re, h0_im, A_re, A_im, log_dt, B_re, B_im, C_re, C_im):
    f32 = jnp.float32
    A_re, A_im = A_re.astype(f32), A_im.astype(f32)
    dt = jnp.exp(log_dt.astype(f32))[:, None]
    mag = jnp.exp(A_re * dt)
    ab_re, ab_im = mag * jnp.cos(A_im * dt), mag * jnp.sin(A_im * dt)
    nr, ni = ab_re - 1.0, ab_im
    den = A_re * A_re + A_im * A_im
    f_re = (nr * A_re + ni * A_im) / den
    f_im = (ni * A_re - nr * A_im) / den
    B_re, B_im = B_re.astype(f32), B_im.astype(f32)
    bb_re = f_re[..., None] * B_re - f_im[..., None] * B_im
    bb_im = f_re[..., None] * B_im + f_im[..., None] * B_re
    bu_re = jnp.einsum('blgc,gpc->blgp', u, bb_re)
    bu_im = jnp.einsum('blgc,gpc->blgp', u, bb_im)
    bu_re = bu_re.at[:, 0].add(ab_re * h0_re - ab_im * h0_im)
    bu_im = bu_im.at[:, 0].add(ab_re * h0_im + ab_im * h0_re)
    a_re = jnp.broadcast_to(ab_re, bu_re.shape)
    a_im = jnp.broadcast_to(ab_im, bu_im.shape)

    def combine(e1, e2):
        a1r, a1i, b1r, b1i = e1
        a2r, a2i, b2r, b2i = e2
        return (a2r * a1r - a2i * a1i, a2r * a1i + a2i * a1r,
                a2r * b1r - a2i * b1i + b2r, a2r * b1i + a2i * b1r + b2i)

    _, _, h_re, h_im = lax.associative_scan(combine, (a_re, a_im, bu_re, bu_im), axis=1)
    y = (jnp.einsum('blgp,gcp->blgc', h_re, C_re.astype(f32))
         - jnp.einsum('blgp,gcp->blgc', h_im, C_im.astype(f32)))
    return y, h_re[:, -1], h_im[:, -1]


def causal_conv(xbc, buf, w, b):
    L = xbc.shape[1]
    full = jnp.concatenate([buf, xbc], axis=1)
    out = b.astype(jnp.float32) + sum(full[:, i:i + L] * w[i].astype(jnp.float32)
                                      for i in range(M2_CONV))
    return jax.nn.silu(out), full[:, -(M2_CONV - 1):]


def hybrid_layer(x, pos0, ret_h0, hgrn_h0, s5_h0_re, s5_h0_im, m2_h0, m2_buf0, lb,
                 norm_w, w_in, ret_norm_w, hgrn_norm_w,
                 s5_A_re, s5_A_im, s5_log_dt, s5_B_re, s5_B_im, s5_C_re, s5_C_im,
                 s5_D, s5_glu_w, s5_glu_b,
                 m2_conv_w, m2_conv_b, m2_dt_bias, m2_A_log, m2_D, m2_norm_w, w_out):
    f32 = jnp.float32
    bsz, L, _ = x.shape
    hn = rms_norm(x, norm_w)
    proj = (hn @ w_in).astype(f32)
    points = np.cumsum(PROJ_SIZES)[:-1].tolist()
    (r_q, r_k, r_v, r_g, g_q, g_f, g_i, g_g, s_u, s_g, m_z, m_xbc, m_dt) = jnp.split(proj, points, axis=-1)

    pos = pos0 + jnp.arange(L, dtype=f32)
    rq = rotary(r_q.reshape(bsz, L, RET_HEADS, RET_HEAD_DIM), pos)
    rk = rotary(r_k.reshape(bsz, L, RET_HEADS, RET_HEAD_DIM), pos) * (RET_HEAD_DIM ** -0.5)
    rv = r_v.reshape(bsz, L, RET_HEADS, RET_HEAD_DIM)
    log_gamma = jnp.log1p(-(2.0 ** (-5.0 - jnp.arange(RET_HEADS, dtype=f32))))
    r_la = jnp.broadcast_to(log_gamma, (bsz, L, RET_HEADS))
    r_o, ret_h = chunked_linear_recurrence(rq, rk, rv, r_la, ret_h0.astype(f32), CHUNK)
    o_ret = head_rms(r_o, ret_norm_w) * jax.nn.silu(r_g)

    hq = jax.nn.silu(g_q).reshape(bsz, L, HGRN_HEADS, HGRN_HEAD_DIM)
    fr = g_f.reshape(bsz, L, HGRN_HEADS, HGRN_HEAD_DIM)
    lb_h = lb.reshape(HGRN_HEADS, HGRN_HEAD_DIM)
    log_f = jax.nn.log_sigmoid(fr) + jnp.log1p(lb_h * jnp.exp(jnp.minimum(-fr, EXP_CLIP)))
    hk = (1.0 - lb_h) * jax.nn.sigmoid(-fr)
    hv = g_i.reshape(bsz, L, HGRN_HEADS, HGRN_HEAD_DIM)
    h_o, hgrn_h = chunked_linear_recurrence(hq, hk, hv, log_f, hgrn_h0.astype(f32), CHUNK)
    o_hgrn = head_rms(h_o, hgrn_norm_w) * jax.nn.silu(g_g)

    u = s_u.reshape(bsz, L, S5_GROUPS, S5_CH_PER_GROUP)
    sy, s5_re, s5_im = s5_scan(u, s5_h0_re.astype(f32), s5_h0_im.astype(f32), s5_A_re, s5_A_im,
                               s5_log_dt, s5_B_re, s5_B_im, s5_C_re, s5_C_im)
    sy = sy + s5_D.astype(f32).reshape(S5_GROUPS, S5_CH_PER_GROUP) * u
    gy = jax.nn.gelu(sy.reshape(bsz, L, BRANCH_W))
    o_s5 = gy * jax.nn.sigmoid(gy @ s5_glu_w.astype(f32) + s5_glu_b.astype(f32)) * jax.nn.silu(s_g)

    xbc, m2_buf = causal_conv(m_xbc, m2_buf0.astype(f32), m2_conv_w, m2_conv_b)
    gn = M2_GROUPS * M2_STATE
    xm = xbc[..., :BRANCH_W].reshape(bsz, L, M2_HEADS, M2_HEAD_DIM)
    Bm = jnp.repeat(xbc[..., BRANCH_W:BRANCH_W + gn].reshape(bsz, L, M2_GROUPS, M2_STATE),
                    M2_HEADS // M2_GROUPS, axis=2)
    Cm = jnp.repeat(xbc[..., BRANCH_W + gn:].reshape(bsz, L, M2_GROUPS, M2_STATE),
                    M2_HEADS // M2_GROUPS, axis=2)
    dt = jax.nn.softplus(m_dt + m2_dt_bias.astype(f32))
    A = -jnp.exp(m2_A_log.astype(f32))
    m_o, m2_h = chunked_linear_recurrence(Cm, Bm, xm * dt[..., None], dt * A,
                                          m2_h0.astype(f32), CHUNK)
    my = (m_o + m2_D.astype(f32)[:, None] * xm).reshape(bsz, L, BRANCH_W) * jax.nn.silu(m_z)
    o_m2 = my * lax.rsqrt(jnp.mean(my * my, axis=-1, keepdims=True) + EPS) * m2_norm_w.astype(f32)

    mixed = jnp.concatenate([o_ret, o_hgrn, o_s5, o_m2], axis=-1).astype(x.dtype)
    x_out = x + mixed @ w_out
    return x_out, (ret_h, hgrn_h, s5_re, s5_im, m2_h, m2_buf)


def setup_inputs(seed: int = 0) -> dict:
    key = jax.random.key(seed)
    ks = iter(jax.random.split(key, 48))
    f32 = jnp.float32

    def nrm(shape, s):
        return s * jax.random.normal(next(ks), shape, f32)

    def unif(shape, lo, hi):
        return jax.random.uniform(next(ks), shape, f32, lo, hi)

    inp = {}
    inp["x_prompt"] = nrm((BATCH, SEQ, D_MODEL), 1.0)
    inp["x_sample"] = nrm((DEC_BATCH, DEC_SEQ, D_MODEL), 1.0)
    inp["state_ret"] = nrm((DEPTH, DEC_BATCH, RET_HEADS, RET_HEAD_DIM, RET_HEAD_DIM), 0.3)
    inp["state_hgrn"] = nrm((DEPTH, DEC_BATCH, HGRN_HEADS, HGRN_HEAD_DIM, HGRN_HEAD_DIM), 0.5)
    inp["state_s5_re"] = nrm((DEPTH, DEC_BATCH, S5_GROUPS, S5_STATE), 0.1)
    inp["state_s5_im"] = nrm((DEPTH, DEC_BATCH, S5_GROUPS, S5_STATE), 0.1)
    inp["state_m2_ssm"] = nrm((DEPTH, DEC_BATCH, M2_HEADS, M2_STATE, M2_HEAD_DIM), 0.3)
    inp["state_m2_conv"] = nrm((DEPTH, DEC_BATCH, M2_CONV - 1, M2_CONV_CH), 1.0)
    inp["norm_w"] = 1.0 + nrm((DEPTH, D_MODEL), 0.01)
    inp["w_in"] = nrm((DEPTH, D_MODEL, PROJ_TOTAL), D_MODEL ** -0.5)
    inp["ret_norm_w"] = 1.0 + nrm((DEPTH, BRANCH_W), 0.01)
    inp["hgrn_lb_logits"] = 1.0 + nrm((DEPTH, BRANCH_W), 0.1)
    inp["hgrn_norm_w"] = 1.0 + nrm((DEPTH, BRANCH_W), 0.01)
    inp["s5_A_re"] = -0.5 + nrm((DEPTH, S5_GROUPS, S5_STATE), 0.01)
    inp["s5_A_im"] = jnp.pi * jnp.arange(S5_STATE, dtype=f32) + nrm((DEPTH, S5_GROUPS, S5_STATE), 0.01)
    inp["s5_log_dt"] = unif((DEPTH, S5_GROUPS), math.log(1e-3), math.log(1e-1))
    inp["s5_B_re"] = nrm((DEPTH, S5_GROUPS, S5_STATE, S5_CH_PER_GROUP), (2 * S5_CH_PER_GROUP) ** -0.5)
    inp["s5_B_im"] = nrm((DEPTH, S5_GROUPS, S5_STATE, S5_CH_PER_GROUP), (2 * S5_CH_PER_GROUP) ** -0.5)
    inp["s5_C_re"] = nrm((DEPTH, S5_GROUPS, S5_CH_PER_GROUP, S5_STATE), (2 * S5_STATE) ** -0.5)
    inp["s5_C_im"] = nrm((DEPTH, S5_GROUPS, S5_CH_PER_GROUP, S5_STATE), (2 * S5_STATE) ** -0.5)
    inp["s5_D"] = nrm((DEPTH, BRANCH_W), 1.0)
    inp["s5_glu_w"] = nrm((DEPTH, BRANCH_W, BRANCH_W), BRANCH_W ** -0.5)
    inp["s5_glu_b"] = nrm((DEPTH, BRANCH_W), 0.01)
    inp["m2_conv_w"] = nrm((DEPTH, M2_CONV, M2_CONV_CH), M2_CONV ** -0.5)
    inp["m2_conv_b"] = nrm((DEPTH, M2_CONV_CH), 0.01)
    dt0 = jnp.exp(unif((DEPTH, M2_HEADS), math.log(1e-3), math.log(1e-1)))
    inp["m2_dt_bias"] = dt0 + jnp.log(-jnp.expm1(-dt0))
    inp["m2_A_log"] = jnp.log(unif((DEPTH, M2_HEADS), 1.0, 16.0))
    inp["m2_D"] = 1.0 + nrm((DEPTH, M2_HEADS), 0.1)
    inp["m2_norm_w"] = 1.0 + nrm((DEPTH, BRANCH_W), 0.01)
    inp["w_out"] = nrm((DEPTH, D_MIX, D_MODEL), 0.5 * D_MIX ** -0.5)
    inp["final_norm_w"] = 1.0 + nrm((D_MODEL,), 0.01)
    return inp


def reference(x_prompt, x_sample, state_ret, state_hgrn, state_s5_re, state_s5_im,
              state_m2_ssm, state_m2_conv, norm_w, w_in, ret_norm_w, hgrn_lb_logits,
              hgrn_norm_w, s5_A_re, s5_A_im, s5_log_dt, s5_B_re, s5_B_im, s5_C_re,
              s5_C_im, s5_D, s5_glu_w, s5_glu_b, m2_conv_w, m2_conv_b, m2_dt_bias,
              m2_A_log, m2_D, m2_norm_w, w_out, final_norm_w):
    f32 = jnp.float32
    lb_sm = jax.nn.softmax(hgrn_lb_logits.astype(f32), axis=0)
    lb_all = jnp.clip(jnp.cumsum(lb_sm, axis=0) - lb_sm[0], 0.0, 1.0)

    def zero_states(b):
        return (jnp.zeros((b, RET_HEADS, RET_HEAD_DIM, RET_HEAD_DIM), f32),
                jnp.zeros((b, HGRN_HEADS, HGRN_HEAD_DIM, HGRN_HEAD_DIM), f32),
                jnp.zeros((b, S5_GROUPS, S5_STATE), f32),
                jnp.zeros((b, S5_GROUPS, S5_STATE), f32),
                jnp.zeros((b, M2_HEADS, M2_STATE, M2_HEAD_DIM), f32),
                jnp.zeros((b, M2_CONV - 1, M2_CONV_CH), f32))

    xp, xs = x_prompt, x_sample
    new_p, new_s = [], []
    for l in range(DEPTH):
        lw = (norm_w[l], w_in[l], ret_norm_w[l], hgrn_norm_w[l],
              s5_A_re[l], s5_A_im[l], s5_log_dt[l], s5_B_re[l], s5_B_im[l], s5_C_re[l],
              s5_C_im[l], s5_D[l], s5_glu_w[l], s5_glu_b[l],
              m2_conv_w[l], m2_conv_b[l], m2_dt_bias[l], m2_A_log[l], m2_D[l], m2_norm_w[l],
              w_out[l])
        xp, sp = hybrid_layer(xp, 0, *zero_states(xp.shape[0]), lb_all[l], *lw)
        xs, ss = hybrid_layer(xs, PAST_LEN, state_ret[l], state_hgrn[l], state_s5_re[l],
                              state_s5_im[l], state_m2_ssm[l], state_m2_conv[l], lb_all[l], *lw)
        new_p.append(sp)
        new_s.append(ss)

    y_prompt = rms_norm(xp, final_norm_w)
    y_sample = rms_norm(xs, final_norm_w)
    stk = lambda lst, i: jnp.stack([s[i] for s in lst], axis=0)
    return (y_prompt, y_sample,
            stk(new_p, 0), stk(new_p, 1), stk(new_p, 2), stk(new_p, 3), stk(new_p, 4), stk(new_p, 5),
            stk(new_s, 0), stk(new_s, 1), stk(new_s, 2), stk(new_s, 3), stk(new_s, 4), stk(new_s, 5))
```

```python
import numpy as np
import ml_dtypes
from contextlib import ExitStack
import concourse.bass as bass
import concourse.mybir as mybir
from concourse.bass_utils import run_bass_kernel_spmd

F32 = mybir.dt.float32
BF16 = mybir.dt.bfloat16
AF = mybir.ActivationFunctionType
ALU = mybir.AluOpType
AX = mybir.AxisListType

D = 1024
L = 2048
NS = 16
DEPTH = 2
PT = 3332
PAST = 16384
EPS = 1e-6
NB = 4
BL = 512
GAM = [1.0 - 2.0 ** (-5.0 - h) for h in range(4)]


class Sched:
    def __init__(self, nc, es):
        self.nc = nc
        self.ops = []
        self.eng = {"pe": nc.tensor, "act": nc.scalar, "dve": nc.vector, "sp": nc.sync, "pool": nc.gpsimd}
        self.es = es
        self.epoch = 0
        self.sem = {(e, 0): es.enter_context(nc.semaphore("sem_%s_0" % e)) for e in ("pe", "act", "dve")}
        self.nslots = 24
        self.dsem = [es.enter_context(nc.semaphore("dsem%d" % i)) for i in range(self.nslots)]

    def add(self, eng, fn, reads=(), writes=(), dma=False):
        self.ops.append(dict(eng=eng, fn=fn, reads=tuple(reads), writes=tuple(writes), dma=dma))

    def barrier(self):
        for e in ("act", "dve"):
            eng = self.eng[e]
            self.ops.append(dict(eng=e, fn=(lambda eng=eng: eng.nop()), reads=(),
                                 writes=(("bar",) if e == "dve" else ()), dma=False, barrier=True))

    def dma(self, q, out, in_, reads=(), writes=()):
        e = self.eng[q]
        self.add(q, lambda: e.dma_start(out=out, in_=in_), reads, writes, dma=True)

    def emit(self):
        ops = self.ops
        last_w, readers = {}, {}
        deps = []
        needed = [bool(op['dma']) for op in ops]
        last_on = {}
        dma_since = set()
        for i, op in enumerate(ops):
            d = set()
            if op.get("barrier"):
                d |= {j for e2, j in last_on.items() if not (e2 == "pe" and op["eng"] == "pe")}
                d |= dma_since
            for k in op["reads"]:
                if k in last_w:
                    d.add(last_w[k])
            for k in op["writes"]:
                if k in last_w:
                    d.add(last_w[k])
                d |= readers.get(k, set())
            d.discard(i)
            if op["eng"] == "pe":
                d = {j for j in d if not (ops[j]["eng"] == "pe" and not ops[j]["dma"])}
            deps.append(d)
            if op["dma"]:
                dma_since.add(i)
            elif op.get("barrier"):
                if op["eng"] == "dve":
                    dma_since = set()
            else:
                last_on[op["eng"]] = i
            for j in d:
                needed[j] = True
            for k in op["writes"]:
                last_w[k] = i
                readers[k] = set()
            for k in op["reads"]:
                readers.setdefault(k, set()).add(i)
        cnt = {e: 0 for e in ("pe", "act", "dve")}
        token = {}
        waited = {}
        slot_use = [0] * self.nslots
        slot_rr = 0
        for i, op in enumerate(ops):
            e = op["eng"]
            eng = self.eng[e]
            w = waited.setdefault(e, {})
            need = {}
            for j in deps[i]:
                s, v = token[j]
                need[s] = max(need.get(s, 0), v)
            if op["dma"] and needed[i]:
                if e == "pool":
                    slot = len(self.dsem)
                    self.dsem.append(self.es.enter_context(self.nc.semaphore("psem%d" % slot)))
                    slot_use.append(0)
                else:
                    slot = slot_rr
                    slot_rr = (slot_rr + 1) % self.nslots
                    if slot_use[slot] > 0:
                        s = ("d", slot)
                        need[s] = max(need.get(s, 0), 16 * slot_use[slot])
            for s, v in need.items():
                if w.get(s, 0) < v:
                    semh = self.dsem[s[1]] if s[0] == "d" else self.sem[s]
                    eng.wait_ge(semh, v)
                    w[s] = v
            ins = op["fn"]()
            if needed[i]:
                if op["dma"]:
                    slot_use[slot] += 1
                    ins.then_inc(self.dsem[slot], 16)
                    token[i] = (("d", slot), 16 * slot_use[slot])
                else:
                    cnt[e] += 1
                    ins.then_inc(self.sem[(e, self.epoch)], 1)
                    token[i] = ((e, self.epoch), cnt[e])
            if op.get("barrier") and e == "dve" and max(cnt.values()) > 600:
                self.epoch += 1
                for e2 in cnt:
                    cnt[e2] = 0
                    self.sem[(e2, self.epoch)] = self.es.enter_context(
                        self.nc.semaphore("sem_%s_%d" % (e2, self.epoch)))
        return len(ops)


def _bf(a):
    return np.ascontiguousarray(a).astype(ml_dtypes.bfloat16)


def make_consts():
    c = {}
    c["ident_bf"] = _bf(np.eye(128))
    c["ident_f"] = np.eye(128, dtype=np.float32)
    s = np.arange(128)
    dm = np.zeros((128, 4, 128), np.float32)
    qdec = np.zeros((128, 2, 128), np.float32)
    kend = np.zeros((128, 2, 128), np.float32)
    for h in range(4):
        g = GAM[h]
        diff = s[None, :] - s[:, None]
        dm[:, h, :] = np.where(diff >= 0, 0.125 * g ** np.maximum(diff, 0).astype(np.float64), 0.0)
        qdec[(h % 2) * 64:(h % 2) * 64 + 64, h // 2, :] = (g ** (s + 1.0))[None, :]
        kend[:, h // 2, (h % 2) * 64:(h % 2) * 64 + 64] = (0.125 * g ** (127.0 - s))[:, None]
    c["dmT"] = dm
    c["qdec"] = qdec
    c["kend"] = kend
    half = 32
    inv = 1.0 / (10000.0 ** (np.arange(half, dtype=np.float32) / half))
    pos = np.arange(L, dtype=np.float32)
    ang = (pos[None, :] * inv[:, None]).astype(np.float32)
    cs, sn = np.cos(ang).astype(np.float32), np.sin(ang).astype(np.float32)
    c["cosT"] = np.concatenate([cs, cs, cs, cs], 0)
    c["sinT"] = np.concatenate([-sn, sn, -sn, sn], 0)
    angs = (np.float32(PAST) * inv).astype(np.float32)
    c["cos_s"] = np.tile(np.cos(angs).astype(np.float32)[None, :], (64, 1))
    c["sin_s"] = np.tile(np.sin(angs).astype(np.float32)[None, :], (64, 1))
    c["gam_s"] = np.tile(np.array(GAM, np.float32)[None, :], (16, 1)).reshape(64, 1)
    caus = (s[:, None] <= s[None, :])
    same64 = (s[:, None] // 64) == (s[None, :] // 64)
    c["maskBD"] = (caus & same64).astype(np.float32)
    c["mstrBD"] = ((s[:, None] > s[None, :]) & same64).astype(np.float32)
    c["U128"] = caus.astype(np.float32)
    c["M128"] = (s[:, None] > s[None, :]).astype(np.float32)
    c["negm"] = np.where(caus, 0.0, -60000.0).astype(np.float32)
    blk = np.zeros((128, 128), np.float32)
    blk[:64, :64] = 1
    blk[64:, 64:] = 1
    c["blkones"] = _bf(blk)
    c["allones"] = _bf(np.ones((128, 128)))
    c["ones_f"] = np.ones((128, 128), np.float32)
    me = np.zeros((128, 2, 16), np.float32)
    me[:64, 0, :] = 1
    me[64:, 1, :] = 1
    c["maskE"] = me
    c["iotaR"] = np.tile(np.arange(128, dtype=np.float32)[None, :], (128, 1))
    used = ("cos_s", "sin_s", "gam_s", "maskE", "iotaR", "ones_f", "ident_f", "ident_bf", "dmT", "qdec", "kend", "cosT", "sinT", "blkones", "allones", "maskBD", "mstrBD", "U128", "M128", "negm")
    return {k: v for k, v in c.items() if k in used}


CONST_SPECS = None


class Prog:
    def __init__(self, consts):
        self.consts = consts
        self.nc = bass.Bass("TRN2", target_bir_lowering=False)
        self.es = ExitStack()
        self.S = Sched(self.nc, self.es)
        self.din = {}
        self.dout = {}
        self.uid = 0

    def inp(self, name, shape, dt=F32):
        t = self.nc.dram_tensor(name, list(shape), dt, kind="ExternalInput").ap()
        self.din[name] = t
        return t

    def outp(self, name, shape):
        t = self.nc.dram_tensor(name, list(shape), F32, kind="ExternalOutput").ap()
        self.dout[name] = t
        return t

    def sb(self, name, shape, dt=F32):
        return self.es.enter_context(self.nc.sbuf_tensor(name, list(shape), dt))

    def key(self):
        self.uid += 1
        return ("k", self.uid)


def build(consts):
    P = Prog(consts)
    nc, S = P.nc, P.S
    V, A, T = nc.vector, nc.scalar, nc.tensor

    x_in = P.inp("x_prompt", [L, D])
    w_in = P.inp("w_in", [DEPTH, D, PT])
    w_sw = P.inp("w_sw", [DEPTH, D, 512])
    w_out = P.inp("w_out", [DEPTH, D, D])
    smallp = P.inp("smallp", [DEPTH, 128, 96])
    s5b_d = P.inp("s5b", [DEPTH, 128, 2, 8, 16])
    s5c_d = P.inp("s5c", [DEPTH, 128, 2, 8, 16])
    glu_d = P.inp("glu_w", [DEPTH, 256, 256])
    rowp = P.inp("rowp", [DEPTH, 1, 528])
    fin_w = P.inp("fin_w", [1, D])
    xs_in = P.inp("x_sample", [NS, D])
    st_in = {"ret": P.inp("st_ret", [DEPTH, NS, 4, 64, 64]), "hgrn": P.inp("st_hgrn", [DEPTH, NS, 4, 64, 64]),
             "m2": P.inp("st_m2", [DEPTH, NS, 4, 64, 64]), "conv": P.inp("st_conv", [DEPTH, NS, 3, 512]),
             "s5re": P.inp("st_s5re", [DEPTH, NS, 16, 64]), "s5im": P.inp("st_s5im", [DEPTH, NS, 16, 64])}
    rowsS = P.inp("rowsS", [DEPTH, 64, 4, 64])
    rowsM = P.inp("rowsM", [DEPTH, 32, 2])
    rowsT = P.inp("rowsT", [DEPTH, 1, 3840])
    tabF = nc.dram_tensor("s5tabF", [2, 3, 128, 512], F32, kind="Internal").ap()
    tabB = nc.dram_tensor("s5tabB", [2, 4, 128, 512], BF16, kind="Internal").ap()
    scr = {k_: nc.dram_tensor("scr_" + k_, sh, F32, kind="Internal").ap() for k_, sh in
           (("X", [NS, D]), ("P", [13, NS, 256]), ("XBC", [NS, 512]), ("DT", [NS, 4]), ("M", [4, NS, 256]),
            ("XM", [NS, 256]), ("B", [NS, 128]), ("C", [NS, 128]), ("DT2", [NS, 4]), ("DE", [NS, 4]),
            ("3", [NS, 256]), ("4", [NS, 256]))}
    cdram = {}
    for k, v in consts.items():
        cdram[k] = P.inp("c_" + k, v.shape, BF16 if v.dtype == ml_dtypes.bfloat16 else F32)
    y_p = P.outp("y_prompt", [L, D])
    o_ret = P.outp("p_ret", [DEPTH, 4, 64, 64])
    o_hgrn = P.outp("p_hgrn", [DEPTH, 4, 64, 64])
    o_m2 = P.outp("p_m2", [DEPTH, 4, 64, 64])
    o_conv = P.outp("p_conv", [DEPTH, 3, 512])
    o_s5re = P.outp("p_s5re", [DEPTH, 16, 64])
    o_s5im = P.outp("p_s5im", [DEPTH, 16, 64])
    y_s = P.outp("y_sample", [NS, D])
    s_outs = {"s_ret": P.outp("s_ret", [DEPTH, NS, 4, 64, 64]), "s_hgrn": P.outp("s_hgrn", [DEPTH, NS, 4, 64, 64]),
              "s_s5re": P.outp("s_s5re", [DEPTH, NS, 16, 64]), "s_s5im": P.outp("s_s5im", [DEPTH, NS, 16, 64]),
              "s_m2": P.outp("s_m2", [DEPTH, NS, 4, 64, 64]), "s_conv": P.outp("s_conv", [DEPTH, NS, 3, 512])}

    x = P.sb("x", [128, 16, D])
    Win = P.sb("Win", [128, 8, PT], BF16)
    Wsw = P.sb("Wsw", [128, 8, 512], BF16)
    Wout = P.sb("Wout", [128, 8, D], BF16)
    hnT = P.sb("hnT", [128, 8, BL], BF16)
    mixT = P.sb("mixT", [128, 8, BL], BF16)
    sp_t = P.sb("sp_t", [128, 96])
    rp_t = P.sb("rp_t", [128, 528])
    C = {}
    for k, v in consts.items():
        if k in ("cosT", "sinT"):
            continue
        C[k] = P.sb("C_" + k, v.shape, BF16 if v.dtype == ml_dtypes.bfloat16 else F32)
    ss16 = P.sb("ss16", [128, 16])
    rstd = P.sb("rstd", [128, 16])
    hst = {m: P.sb("hst_" + m, [128, 2, 64]) for m in ("ret", "hgrn", "m2")}
    hstbH = {m: [P.sb("hstbH_%s%d" % (m, i), [128, 2, 64], BF16) for i in range(2)] for m in ("ret", "hgrn")}
    hstD_bf = [P.sb("hstDbf%d" % h, [128, 64], BF16) for h in range(4)]
    xhist = P.sb("xhist", [128, 4, 3])
    S5 = {n_: P.sb("s5_" + n_, [128, 8]) for n_ in ("th", "rho", "c1", "s1", "hr", "hi")}
    s5mixS = P.sb("s5mixS", [128, 2, 16], BF16)
    BXc = [P.sb("BXc%d" % i, [128, 8, 32], BF16) for i in range(2)]
    CXc = [P.sb("CXc%d" % i, [128, 8, 32], BF16) for i in range(2)]
    AW = 8500
    arena_t = P.sb("arena", [128, AW])
    arena_f = arena_t[:, :]
    arena_b = arena_t[:, :].bitcast(BF16)
    arena_i = arena_t[:, :].bitcast(mybir.dt.int32)

    class Arena:
        off = 0

        def reset(self):
            self.off = 0

        def _shape(self, ap, shape):
            if len(shape) == 1:
                return ap
            if len(shape) == 2:
                return ap.rearrange("p (a b) -> p a b", a=shape[0])
            return ap.rearrange("p (a b c) -> p a b c", a=shape[0], b=shape[1])

        def f32(self, *shape):
            n = int(np.prod(shape))
            ap = arena_f[:, self.off:self.off + n]
            self.off += n
            assert self.off <= AW, self.off
            return self._shape(ap, shape)

        def i32_like(self, f32_ap_off, n):
            return arena_i[:, f32_ap_off:f32_ap_off + n]

        def bf16(self, *shape):
            n = int(np.prod(shape))
            ap = arena_b[:, 2 * self.off:2 * self.off + n]
            self.off += (n + 1) // 2
            assert self.off <= AW, self.off
            return self._shape(ap, shape)

    AR = Arena()
    ps = [P.es.enter_context(nc.psum_tensor("ps%d" % i, [128, 512], F32)) for i in range(8)]
    psb = [p_[:, 0:512].bitcast(BF16) for p_ in ps]

    def PS(i):
        return ("ps", i)

    def act(out, in_, func, reads, writes, **kw):
        S.add("act", lambda: A.activation(out=out, in_=in_, func=func, **kw), reads, writes)

    def tt(out, in0, in1, op, reads, writes):
        S.add("dve", lambda: V.tensor_tensor(out=out, in0=in0, in1=in1, op=op), reads, writes)

    def ts(out, in0, s1, s2, op0, op1, reads, writes):
        if op1 is None:
            S.add("dve", lambda: V.tensor_scalar(out=out, in0=in0, scalar1=s1, scalar2=None, op0=op0), reads, writes)
        else:
            S.add("dve", lambda: V.tensor_scalar(out=out, in0=in0, scalar1=s1, scalar2=s2, op0=op0, op1=op1),
                  reads, writes)

    def stt(out, in0, scalar, in1, op0, op1, reads, writes):
        S.add("dve", lambda: V.scalar_tensor_tensor(out=out, in0=in0, scalar=scalar, in1=in1, op0=op0, op1=op1),
              reads, writes)

    def cp(eng, out, in_, reads, writes):
        if eng == "act":
            S.add("act", lambda: A.copy(out=out, in_=in_), reads, writes)
        else:
            S.add("dve", lambda: V.tensor_copy(out=out, in_=in_), reads, writes)

    def memset(ap, val, key):
        S.add("dve", lambda: V.memset(ap, val), [key], [key])

    def mm(out, lhsT, rhs, start, stop, reads, writes):
        S.add("pe", lambda: T.matmul(out, lhsT=lhsT, rhs=rhs, start=start, stop=stop), reads, writes)

    def tr(out, in_, ident, reads, writes):
        S.add("pe", lambda: T.transpose(out, in_, ident), reads, writes)

    def rsqrt_(out, in_, scale, reads, writes):
        ts(out, in_, scale, EPS, ALU.mult, ALU.add, reads, writes)
        act(out, out, AF.Ln, writes, writes)
        act(out, out, AF.Exp, writes, writes, scale=-0.5)

    for k in consts:
        if k in ("cosT", "sinT"):
            continue
        S.dma("sp", C[k][:], cdram[k], (), [("C", k)])
    for i in range(16):
        S.dma("sp", x[:, i, :], x_in[i * 128:(i + 1) * 128, :], (), [("x", i)])

    def load_wgroup(l, gi):
        wv = w_in[l].rearrange("(kt p) c -> p kt c", p=128)
        a_, b_ = WGRP[gi]
        S.dma("pool", Win[:, :, a_:b_], wv[:, :, a_:b_], (), [("Win", gi)])
        if gi == 0:
            wsv = w_sw[l].rearrange("(kt p) c -> p kt c", p=128)
            S.dma("pool", Wsw[:, :, :], wsv[:, :, :], (), ["Wsw"])

    def load_wout(l):
        wov = w_out[l].rearrange("(kt p) c -> p kt c", p=128)
        S.dma("pool", Wout[:, :, :], wov[:, :, :], (), ["Wout"])

    def load_small(l):
        S.dma("sp", sp_t[:], smallp[l], (), ["sp_t"])
        S.dma("sp", rp_t[:], rowp[l].partition_broadcast(128), (), ["rp_t"])

    def load_weights(l):
        for gi in range(4):
            load_wgroup(l, gi)
        load_wout(l)
        load_small(l)

    SP_NORMW, SP_RETNW, SP_HGNW, SP_M2NW, SP_M2D, SP_LBL = 0, 8, 10, 12, 14, 16
    SP_S5D, SP_GLUB, SP_CW, SP_CB, SP_ARE, SP_AIM = 20, 22, 24, 40, 44, 52

    def head_norm(o_bank, nwcol, gate_ap, gate_key, out_ap, out_key, blk_lhsT, scale, sqb, rr, onb,
                  keys=("sqb", "rr", "onb")):
        ks, kr, ko = keys
        act(sqb, ps[o_bank][:, :], AF.Square, [PS(o_bank)], [ks])
        mm(ps[3][:, :], blk_lhsT, sqb, True, True, [ks, ("C", "blkones"), ("C", "allones")], [PS(3)])
        rsqrt_(rr, ps[3][:, :], scale, [PS(3)], [kr])
        stt(onb, ps[o_bank][:, :], sp_t[:, nwcol:nwcol + 1], rr, ALU.mult, ALU.mult,
            [PS(o_bank), kr, "sp_t"], [ko])
        tt(out_ap, onb, gate_ap, ALU.mult, [ko, gate_key], [out_key])

    def head_norm2(o_banks, nwcol0, gates, gate_keys, outs, out_keys, bufs, ss_banks):
        for pr in range(2):
            sqb_, rr_, onb_, (ks, kr, ko) = bufs[pr]
            act(sqb_, ps[o_banks[pr]][:, :], AF.Square, [PS(o_banks[pr])], [ks])
        for pr in range(2):
            sqb_, rr_, onb_, (ks, kr, ko) = bufs[pr]
            mm(ps[ss_banks[pr]][:, :], C["blkones"][:], sqb_, True, True, [ks, ("C", "blkones")], [PS(ss_banks[pr])])
        for pr in range(2):
            sqb_, rr_, onb_, (ks, kr, ko) = bufs[pr]
            ts(rr_, ps[ss_banks[pr]][:, :], 1.0 / 64, EPS, ALU.mult, ALU.add, [PS(ss_banks[pr])], [kr])
        for pr in range(2):
            sqb_, rr_, onb_, (ks, kr, ko) = bufs[pr]
            act(rr_, rr_, AF.Ln, [kr], [kr])
        for pr in range(2):
            sqb_, rr_, onb_, (ks, kr, ko) = bufs[pr]
            act(rr_, rr_, AF.Exp, [kr], [kr], scale=-0.5)
        for pr in range(2):
            sqb_, rr_, onb_, (ks, kr, ko) = bufs[pr]
            stt(onb_, ps[o_banks[pr]][:, :], sp_t[:, nwcol0 + pr:nwcol0 + pr + 1], rr_, ALU.mult, ALU.mult,
                [PS(o_banks[pr]), kr, "sp_t"], [ko])
        for pr in range(2):
            sqb_, rr_, onb_, (ks, kr, ko) = bufs[pr]
            tt(outs[pr], onb_, gates[pr], ALU.mult, [ko, gate_keys[pr]], [out_keys[pr]])

    def state_update(m, pr, kend_ap, kend_key, v_ap, v_key, dec, dec_keys=(), bank=1):
        mm(ps[bank][:, 0:128], kend_ap, v_ap, True, True, [kend_key, v_key], [PS(bank)])
        for hh in range(2):
            r = slice(hh * 64, hh * 64 + 64)
            stt(hst[m][r, pr, :], hst[m][r, pr, :], dec[hh], ps[bank][r, hh * 64:hh * 64 + 64], ALU.mult, ALU.add,
                [PS(bank), ("hst", m, pr)] + list(dec_keys), [("hst", m, pr)])
            cp("act", hstbH[m][hh][r, pr, :], hst[m][r, pr, :], [("hst", m, pr)], [("hstbH", m, pr)])

    def rmsnorm_alloc():
        AR.reset()
        return [AR.bf16(D), AR.bf16(D)], [AR.bf16(D), AR.bf16(D)]

    def rmsnorm_tile(b, j, bufs):
        junks, hnbs = bufs
        i = 4 * b + j
        junk, hnb, pb = junks[j % 2], hnbs[j % 2], (7 if j % 2 == 0 else 3)
        act(junk, x[:, i, :], AF.Square, [("x", i)], [("junk", j % 2), ("ss", i)], accum_out=ss16[:, i:i + 1])
        rsqrt_(rstd[:, i:i + 1], ss16[:, i:i + 1], 1.0 / D, [("ss", i)], [("rstd", i)])
        ts(hnb, x[:, i, :], rstd[:, i:i + 1], None, ALU.mult, None, [("x", i), ("rstd", i)], [("hnb", j % 2)])
        pt = psb[pb]
        for kt in range(8):
            tr(pt[:, kt * 128:(kt + 1) * 128], hnb[:, kt * 128:(kt + 1) * 128], C["ident_bf"][:],
               [("hnb", j % 2), ("C", "ident_bf")], [PS(pb)])
        tt(hnT[:, :, j * 128:(j + 1) * 128], pt.rearrange("p (k t) -> p k t", k=8),
           sp_t[:, SP_NORMW:SP_NORMW + 8].unsqueeze(2).to_broadcast([128, 8, 128]), ALU.mult,
           [PS(pb), "sp_t"], ["hnT"])

    def out_proj_tile(b, j):
        i = 4 * b + j
        for n in range(2):
            for kt in range(8):
                mm(ps[4 + n][:, :], mixT[:, kt, j * 128:(j + 1) * 128], Wout[:, kt, n * 512:(n + 1) * 512],
                   kt == 0, kt == 7, [("mixT", q) for q in range(8)] + ["Wout"], [PS(4 + n)])
            tt(x[:, i, n * 512:(n + 1) * 512], ps[4 + n][:, :], x[:, i, n * 512:(n + 1) * 512], ALU.add,
               [PS(4 + n), ("x", i)], [("x", i)])

    WGRP = ((0, 1024), (1024, 2048), (2048, 2560), (2560, PT))

    def wkey(c0):
        for gi, (a_, b_) in enumerate(WGRP):
            if a_ <= c0 < b_:
                return ("Win", gi)

    def proj_fm(bank, wt, c0, reads=None):
        if reads is None:
            reads = (wkey(c0),)
        for kt in range(8):
            mm(ps[bank][:, :], wt[:, kt, c0:c0 + 128], hnT[:, kt, :], kt == 0, kt == 7,
               ["hnT"] + list(reads), [PS(bank)])

    def proj_tm(bank, c0, n, j, ncol0=0):
        for kt in range(8):
            mm(ps[bank][:, ncol0:ncol0 + n], hnT[:, kt, j * 128:(j + 1) * 128], Win[:, kt, c0:c0 + n], kt == 0, kt == 7,
               ["hnT", wkey(c0)], [PS(bank)])

    def phase_ret(l, b):
        S.barrier()
        AR.reset()
        tA, tB, tC, tD = AR.f32(BL), AR.f32(BL), AR.f32(BL), AR.f32(BL)
        qrot, krot, qdd, gs = AR.bf16(2, BL), AR.bf16(2, BL), AR.bf16(2, BL), AR.bf16(2, BL)
        krotH = [AR.bf16(2, BL), AR.bf16(2, BL)]
        v_tm = AR.bf16(4, 256)
        PT2 = [AR.bf16(2, 128), AR.bf16(2, 128)]
        kend2 = [AR.bf16(128), AR.bf16(128)]
        sqb, rr, onb = AR.bf16(BL), AR.f32(BL), AR.f32(BL)
        C["cosT"], C["sinT"] = AR.f32(BL), AR.f32(BL)
        for hh in range(2):
            for pr in range(2):
                memset(krotH[hh][:, pr, :], 0.0, ("krotH", hh, pr))
        S.dma("sp", C["cosT"], cdram["cosT"][:, b * BL:(b + 1) * BL], ["bar"], [("C", "cosT")])
        S.dma("sp", C["sinT"], cdram["sinT"][:, b * BL:(b + 1) * BL], ["bar"], [("C", "sinT")])
        for pr in range(2):
            proj_fm(0, Win, pr * 128)
            proj_fm(1, Wsw, pr * 128, reads=("Wsw",))
            proj_fm(2, Win, 256 + pr * 128)
            proj_fm(4, Wsw, 256 + pr * 128, reads=("Wsw",))
            proj_fm(5, Win, 768 + pr * 128)
            tt(tA, ps[0][:, :], C["cosT"], ALU.mult, [PS(0), ("C", "cosT")], ["tA"])
            tt(tB, ps[1][:, :], C["sinT"], ALU.mult, [PS(1), ("C", "sinT")], ["tB"])
            tt(tA, tA, tB, ALU.add, ["tA", "tB"], ["tA"])
            cp("act", qrot[:, pr, :], tA, ["tA"], [("qrot", pr)])
            tt(qdd[:, pr, :].rearrange("p (c t) -> p c t", c=4), tA.rearrange("p (c t) -> p c t", c=4),
               C["qdec"][:, pr, :].unsqueeze(1).to_broadcast([128, 4, 128]), ALU.mult,
               ["tA", ("C", "qdec")], [("qdd", pr)])
            tt(tC, ps[2][:, :], C["cosT"], ALU.mult, [PS(2), ("C", "cosT")], ["tC"])
            tt(tD, ps[4][:, :], C["sinT"], ALU.mult, [PS(4), ("C", "sinT")], ["tD"])
            tt(krot[:, pr, :], tC, tD, ALU.add, ["tC", "tD"], [("krot", pr)])
            for hh in range(2):
                r = slice(hh * 64, hh * 64 + 64)
                tt(krotH[hh][r, pr, :], tC[r, :], tD[r, :], ALU.add, ["tC", "tD"], [("krotH", hh, pr)])
            act(gs[:, pr, :], ps[5][:, :], AF.Silu, [PS(5)], [("gs", pr)])
        for j in range(4):
            proj_tm(6, 512, 256, j)
            cp("act", v_tm[:, j, :], ps[6][:, 0:256], [PS(6)], [("v_tm", j)])
        SBk, HBk, OBk, TBk = [0, 4], [1, 5], [2, 6], [7, 3]

        def scores(j_):
            cs_ = slice(j_ * 128, (j_ + 1) * 128)
            for pr in range(2):
                for hh in range(2):
                    mm(ps[SBk[pr]][:, hh * 128:(hh + 1) * 128], krotH[hh][:, pr, cs_], qrot[:, pr, cs_], True, True,
                       [("krotH", hh, pr), ("qrot", pr)], [PS(SBk[pr])])
                tr(psb[TBk[pr]][:, 0:128], krot[:, pr, cs_], C["ident_bf"][:], [("krot", pr), ("C", "ident_bf")], [PS(TBk[pr])])

        scores(0)
        for j in range(4):
            cs = slice(j * 128, (j + 1) * 128)
            for pr in range(2):
                tt(PT2[pr], ps[SBk[pr]][:, 0:256].rearrange("p (h t) -> p h t", h=2), C["dmT"][:, 2 * pr:2 * pr + 2, :],
                   ALU.mult, [PS(SBk[pr]), ("C", "dmT")], [("PTt", pr)])
                tt(kend2[pr], psb[TBk[pr]][:, 0:128], C["kend"][:, pr, :], ALU.mult, [PS(TBk[pr]), ("C", "kend")],
                   [("kendb", pr)])
            if j + 1 < 4:
                scores(j + 1)
            for pr in range(2):
                for hh in range(2):
                    r = slice(hh * 64, hh * 64 + 64)
                    h = 2 * pr + hh
                    mm(ps[OBk[pr]][r, cs], v_tm[:, j, h * 64:(h + 1) * 64], PT2[pr][:, hh, :], True, False,
                       [("v_tm", j), ("PTt", pr)], [PS(OBk[pr])])
                    mm(ps[OBk[pr]][r, cs], hstbH["ret"][hh][:, pr, :], qdd[:, pr, cs], False, True,
                       [("hstbH", "ret", pr), ("qdd", pr)], [PS(OBk[pr])])
                state_update("ret", pr, kend2[pr], ("kendb", pr), v_tm[:, j, pr * 128:(pr + 1) * 128], ("v_tm", j),
                             [GAM[2 * pr] ** 128, GAM[2 * pr + 1] ** 128], bank=HBk[pr])
        head_norm2(OBk, SP_RETNW, [gs[:, 0, :], gs[:, 1, :]], [("gs", 0), ("gs", 1)],
                   [mixT[:, 0, :], mixT[:, 1, :]], [("mixT", 0), ("mixT", 1)],
                   [(sqb, tA, tB, ("sqb", "tA", "tB")), (qdd[:, 0, :], tC, tD, (("qdd", 0), "tC", "tD"))], [3, 7])

    def phase_hgrn(l, b):
        S.barrier()
        AR.reset()
        qs, kTf, E2, E1t = AR.f32(BL), AR.f32(BL), AR.f32(BL), AR.f32(BL)
        t256a, t256b = qs[:, 0:256], qs[:, 256:512]
        rr, onb = E2, kTf
        sv = AR.off
        AR.off = sv - BL
        sqb = AR.bf16(BL)
        AR.off = sv
        decs = AR.f32(2, 8)
        qT, gs = AR.bf16(2, BL), AR.bf16(2, BL)
        kTH = [AR.bf16(2, BL), AR.bf16(2, BL)]
        logf = AR.f32(4, 256)
        k_tm, v_tm = AR.bf16(4, 256), AR.bf16(4, 256)
        kendH = [AR.bf16(4, 256), AR.bf16(4, 256)]
        PT2 = [AR.bf16(2, 128), AR.bf16(2, 128)]
        lbF, omF, nomF = AR.f32(2), AR.f32(2), AR.f32(2)
        lbR, omR = AR.f32(256), AR.f32(256)
        if l == 0:
            memset(lbF, 0.0, "lbF")
            memset(lbR, 0.0, "lbR")
        else:
            tt(lbF, sp_t[:, SP_LBL + 2:SP_LBL + 4], sp_t[:, SP_LBL:SP_LBL + 2], ALU.subtract, ["sp_t"], ["lbF"])
            act(lbF, lbF, AF.Sigmoid, ["lbF"], ["lbF"])
            tt(lbR, rp_t[:, 256:512], rp_t[:, 0:256], ALU.subtract, ["rp_t"], ["lbR"])
            act(lbR, lbR, AF.Sigmoid, ["lbR"], ["lbR"])
        ts(omF, lbF, -1.0, 1.0, ALU.mult, ALU.add, ["lbF"], ["omF"])
        ts(nomF, lbF, 1.0, -1.0, ALU.mult, ALU.add, ["lbF"], ["nomF"])
        ts(omR, lbR, -1.0, 1.0, ALU.mult, ALU.add, ["lbR"], ["omR"])
        for hh in range(2):
            for pr in range(2):
                memset(kTH[hh][:, pr, :], 0.0, ("kTH", hh, pr))
            memset(kendH[hh].rearrange("p a b -> p (a b)"), 0.0, ("kendH", hh))
        for j in range(4):
            proj_tm(4 + (j % 2), 1280, 256, j)
            act(logf[:, j, :], ps[4 + (j % 2)][:, 0:256], AF.Sigmoid, [PS(4 + (j % 2))], [("logf", j)])
        for j in range(4):
            proj_tm(j % 2, 1536, 256, j)
            cp("act", v_tm[:, j, :], ps[j % 2][:, 0:256], [PS(j % 2)], [("v_tm", j)])
        for j in range(4):
            tt(logf[:, j, :], logf[:, j, :], omR, ALU.mult, [("logf", j), "omR"], [("logf", j)])
            tt(logf[:, j, :], logf[:, j, :], lbR, ALU.add, [("logf", j), "lbR"], [("logf", j)])
            ts(k_tm[:, j, :], logf[:, j, :], -1.0, 1.0, ALU.mult, ALU.add, [("logf", j)], [("k_tm", j)])
        for j in range(4):
            act(logf[:, j, :], logf[:, j, :], AF.Ln, [("logf", j)], [("logf", j)])
        for j in range(4):
            tb_ = t256a if j % 2 == 0 else t256b
            mm(ps[7][:, (j % 2) * 256:(j % 2) * 256 + 256], C["mstrBD"][:], logf[:, j, :], True, True,
               [("C", "mstrBD"), ("logf", j)], [PS(7)])
            act(tb_, ps[7][:, (j % 2) * 256:(j % 2) * 256 + 256], AF.Exp, [PS(7)], [("qs", j % 2)])
            for u in range(2):
                r = slice(u * 64, u * 64 + 64)
                tt(kendH[u][r, j, :], k_tm[r, j, :], tb_[r, :], ALU.mult, [("k_tm", j), ("qs", j % 2)], [("kendH", u)])
        for pr in range(2):
            proj_fm(0, Win, 1024 + pr * 128)
            proj_fm(1, Win, 1280 + pr * 128)
            proj_fm(2, Win, 1792 + pr * 128)
            act(qs, ps[0][:, :], AF.Silu, [PS(0), ("qs", 0), ("qs", 1)], ["qs", ("qs", 0), ("qs", 1)])
            act(kTf, ps[1][:, :], AF.Sigmoid, [PS(1)], ["kTf"])
            ts(kTf, kTf, nomF[:, pr:pr + 1], omF[:, pr:pr + 1], ALU.mult, ALU.add, ["kTf", "nomF", "omF"], ["kTf"])
            act(gs[:, pr, :], ps[2][:, :], AF.Silu, [PS(2)], [("gs", pr)])
            for j in range(4):
                mm(ps[6][:, j * 128:(j + 1) * 128], logf[:, j, pr * 128:(pr + 1) * 128], C["maskBD"][:], True, True,
                   [("logf", j), ("C", "maskBD")], [PS(6)])
            act(E1t, ps[6][:, :], AF.Exp, [PS(6)], ["E1t"])
            act(E2, ps[6][:, :], AF.Exp, [PS(6)], ["E2"], scale=-1.0)
            tt(qT[:, pr, :], qs, E1t, ALU.mult, ["qs", "E1t"], [("qT", pr)])
            cp("dve", decs[:, pr, :], E1t.rearrange("p (c t) -> p c t", t=64)[:, :, 63], ["E1t"], [("decs", pr)])
            for hh in range(2):
                r = slice(hh * 64, hh * 64 + 64)
                tt(kTH[hh][r, pr, :], kTf[r, :], E2[r, :], ALU.mult, ["kTf", "E2"], [("kTH", hh, pr)])
        SBk, HBk, OBk = [0, 4], [1, 5], [2, 6]

        def scores(j_):
            cs_ = slice(j_ * 128, (j_ + 1) * 128)
            for pr in range(2):
                for hh in range(2):
                    mm(ps[SBk[pr]][:, hh * 128:(hh + 1) * 128], kTH[hh][:, pr, cs_], qT[:, pr, cs_], True, True,
                       [("kTH", hh, pr), ("qT", pr)], [PS(SBk[pr])])

        scores(0)
        for j in range(4):
            cs = slice(j * 128, (j + 1) * 128)
            for pr in range(2):
                tt(PT2[pr], ps[SBk[pr]][:, 0:256].rearrange("p (h t) -> p h t", h=2),
                   C["maskBD"][:, :].unsqueeze(1).to_broadcast([128, 2, 128]), ALU.mult,
                   [PS(SBk[pr]), ("C", "maskBD")], [("PTt", pr)])
            if j + 1 < 4:
                scores(j + 1)
            for u in range(2):
                cu = slice(j * 128 + u * 64, j * 128 + u * 64 + 64)
                for pr in range(2):
                    for hh in range(2):
                        r = slice(hh * 64, hh * 64 + 64)
                        h = 2 * pr + hh
                        mm(ps[OBk[pr]][r, cu], v_tm[:, j, h * 64:(h + 1) * 64], PT2[pr][:, hh, u * 64:u * 64 + 64], True, False,
                           [("v_tm", j), ("PTt", pr)], [PS(OBk[pr])])
                        mm(ps[OBk[pr]][r, cu], hstbH["hgrn"][hh][:, pr, :], qT[:, pr, cu], False, True,
                           [("hstbH", "hgrn", pr), ("qT", pr)], [PS(OBk[pr])])
                    ci = 2 * j + u
                    dec = [decs[hh * 64:hh * 64 + 64, pr, ci:ci + 1] for hh in range(2)]
                    state_update("hgrn", pr, kendH[u][:, j, pr * 128:(pr + 1) * 128], ("kendH", u),
                                 v_tm[:, j, pr * 128:(pr + 1) * 128], ("v_tm", j), dec, [("decs", pr)], bank=HBk[pr])
        head_norm2(OBk, SP_HGNW, [gs[:, 0, :], gs[:, 1, :]], [("gs", 0), ("gs", 1)],
                   [mixT[:, 2, :], mixT[:, 3, :]], [("mixT", 2), ("mixT", 3)],
                   [(sqb, rr, onb, ("E1t", "E2", "kTf")), (kTH[0][:, 0, :], qs, logf.rearrange("p a b -> p (a b)")[:, 0:BL],
                                                          (("kTH", 0, 0), "qs", ("logf", 0)))], [3, 7])

    def phase_m2(l, b):
        S.barrier()
        AR.reset()
        xraw = AR.f32(4, 3 + BL)
        alias0 = AR.off - 4 * (3 + BL)
        xc = AR.f32(4, BL)
        xcb = AR.bf16(4, BL)
        BTH = [AR.bf16(BL), AR.bf16(BL)]
        gz = AR.bf16(2, BL)
        dt_t, la_t = AR.f32(4, 4), AR.f32(4, 32)
        x4, arow = AR.f32(4), AR.f32(4)
        laB = AR.f32(4, 128)
        dm4 = laB
        cum_s, vsc, etc = AR.f32(4), AR.f32(4), AR.f32(4)
        ecb = AR.f32(4, 128)
        PTm, cdec = AR.bf16(4, 128), AR.bf16(4, 128)
        v_tm, v_end = AR.bf16(4, 64), AR.bf16(4, 64)
        B_tm = AR.bf16(128)
        save = AR.off
        AR.off = alias0
        my2 = AR.f32(2, BL)
        sqb = AR.bf16(2, BL)
        rr = AR.f32(BL)
        assert AR.off <= alias0 + 4 * (3 + BL)
        AR.off = save
        act(arow, rp_t[:, 516:520], AF.Exp, ["rp_t"], ["arow"])
        ts(arow, arow, -1.0, None, ALU.mult, None, ["arow"], ["arow"])
        for g in range(2):
            memset(BTH[g], 0.0, ("BTH", g))
        for pr in range(2):
            proj_fm(0, Win, 2560 + pr * 128)
            act(gz[:, pr, :], ps[0][:, :], AF.Silu, [PS(0)], [("gz", pr)])
        cp("dve", xraw[:, :, 0:3], xhist[:, :, :], ["xhist"], ["xraw_h"])
        for tl in range(4):
            bk = 1 if tl % 2 == 0 else 5
            proj_fm(bk, Win, 2816 + tl * 128)
            cp("act", xraw[:, tl, 3:3 + BL], ps[bk][:, :], [PS(bk)], [("xraw", tl)])
        for tl in range(4):
            ts(xc[:, tl, :], xraw[:, tl, 0:BL], sp_t[:, SP_CW + 4 * tl:SP_CW + 4 * tl + 1],
               sp_t[:, SP_CB + tl:SP_CB + tl + 1], ALU.mult, ALU.add, [("xraw", tl), "xraw_h", "sp_t"], [("xc", tl)])
            for i in range(1, 4):
                stt(xc[:, tl, :], xraw[:, tl, i:i + BL], sp_t[:, SP_CW + 4 * tl + i:SP_CW + 4 * tl + i + 1],
                    xc[:, tl, :], ALU.mult, ALU.add, [("xraw", tl), "xraw_h", ("xc", tl), "sp_t"], [("xc", tl)])
        for tl in range(4):
            if tl < 2:
                act(xc[:, tl, :], xc[:, tl, :], AF.Silu, [("xc", tl)], [("xc", tl)])
            else:
                act(xcb[:, tl, :], xc[:, tl, :], AF.Silu, [("xc", tl)], [("xcb", tl)])
        for tl in range(2):
            cp("act", xcb[:, tl, :], xc[:, tl, :], [("xc", tl)], [("xcb", tl)])
        cp("dve", xhist[:, :, :], xraw[:, :, BL:BL + 3], [("xraw", t_) for t_ in range(4)] + ["xraw_h"], ["xhist"])
        for g in range(2):
            r = slice(g * 64, g * 64 + 64)
            cp("dve", BTH[g][r, :], xcb[r, 2, :], [("xcb", 2)], [("BTH", g)])
        dkeys = [("dt", j_) for j_ in range(4)]
        for j in range(4):
            proj_tm(4, 3300, 32, j)
            tt(dt_t[:, j, :], ps[4][:, 28:32], rp_t[:, 512:516], ALU.add, [PS(4), "rp_t"], [("dt", j)])
        dflat = dt_t.rearrange("p a b -> p (a b)")
        act(dflat, dflat, AF.Exp, dkeys, dkeys)
        ts(dflat, dflat, 1.0, None, ALU.add, None, dkeys, dkeys)
        act(dflat, dflat, AF.Ln, dkeys, dkeys)
        for j in range(4):
            memset(la_t[:, j, :], 0.0, ("la", j))
            tt(la_t[:, j, 0:4], dt_t[:, j, :], arow, ALU.mult, [("dt", j), "arow"], [("la", j)])
        obs = [2, 6]
        STG = getattr(build, "STAGE", 9)
        if STG < 2:
            return
        for j in range(4):
            cs = slice(j * 128, (j + 1) * 128)
            mm(ps[4][:, 64:96], C["U128"][:], la_t[:, j, :], True, True, [("C", "U128"), ("la", j)], [PS(4)])
            mm(ps[4][:, 128:160], C["M128"][:], la_t[:, j, :], True, True, [("C", "M128"), ("la", j)], [PS(4)])
            ts(cum_s, ps[4][:, 64:68], -1.0, None, ALU.mult, None, [PS(4)], ["cum_s"])
            act(etc, ps[4][:, 128:132], AF.Exp, [PS(4)], ["etc"])
            tt(vsc, etc, dt_t[:, j, :], ALU.mult, ["etc", ("dt", j)], ["vsc"])
            for h in range(4):
                ts(laB[:, h, :], C["U128"][:], la_t[:, j, h:h + 1], None, ALU.mult, None,
                   [("la", j), ("C", "U128")], [("laB", h), "dm4"])
                mm(ps[0][:, h * 128:(h + 1) * 128], C["ones_f"][:], laB[:, h, :], True, True,
                   [("laB", h), ("C", "ones_f")], [PS(0)])
            act(ecb.rearrange("p a b -> p (a b)"), ps[0][:, :], AF.Exp, [PS(0)], ["ecb"])
            if STG < 2.5:
                continue
            for g in range(2):
                mm(ps[1][:, g * 128:(g + 1) * 128], BTH[g][:, cs], xcb[:, 3, cs], True, True,
                   [("BTH", g), ("xcb", 3)], [PS(1)])
            tt(cdec, xcb[:, 3, cs].unsqueeze(1).to_broadcast([128, 4, 128]), ecb, ALU.mult, [("xcb", 3), "ecb"], ["cdec"])
            tt(dm4, ps[0][:, :].rearrange("p (h t) -> p h t", h=4), cum_s.unsqueeze(2).to_broadcast([128, 4, 128]),
               ALU.add, [PS(0), "cum_s"], ["dm4"] + [("laB", h_) for h_ in range(4)])
            tt(dm4, dm4, C["negm"][:, :].unsqueeze(1).to_broadcast([128, 4, 128]), ALU.add, ["dm4", ("C", "negm")], ["dm4"])
            act(dm4.rearrange("p h t -> p (h t)"), dm4.rearrange("p h t -> p (h t)"), AF.Exp, ["dm4"], ["dm4"])
            for g in range(2):
                tt(PTm[:, 2 * g:2 * g + 2, :], dm4[:, 2 * g:2 * g + 2, :],
                   ps[1][:, g * 128:(g + 1) * 128].unsqueeze(1).to_broadcast([128, 2, 128]), ALU.mult,
                   [PS(1), "dm4"], ["PTm"])
            if STG < 3:
                continue
            for tl in range(2):
                tr(psb[7][:, tl * 128:(tl + 1) * 128], xcb[:, tl, cs], C["ident_bf"][:],
                   [("xcb", tl), ("C", "ident_bf")], [PS(7)])
            tr(psb[7][:, 256:384], xcb[:, 2, cs], C["ident_bf"][:], [("xcb", 2), ("C", "ident_bf")], [PS(7)])
            xview = psb[7][:, 0:256].rearrange("p (h v) -> p h v", h=4)
            tt(v_tm, xview, dt_t[:, j, :].unsqueeze(2).to_broadcast([128, 4, 64]), ALU.mult,
               [PS(7), ("dt", j)], ["v_tm"])
            tt(v_end, xview, vsc.unsqueeze(2).to_broadcast([128, 4, 64]), ALU.mult, [PS(7), "vsc"], ["v_end"])
            cp("act", B_tm, psb[7][:, 256:384], [PS(7)], ["B_tm"])
            for h in range(4):
                g, hh = h // 2, h % 2
                r = slice(hh * 64, hh * 64 + 64)
                mm(ps[obs[g]][r, cs], v_tm[:, h, :], PTm[:, h, :], True, False, ["v_tm", "PTm"], [PS(obs[g])])
                mm(ps[obs[g]][r, cs], hstD_bf[h][:, :], cdec[:, h, :], False, True,
                   [("hstDbf", h), "cdec"], [PS(obs[g])])
            mm(ps[5][:, 0:256], B_tm, v_end.rearrange("p h v -> p (h v)"), True, True, ["B_tm", "v_end"], [PS(5)])
            for h in range(4):
                g, hh = h // 2, h % 2
                r = slice(g * 64, g * 64 + 64)
                stt(hst["m2"][r, hh, :], hst["m2"][r, hh, :], ecb[r, h, 127:128], ps[5][r, h * 64:(h + 1) * 64],
                    ALU.mult, ALU.add, [PS(5), ("hst", "m2", h), "ecb"], [("hst", "m2", h)])
                cp("act", hstD_bf[h][r, :], hst["m2"][r, hh, :], [("hst", "m2", h)], [("hstDbf", h)])
        if STG < 4:
            return
        for pr in range(2):
            stt(my2[:, pr, :], xc[:, pr, :], sp_t[:, SP_M2D + pr:SP_M2D + pr + 1], ps[obs[pr]][:, :],
                ALU.mult, ALU.add, [("xc", pr), "sp_t", PS(obs[pr]), "xhist"], [("my2", pr)])
            tt(my2[:, pr, :], my2[:, pr, :], gz[:, pr, :], ALU.mult, [("my2", pr), ("gz", pr)], [("my2", pr)])
            act(sqb[:, pr, :], my2[:, pr, :], AF.Square, [("my2", pr)], [("sqb", pr)])
        for pr in range(2):
            mm(ps[3][:, :], C["allones"][:], sqb[:, pr, :], pr == 0, pr == 1, [("sqb", pr), ("C", "allones")], [PS(3)])
        rsqrt_(rr, ps[3][:, :], 1.0 / 256, [PS(3)], ["rr"])
        for pr in range(2):
            stt(mixT[:, 6 + pr, :], my2[:, pr, :], sp_t[:, SP_M2NW + pr:SP_M2NW + pr + 1], rr, ALU.mult, ALU.mult,
                [("my2", pr), "rr", "sp_t"], [("mixT", 6 + pr)])


    TWO_PI = float(2 * np.pi)

    def sincos(out_s, out_c, ang, n, key_in, key_s, key_c, rows=slice(0, 128)):
        o0 = AR.off
        kf, r_ = AR.f32(n)[rows, :], AR.f32(n)[rows, :]
        ki = arena_i[rows, AR.off:AR.off + n]
        AR.off += n
        assert AR.off <= AW
        for shift, out, kout in ((0.0, out_s, key_s), (float(np.pi / 2), out_c, key_c)):
            ts(r_, ang, shift, None, ALU.add, None, [key_in], ["sc_r"])
            ts(ki, r_, 1.0 / TWO_PI, None, ALU.mult, None, ["sc_r"], ["sc_ki"])
            cp("dve", kf, ki, ["sc_ki"], ["sc_kf"])
            stt(r_, kf, -TWO_PI, r_, ALU.mult, ALU.add, ["sc_kf", "sc_r"], ["sc_r"])
            ts(r_, r_, float(np.pi), float(-np.pi), ALU.min, ALU.max, ["sc_r"], ["sc_r"])
            act(out, r_, AF.Sin, ["sc_r"], [kout])
        AR.off = o0

    def s5_params(l):
        S.barrier()
        AR.reset()
        Bt, Ct = AR.f32(2, 8, 16), AR.f32(2, 8, 16)
        S.dma("sp", Bt, s5b_d[l], ["bar"], ["s5Bt"])
        S.dma("sp", Ct, s5c_d[l], ["bar"], ["s5Ct"])
        dtv, lr, abre, abim, nr, den, t3, fre, fim = [AR.f32(8) for _ in range(9)]
        Are, Aim = sp_t[:, SP_ARE:SP_ARE + 8], sp_t[:, SP_AIM:SP_AIM + 8]
        act(dtv, sp_t[:, 60:68], AF.Exp, ["sp_t"], ["dtv"])
        tt(S5["th"][:, :], Aim, dtv, ALU.mult, ["sp_t", "dtv"], ["s5th"])
        tt(lr, Are, dtv, ALU.mult, ["sp_t", "dtv"], ["lr"])
        act(S5["rho"][:, :], lr, AF.Exp, ["lr"], ["s5rho"])
        sincos(S5["s1"][:, :], S5["c1"][:, :], S5["th"][:, :], 8, "s5th", "s5s1", "s5c1")
        tt(abre, S5["rho"][:, :], S5["c1"][:, :], ALU.mult, ["s5rho", "s5c1"], ["abre"])
        tt(abim, S5["rho"][:, :], S5["s1"][:, :], ALU.mult, ["s5rho", "s5s1"], ["abim"])
        ts(nr, abre, -1.0, None, ALU.add, None, ["abre"], ["nr"])
        tt(den, Are, Are, ALU.mult, ["sp_t"], ["den"])
        tt(t3, Aim, Aim, ALU.mult, ["sp_t"], ["t3"])
        tt(den, den, t3, ALU.add, ["den", "t3"], ["den"])
        S.add("dve", lambda: V.reciprocal(out=den, in_=den), ["den"], ["den"])
        tt(fre, nr, Are, ALU.mult, ["nr", "sp_t"], ["fre"])
        tt(t3, abim, Aim, ALU.mult, ["abim", "sp_t"], ["t3"])
        tt(fre, fre, t3, ALU.add, ["fre", "t3"], ["fre"])
        tt(fre, fre, den, ALU.mult, ["fre", "den"], ["fre"])
        tt(fim, abim, Are, ALU.mult, ["abim", "sp_t"], ["fim"])
        tt(t3, nr, Aim, ALU.mult, ["nr", "sp_t"], ["t3"])
        tt(fim, fim, t3, ALU.subtract, ["fim", "t3"], ["fim"])
        tt(fim, fim, den, ALU.mult, ["fim", "den"], ["fim"])
        bbre, bbim, u1 = AR.f32(8, 16), AR.f32(8, 16), AR.f32(8, 16)
        fb = lambda f_: f_.unsqueeze(2).to_broadcast([128, 8, 16])
        tt(bbre, Bt[:, 0, :, :], fb(fre), ALU.mult, ["s5Bt", "fre"], ["bbre"])
        tt(u1, Bt[:, 1, :, :], fb(fim), ALU.mult, ["s5Bt", "fim"], ["u1"])
        tt(bbre, bbre, u1, ALU.subtract, ["bbre", "u1"], ["bbre"])
        tt(bbim, Bt[:, 1, :, :], fb(fre), ALU.mult, ["s5Bt", "fre"], ["bbim"])
        tt(u1, Bt[:, 0, :, :], fb(fim), ALU.mult, ["s5Bt", "fim"], ["u1"])
        tt(bbim, bbim, u1, ALU.add, ["bbim", "u1"], ["bbim"])
        nCim = AR.f32(8, 16)
        ts(nCim, Ct[:, 1, :, :], -1.0, None, ALU.mult, None, ["s5Ct"], ["nCim"])
        for t_ in range(2):
            memset(BXc[t_].rearrange("p a b -> p (a b)"), 0.0, ("BXc", t_))
            memset(CXc[t_].rearrange("p a b -> p (a b)"), 0.0, ("CXc", t_))
        for e in range(2):
            r = slice(e * 64, e * 64 + 64)
            cc = slice(16 * e, 16 * e + 16)
            cp("dve", BXc[0][r, :, cc], bbre[r, :, :], ["bbre"], [("BXc", 0)])
            cp("dve", BXc[1][r, :, cc], bbim[r, :, :], ["bbim"], [("BXc", 1)])
            cp("dve", CXc[0][r, :, cc], Ct[r, 0, :, :], ["s5Ct"], [("CXc", 0)])
            cp("dve", CXc[1][r, :, cc], nCim[r, :, :], ["nCim"], [("CXc", 1)])
        memset(S5["hr"][:, :], 0.0, "s5hr")
        memset(S5["hi"][:, :], 0.0, "s5hi")
        fl = lambda a_: a_.rearrange("p a b -> p (a b)")
        for hf in range(2):
            S.barrier()
            AR.reset()
            cosR, sinR, rhoB, angR = AR.f32(4, 128), AR.f32(4, 128), AR.f32(4, 128), AR.f32(4, 128)
            LBh = [AR.bf16(4, 128), AR.bf16(4, 128)]
            CXh = [AR.bf16(4, 128), AR.bf16(4, 128)]
            BXP = [AR.bf16(4, 128), AR.bf16(4, 128)]
            for jj in range(4):
                j = 4 * hf + jj
                ts(angR[:, jj, :], C["iotaR"][:], S5["th"][:, j:j + 1], None, ALU.mult, None,
                   [("C", "iotaR"), "s5th"], ["angR"])
                ts(rhoB[:, jj, :], C["ones_f"][:], S5["rho"][:, j:j + 1], None, ALU.mult, None,
                   [("C", "ones_f"), "s5rho"], ["rhoB"])
            sincos(fl(sinR), fl(cosR), fl(angR), 512, "angR", "sinR", "cosR")
            for t_ in range(2):
                memset(fl(BXP[t_]), 0.0, ("BXP", t_))
                memset(fl(CXh[t_]), 0.0, ("CXh", t_))
                for jj in range(4):
                    cc = slice(32 * jj, 32 * jj + 32)
                    cp("dve", BXP[t_][:, jj, cc], BXc[t_][:, 4 * hf + jj, :], [("BXc", t_)], [("BXP", t_)])
                    cp("dve", CXh[t_][:, jj, cc], CXc[t_][:, 4 * hf + jj, :], [("CXc", t_)], [("CXh", t_)])
                for jj in range(4):
                    tr(psb[7][:, jj * 128:(jj + 1) * 128], BXP[t_][:, jj, :], C["ident_bf"][:],
                       [("BXP", t_), ("C", "ident_bf")], [PS(7)])
                cp("dve", LBh[t_], psb[7][:, 0:512].rearrange("p (a b) -> p a b", a=4), [PS(7)], [("LBh", t_)])
            for i_, (src, kk) in enumerate(((cosR, "cosR"), (sinR, "sinR"), (rhoB, "rhoB"))):
                S.dma("sp", tabF[hf, i_], fl(src), [kk], ["s5tab"])
            for i_, (src, kk) in enumerate(((LBh[0], ("LBh", 0)), (LBh[1], ("LBh", 1)), (CXh[0], ("CXh", 0)), (CXh[1], ("CXh", 1)))):
                S.dma("sp", tabB[hf, i_], fl(src), [kk], ["s5tab"])

    def phase_s5(l, b):
        S.barrier()
        AR.reset()
        uT, uTb, gsg = AR.f32(2, BL), AR.bf16(2, BL), AR.bf16(2, BL)
        GW = AR.bf16(2, 256)
        o_tab = AR.off
        cosR, sinR, rhoB = AR.f32(4, 128), AR.f32(4, 128), AR.f32(4, 128)
        o_lb = AR.off
        LBh = [AR.bf16(4, 128), AR.bf16(4, 128)]
        CXh = [AR.bf16(4, 128), AR.bf16(4, 128)]
        tabF_sb = arena_f[:, o_tab:o_tab + 3 * BL].rearrange("p (t c) -> p t c", t=3)
        tabB_sb = arena_b[:, 2 * o_lb:2 * o_lb + 4 * BL].rearrange("p (t c) -> p t c", t=4)
        mark = AR.off
        t1, t2 = AR.f32(4, 128), AR.f32(4, 128)
        hb0 = AR.off
        hreb, himb = AR.bf16(4, 128), AR.bf16(4, 128)
        gyb = arena_b[:, 2 * hb0:2 * hb0 + 2 * BL].rearrange("p (a b) -> p a b", a=2)
        btr, bti = AR.f32(4, 128), AR.f32(4, 128)
        g_re, g_im = AR.f32(4, 128), AR.f32(4, 128)
        gir, gii, gt = AR.f32(4), AR.f32(4), AR.f32(4)
        fl = lambda a_: a_.rearrange("p a b -> p (a b)")
        gv = glu_d[l].rearrange("(ct p) o -> p ct o", p=128)
        S.dma("pool", GW[:, :, :], gv[:, :, :], ["bar"], ["GW"])
        for ut in range(2):
            proj_fm(3, Win, 2048 + ut * 128)
            cp("act", uT[:, ut, :], ps[3][:, :], [PS(3)], [("uT", ut)])
            cp("act", uTb[:, ut, :], uT[:, ut, :], [("uT", ut)], [("uTb", ut)])
            proj_fm(7, Win, 2304 + ut * 128)
            act(gsg[:, ut, :], ps[7][:, :], AF.Silu, [PS(7)], [("gsg", ut)])
        yb = [5, 6]
        for hf in range(2):
            hs = slice(4 * hf, 4 * hf + 4)
            S.dma("sp", tabF_sb, tabF[hf].rearrange("t p c -> p t c"), ["bar", "s5tab"], ["cosR", "sinR", "rhoB"])
            S.dma("sp", tabB_sb, tabB[hf].rearrange("t p c -> p t c"), ["bar", "s5tab"],
                  [("LBh", 0), ("LBh", 1), ("CXh", 0), ("CXh", 1)])
            def bu_mm(cj_):
                cs_ = slice(cj_ * 128, (cj_ + 1) * 128)
                br_, bi_ = (0, 1) if cj_ % 2 == 0 else (2, 4)
                for jj in range(4):
                    mm(ps[br_][:, jj * 128:(jj + 1) * 128], LBh[0][:, jj, :], uTb[:, hf, cs_], True, True,
                       [("LBh", 0), ("uTb", hf)], [PS(br_)])
                    mm(ps[bi_][:, jj * 128:(jj + 1) * 128], LBh[1][:, jj, :], uTb[:, hf, cs_], True, True,
                       [("LBh", 1), ("uTb", hf)], [PS(bi_)])

            bu_mm(0)
            for cj in range(4):
                cs = slice(cj * 128, (cj + 1) * 128)
                br, bi = (0, 1) if cj % 2 == 0 else (2, 4)
                if cj + 1 < 4:
                    bu_mm(cj + 1)
                tt(fl(t1), ps[br][:, :], fl(cosR), ALU.mult, [PS(br), "cosR"], ["t1"])
                tt(fl(t2), ps[bi][:, :], fl(sinR), ALU.mult, [PS(bi), "sinR"], ["t2"])
                tt(fl(btr), fl(t1), fl(t2), ALU.add, ["t1", "t2"], ["btr"])
                tt(fl(t1), ps[bi][:, :], fl(cosR), ALU.mult, [PS(bi), "cosR"], ["t1"])
                tt(fl(t2), ps[br][:, :], fl(sinR), ALU.mult, [PS(br), "sinR"], ["t2"])
                tt(fl(bti), fl(t1), fl(t2), ALU.subtract, ["t1", "t2"], ["bti"])
                tt(gir, S5["hr"][:, hs], S5["c1"][:, hs], ALU.mult, ["s5hr", "s5c1"], ["gir"])
                tt(gt, S5["hi"][:, hs], S5["s1"][:, hs], ALU.mult, ["s5hi", "s5s1"], ["gt"])
                tt(gir, gir, gt, ALU.subtract, ["gir", "gt"], ["gir"])
                tt(gii, S5["hr"][:, hs], S5["s1"][:, hs], ALU.mult, ["s5hr", "s5s1"], ["gii"])
                tt(gt, S5["hi"][:, hs], S5["c1"][:, hs], ALU.mult, ["s5hi", "s5c1"], ["gt"])
                tt(gii, gii, gt, ALU.add, ["gii", "gt"], ["gii"])
                for jj in range(4):
                    S.add("dve", lambda jj=jj: V.tensor_tensor_scan(out=g_re[:, jj, :], data0=rhoB[:, jj, :],
                          data1=btr[:, jj, :], initial=gir[:, jj:jj + 1], op0=ALU.mult, op1=ALU.add),
                          ["rhoB", "btr", "gir"], ["g_re"])
                    S.add("dve", lambda jj=jj: V.tensor_tensor_scan(out=g_im[:, jj, :], data0=rhoB[:, jj, :],
                          data1=bti[:, jj, :], initial=gii[:, jj:jj + 1], op0=ALU.mult, op1=ALU.add),
                          ["rhoB", "bti", "gii"], ["g_im"])
                tt(fl(t1), fl(g_re), fl(cosR), ALU.mult, ["g_re", "cosR"], ["t1"])
                tt(fl(t2), fl(g_im), fl(sinR), ALU.mult, ["g_im", "sinR"], ["t2"])
                tt(fl(btr), fl(t1), fl(t2), ALU.subtract, ["t1", "t2"], ["btr"])
                tt(fl(t1), fl(g_re), fl(sinR), ALU.mult, ["g_re", "sinR"], ["t1"])
                tt(fl(t2), fl(g_im), fl(cosR), ALU.mult, ["g_im", "cosR"], ["t2"])
                tt(fl(bti), fl(t1), fl(t2), ALU.add, ["t1", "t2"], ["bti"])
                cp("dve", S5["hr"][:, hs], btr[:, :, 127], ["btr"], ["s5hr"])
                cp("dve", S5["hi"][:, hs], bti[:, :, 127], ["bti"], ["s5hi"])
                cp("act", fl(hreb), fl(btr), ["btr"], ["hreb"])
                cp("act", fl(himb), fl(bti), ["bti"], ["himb"])
                for jj in range(4):
                    mm(ps[yb[hf]][:, cs], CXh[0][:, jj, :], hreb[:, jj, :], jj == 0, False, [("CXh", 0), "hreb"], [PS(yb[hf])])
                    mm(ps[yb[hf]][:, cs], CXh[1][:, jj, :], himb[:, jj, :], False, jj == 3, [("CXh", 1), "himb"], [PS(yb[hf])])
        S.barrier()
        gyF = [fl(g_re), fl(g_im)]
        C1 = float(np.sqrt(2.0 / np.pi))
        E = [dict(sy=fl(t1), ksy="t1", x2=fl(t2), kx2="t2", sg=fl(btr), ksg="btr"),
             dict(sy=fl(bti), ksy="bti", x2=fl(cosR), kx2="cosR", sg=fl(sinR), ksg="sinR")]
        for ut in range(2):
            e_ = E[ut]
            stt(e_["sy"], uT[:, ut, :], sp_t[:, SP_S5D + ut:SP_S5D + ut + 1], ps[yb[ut]][:, :], ALU.mult, ALU.add,
                [("uT", ut), "sp_t", PS(yb[ut])], [e_["ksy"]])
        for ut in range(2):
            e_ = E[ut]
            tt(e_["x2"], e_["sy"], e_["sy"], ALU.mult, [e_["ksy"]], [e_["kx2"]])
        for ut in range(2):
            e_ = E[ut]
            ts(e_["x2"], e_["x2"], 2.0 * C1 * 0.044715, 2.0 * C1, ALU.mult, ALU.add, [e_["kx2"]], [e_["kx2"]])
        for ut in range(2):
            e_ = E[ut]
            tt(e_["x2"], e_["x2"], e_["sy"], ALU.mult, [e_["kx2"], e_["ksy"]], [e_["kx2"]])
        for ut in range(2):
            e_ = E[ut]
            act(e_["sg"], e_["x2"], AF.Sigmoid, [e_["kx2"]], [e_["ksg"]])
        for ut in range(2):
            e_ = E[ut]
            tt(gyF[ut], e_["sy"], e_["sg"], ALU.mult, [e_["ksy"], e_["ksg"]], [("gyF", ut)])
        for ut in range(2):
            cp("act", gyb[:, ut, :], gyF[ut], [("gyF", ut)], [("gyb", ut)])
        gb = [3, 7]
        for ot in range(2):
            for ct in range(2):
                mm(ps[gb[ot]][:, :], GW[:, ct, ot * 128:(ot + 1) * 128], gyb[:, ct, :], ct == 0, ct == 1,
                   ["GW", ("gyb", ct)], [PS(gb[ot])])
        for ot in range(2):
            e_ = E[ot]
            act(e_["sg"], ps[gb[ot]][:, :], AF.Sigmoid, [PS(gb[ot]), "sp_t"], [e_["ksg"]],
                bias=sp_t[:, SP_GLUB + ot:SP_GLUB + ot + 1])
        for ot in range(2):
            e_ = E[ot]
            tt(e_["sg"], e_["sg"], gyF[ot], ALU.mult, [e_["ksg"], ("gyF", ot)], [e_["ksg"]])
        for ot in range(2):
            e_ = E[ot]
            tt(mixT[:, 4 + ot, :], e_["sg"], gsg[:, ot, :], ALU.mult, [e_["ksg"], ("gsg", ot)], [("mixT", 4 + ot)])


    R16, R32, R64 = slice(0, 16), slice(0, 32), slice(0, 64)

    def dma_sb(out, in_, reads, writes, q="sp"):
        S.dma(q, out, in_, list(reads) + ["bar"], writes)

    def sample_layer(l):
        S.barrier()
        AR.reset()
        xs = AR.f32(D)
        junk, hnb = AR.bf16(D), AR.bf16(D)
        hnTs = AR.bf16(8, 16)
        projS = AR.f32(PT)
        ssx, rsx = AR.f32(1), AR.f32(1)
        dma_sb(xs[R16, :], scr["X"], ["scrX"], ["xs"])
        act(junk[R16, :], xs[R16, :], AF.Square, ["xs"], ["junk_s", "ssx"], accum_out=ssx[R16, :])
        rsqrt_(rsx[R16, :], ssx[R16, :], 1.0 / D, ["ssx"], ["rsx"])
        ts(hnb[R16, :], xs[R16, :], rsx[R16, 0:1], None, ALU.mult, None, ["xs", "rsx"], ["hnb_s"])
        for kt in range(8):
            tr(psb[7][:, kt * 16:(kt + 1) * 16], hnb[R16, kt * 128:(kt + 1) * 128], C["ident_bf"][R16, R16],
               ["hnb_s", ("C", "ident_bf")], [PS(7)])
        tt(hnTs, psb[7][:, 0:128].rearrange("p (k t) -> p k t", k=8),
           sp_t[:, SP_NORMW:SP_NORMW + 8].unsqueeze(2).to_broadcast([128, 8, 16]), ALU.mult, [PS(7), "sp_t"], ["hnTs"])
        c0 = 0
        ib = 0
        while c0 < PT:
            n_ = min(512, PT - c0)
            bank = ib % 2
            for kt in range(8):
                mm(ps[bank][R16, 0:n_], hnTs[:, kt, :], Win[:, kt, c0:c0 + n_], kt == 0, kt == 7,
                   ["hnTs"] + [("Win", g_) for g_ in range(4)], [PS(bank)])
            cp("act" if ib % 2 else "dve", projS[R16, c0:c0 + n_], ps[bank][R16, 0:n_], [PS(bank)], ["projS"])
            c0 += n_
            ib += 1
        S.dma("sp", scr["P"].rearrange("c n k -> n c k"), projS[R16, 0:3328].rearrange("n (c k) -> n c k", c=13),
              ["projS"], ["scrP"])
        S.dma("sp", scr["XBC"], projS[R16, 2816:3328], ["projS"], ["scrP2"])
        S.dma("sp", scr["DT"], projS[R16, 3328:3332], ["projS"], ["scrP3"])

        def linrec(m, col0, st_key, out_st, mix_col, nw_idx):
            S.barrier()
            AR.reset()
            S0 = AR.f32(64, 64)
            tmp = AR.f32(32, 64)
            q_, k_, v_, g_ = AR.f32(64), AR.f32(64), AR.f32(64), AR.f32(64)
            ta, tb, o_, oh = AR.f32(64), AR.f32(64), AR.f32(64), AR.f32(64)
            nw, l0r, l1r, fr_ = AR.f32(64), AR.f32(64), AR.f32(64), AR.f32(64)
            cs_, sn_, gm_ = AR.f32(32), AR.f32(32), AR.f32(1)
            ss_, rr_ = AR.f32(1), AR.f32(1)
            view = lambda c_: scr["P"][c_ // 256].rearrange("n (h k) -> (n h) k", h=4)
            dma_sb(q_[R64, :], view(col0), ["scrP"], ["q_"])
            dma_sb(k_[R64, :], view(col0 + 256), ["scrP"], ["k_"])
            dma_sb(v_[R64, :], view(col0 + 512), ["scrP"], ["v_"])
            dma_sb(g_[R64, :], view(col0 + 768), ["scrP"], ["g_"])
            dma_sb(S0[R64, :, :].rearrange("p a b -> p (a b)"), st_in[st_key][l].rearrange("n h k v -> (n h) (k v)"), [], ["S0"])
            dma_sb(nw[R64, :], rowsS[l, :, nw_idx, :], [], ["nw"])
            if m == "ret":
                dma_sb(cs_[R64, :], cdram["cos_s"], [], ["cs_"])
                dma_sb(sn_[R64, :], cdram["sin_s"], [], ["sn_"])
                dma_sb(gm_[R64, :], cdram["gam_s"], [], ["gm_"])
                for src, key in ((q_, "q_"), (k_, "k_")):
                    x1, x2 = src[R64, 0:32], src[R64, 32:64]
                    tt(ta[R64, 0:32], x1, cs_[R64, :], ALU.mult, [key, "cs_"], ["ta"])
                    tt(tb[R64, 0:32], x2, sn_[R64, :], ALU.mult, [key, "sn_"], ["tb"])
                    tt(ta[R64, 32:64], x1, sn_[R64, :], ALU.mult, [key, "sn_"], ["ta"])
                    tt(tb[R64, 32:64], x2, cs_[R64, :], ALU.mult, [key, "cs_"], ["tb"])
                    tt(src[R64, 0:32], ta[R64, 0:32], tb[R64, 0:32], ALU.subtract, ["ta", "tb"], [key])
                    tt(src[R64, 32:64], ta[R64, 32:64], tb[R64, 32:64], ALU.add, ["ta", "tb"], [key])
                ts(k_[R64, :], k_[R64, :], 0.125, None, ALU.mult, None, ["k_"], ["k_"])
            else:
                dma_sb(l0r[R64, :], rowsS[l, :, 2, :], [], ["l0r"])
                dma_sb(l1r[R64, :], rowsS[l, :, 3, :], [], ["l1r"])
                act(q_[R64, :], q_[R64, :], AF.Silu, ["q_"], ["q_"])
                act(fr_[R64, :], k_[R64, :], AF.Sigmoid, ["k_"], ["fr_"])
                if l == 0:
                    memset(l0r[R64, :], 0.0, "l0r")
                else:
                    tt(l0r[R64, :], l1r[R64, :], l0r[R64, :], ALU.subtract, ["l0r", "l1r"], ["l0r"])
                    act(l0r[R64, :], l0r[R64, :], AF.Sigmoid, ["l0r"], ["l0r"])
                ts(l1r[R64, :], l0r[R64, :], -1.0, 1.0, ALU.mult, ALU.add, ["l0r"], ["l1r"])
                tt(fr_[R64, :], fr_[R64, :], l1r[R64, :], ALU.mult, ["fr_", "l1r"], ["fr_"])
                tt(fr_[R64, :], fr_[R64, :], l0r[R64, :], ALU.add, ["fr_", "l0r"], ["fr_"])
                ts(k_[R64, :], fr_[R64, :], -1.0, 1.0, ALU.mult, ALU.add, ["fr_"], ["k_"])
            memset(o_[R64, :], 0.0, "o_")
            for kh in range(2):
                ks = slice(kh * 32, kh * 32 + 32)
                tt(tmp[R64, :, :], k_[R64, ks].unsqueeze(2).to_broadcast([64, 32, 64]),
                   v_[R64, :].unsqueeze(1).to_broadcast([64, 32, 64]), ALU.mult, ["k_", "v_"], ["tmp"])
                if m == "ret":
                    stt(S0[R64, ks, :], S0[R64, ks, :], gm_[R64, 0:1], tmp[R64, :, :], ALU.mult, ALU.add,
                        ["S0", "gm_", "tmp"], ["S0"])
                else:
                    tt(S0[R64, ks, :], S0[R64, ks, :], fr_[R64, ks].unsqueeze(2).to_broadcast([64, 32, 64]), ALU.mult,
                       ["S0", "fr_"], ["S0"])
                    tt(S0[R64, ks, :], S0[R64, ks, :], tmp[R64, :, :], ALU.add, ["S0", "tmp"], ["S0"])
                tt(tmp[R64, :, :], S0[R64, ks, :], q_[R64, ks].unsqueeze(2).to_broadcast([64, 32, 64]), ALU.mult,
                   ["S0", "q_"], ["tmp"])
                S.add("dve", lambda: V.tensor_reduce(out=oh[R64, :], in_=tmp[R64, :, :].rearrange("p k v -> p v k"),
                                                     axis=AX.X, op=ALU.add), ["tmp"], ["oh"])
                tt(o_[R64, :], o_[R64, :], oh[R64, :], ALU.add, ["o_", "oh"], ["o_"])
            S.dma("sp", out_st[l].rearrange("n h k v -> (n h) (k v)"), S0[R64, :, :].rearrange("p a b -> p (a b)"),
                  ["S0"], [P.key()])
            act(ta[R64, :], o_[R64, :], AF.Square, ["o_"], ["ta", "ss_"], accum_out=ss_[R64, :])
            rsqrt_(rr_[R64, :], ss_[R64, :], 1.0 / 64, ["ss_"], ["rr_"])
            stt(o_[R64, :], o_[R64, :], rr_[R64, 0:1], nw[R64, :], ALU.mult, ALU.mult, ["o_", "rr_", "nw"], ["o_"])
            act(g_[R64, :], g_[R64, :], AF.Silu, ["g_"], ["g_"])
            tt(o_[R64, :], o_[R64, :], g_[R64, :], ALU.mult, ["o_", "g_"], ["o_"])
            S.dma("sp", scr["M"][mix_col // 256].rearrange("n (h k) -> (n h) k", h=4), o_[R64, :],
                  ["o_"], [("scrM", mix_col)])

        linrec("ret", 0, "ret", s_outs["s_ret"], 0, 0)
        linrec("hgrn", 1024, "hgrn", s_outs["s_hgrn"], 256, 1)

        S.barrier()
        AR.reset()
        buf = AR.f32(3, 512)
        xn, acc, t5 = AR.f32(512), AR.f32(512), AR.f32(512)
        cw, cb = AR.f32(4, 512), AR.f32(512)
        dts, des = AR.f32(4), AR.f32(4)
        ar4 = AR.f32(4)
        dma_sb(buf[R16, :, :], st_in["conv"][l], [], ["buf"])
        dma_sb(xn[R16, :], scr["XBC"], ["scrP2"], ["xn"])
        dma_sb(dts[R16, :], scr["DT"], ["scrP3"], ["dts"])
        dma_sb(cw[R16, :, :].rearrange("p a b -> p (a b)"), rowsT[l][:, 0:2048].partition_broadcast(16), [], ["cw"])
        dma_sb(cb[R16, :], rowsT[l][:, 2048:2560].partition_broadcast(16), [], ["cb"])
        tt(acc[R16, :], xn[R16, :], cw[R16, 3, :], ALU.mult, ["xn", "cw"], ["acc"])
        tt(acc[R16, :], acc[R16, :], cb[R16, :], ALU.add, ["acc", "cb"], ["acc"])
        for i in range(3):
            tt(t5[R16, :], buf[R16, i, :], cw[R16, i, :], ALU.mult, ["buf", "cw"], ["t5"])
            tt(acc[R16, :], acc[R16, :], t5[R16, :], ALU.add, ["acc", "t5"], ["acc"])
        act(acc[R16, :], acc[R16, :], AF.Silu, ["acc"], ["acc"])
        S.dma("sp", s_outs["s_conv"][l][:, 0:2, :], buf[R16, 1:3, :], ["buf"], [P.key()])
        S.dma("sp", s_outs["s_conv"][l][:, 2, :], xn[R16, :], ["xn"], [P.key()])
        tt(dts[R16, :], dts[R16, :], rp_t[R16, 512:516], ALU.add, ["dts", "rp_t"], ["dts"])
        act(dts[R16, :], dts[R16, :], AF.Exp, ["dts"], ["dts"])
        ts(dts[R16, :], dts[R16, :], 1.0, None, ALU.add, None, ["dts"], ["dts"])
        act(dts[R16, :], dts[R16, :], AF.Ln, ["dts"], ["dts"])
        act(ar4[R16, :], rp_t[R16, 516:520], AF.Exp, ["rp_t"], ["ar4"])
        tt(des[R16, :], dts[R16, :], ar4[R16, :], ALU.mult, ["dts", "ar4"], ["des"])
        act(des[R16, :], des[R16, :], AF.Exp, ["des"], ["des"], scale=-1.0)
        S.dma("sp", scr["XM"], acc[R16, 0:256], ["acc"], [("scr2", 0)])
        S.dma("sp", scr["B"], acc[R16, 256:384], ["acc"], [("scr2", 3)])
        S.dma("sp", scr["C"], acc[R16, 384:512], ["acc"], [("scr2", 4)])
        S.dma("sp", scr["DT2"], dts[R16, :], ["dts"], [("scr2", 1)])
        S.dma("sp", scr["DE"], des[R16, :], ["des"], [("scr2", 2)])
        S.barrier()
        AR.reset()
        S0 = AR.f32(64, 64)
        tmp = AR.f32(32, 64)
        xm, zz = AR.f32(2, 64), AR.f32(2, 64)
        Bm, Cm = AR.f32(64), AR.f32(64)
        dtg, deg, dro = AR.f32(2), AR.f32(2), AR.f32(2)
        vdt, o_, oh, my = AR.f32(64), AR.f32(64), AR.f32(64), AR.f32(2, 64)
        dma_sb(xm[R32, :, :].rearrange("p a b -> p (a b)"), scr["XM"].rearrange("n (g x) -> (n g) x", g=2),
               [("scr2", 0)], ["xm"])
        dma_sb(Bm[R32, :], scr["B"].rearrange("n (g x) -> (n g) x", g=2), [("scr2", 3)], ["Bm"])
        dma_sb(Cm[R32, :], scr["C"].rearrange("n (g x) -> (n g) x", g=2), [("scr2", 4)], ["Cm"])
        dma_sb(dtg[R32, :], scr["DT2"].rearrange("n (g x) -> (n g) x", g=2), [("scr2", 1)], ["dtg"])
        dma_sb(deg[R32, :], scr["DE"].rearrange("n (g x) -> (n g) x", g=2), [("scr2", 2)], ["deg"])
        dma_sb(zz[R32, :, :].rearrange("p a b -> p (a b)"), scr["P"][10].rearrange("n (g x) -> (n g) x", g=2),
               ["scrP"], ["zz"])
        dma_sb(dro[R32, :], rowsM[l], [], ["dro"])
        act(zz[R32, :, :], zz[R32, :, :], AF.Silu, ["zz"], ["zz"])
        stv = st_in["m2"][l].rearrange("n (g e) k v -> (n g) e (k v)", e=2)
        sov = s_outs["s_m2"][l].rearrange("n (g e) k v -> (n g) e (k v)", e=2)
        for e in range(2):
            dma_sb(S0[R32, :, :].rearrange("p a b -> p (a b)"), stv[:, e, :], [], ["S0"])
            ts(vdt[R32, :], xm[R32, e, :], dtg[R32, e:e + 1], None, ALU.mult, None, ["xm", "dtg"], ["vdt"])
            memset(o_[R32, :], 0.0, "o_")
            for kh in range(2):
                ks = slice(kh * 32, kh * 32 + 32)
                tt(tmp[R32, :, :], Bm[R32, ks].unsqueeze(2).to_broadcast([32, 32, 64]),
                   vdt[R32, :].unsqueeze(1).to_broadcast([32, 32, 64]), ALU.mult, ["Bm", "vdt"], ["tmp"])
                stt(S0[R32, ks, :], S0[R32, ks, :], deg[R32, e:e + 1], tmp[R32, :, :], ALU.mult, ALU.add,
                    ["S0", "deg", "tmp"], ["S0"])
                tt(tmp[R32, :, :], S0[R32, ks, :], Cm[R32, ks].unsqueeze(2).to_broadcast([32, 32, 64]), ALU.mult,
                   ["S0", "Cm"], ["tmp"])
                S.add("dve", lambda: V.tensor_reduce(out=oh[R32, :], in_=tmp[R32, :, :].rearrange("p k v -> p v k"),
                                                     axis=AX.X, op=ALU.add), ["tmp"], ["oh"])
                tt(o_[R32, :], o_[R32, :], oh[R32, :], ALU.add, ["o_", "oh"], ["o_"])
            S.dma("sp", sov[:, e, :], S0[R32, :, :].rearrange("p a b -> p (a b)"), ["S0"], [P.key()])
            stt(my[R32, e, :], xm[R32, e, :], dro[R32, e:e + 1], o_[R32, :], ALU.mult, ALU.add, ["xm", "dro", "o_"], ["my"])
        tt(my[R32, :, :], my[R32, :, :], zz[R32, :, :], ALU.mult, ["my", "zz"], ["my"])
        S.dma("sp", scr["3"].rearrange("n (g x) -> (n g) x", g=2), my[R32, :, :].rearrange("p a b -> p (a b)"),
              ["my"], ["scr3"])
        mt, nwr = AR.f32(256), AR.f32(256)
        jk = AR.f32(256)
        ss_, rr_ = AR.f32(1), AR.f32(1)
        dma_sb(mt[R16, :], scr["3"], ["scr3"], ["mt"])
        dma_sb(nwr[R16, :], rowsT[l][:, 2560:2816].partition_broadcast(16), [], ["nwr"])
        act(jk[R16, :], mt[R16, :], AF.Square, ["mt"], ["jk", "ss_"], accum_out=ss_[R16, :])
        rsqrt_(rr_[R16, :], ss_[R16, :], 1.0 / 256, ["ss_"], ["rr_"])
        stt(mt[R16, :], mt[R16, :], rr_[R16, 0:1], nwr[R16, :], ALU.mult, ALU.mult, ["mt", "rr_", "nwr"], ["mt"])
        S.dma("sp", scr["M"][3], mt[R16, :], ["mt"], [("scrM", 768)])

        S.barrier()
        AR.reset()
        hnTs2 = AR.bf16(8, 16)
        xs2, junk2, hnb2 = AR.f32(D), AR.bf16(D), AR.bf16(D)
        ssx2, rsx2 = AR.f32(1), AR.f32(1)
        dma_sb(xs2[R16, :], scr["X"], ["scrX"], ["xs2"])
        act(junk2[R16, :], xs2[R16, :], AF.Square, ["xs2"], ["junk2", "ssx2"], accum_out=ssx2[R16, :])
        rsqrt_(rsx2[R16, :], ssx2[R16, :], 1.0 / D, ["ssx2"], ["rsx2"])
        ts(hnb2[R16, :], xs2[R16, :], rsx2[R16, 0:1], None, ALU.mult, None, ["xs2", "rsx2"], ["hnb2"])
        for kt in range(8):
            tr(psb[7][:, kt * 16:(kt + 1) * 16], hnb2[R16, kt * 128:(kt + 1) * 128], C["ident_bf"][R16, R16],
               ["hnb2", ("C", "ident_bf")], [PS(7)])
        tt(hnTs2, psb[7][:, 0:128].rearrange("p (k t) -> p k t", k=8),
           sp_t[:, SP_NORMW:SP_NORMW + 8].unsqueeze(2).to_broadcast([128, 8, 16]), ALU.mult, [PS(7), "sp_t"], ["hnTs2"])
        uTs, uTsb, sgs = AR.f32(2, 16), AR.bf16(2, 16), AR.f32(2, 16)
        for ut in range(2):
            for (c0_, dst, fn_) in ((2048, uTs, None), (2304, sgs, AF.Silu)):
                for kt in range(8):
                    mm(ps[3][:, 0:16], Win[:, kt, c0_ + ut * 128:c0_ + (ut + 1) * 128], hnTs2[:, kt, :], kt == 0, kt == 7,
                       ["hnTs2", ("Win", 2)], [PS(3)])
                if fn_ is None:
                    cp("act", dst[:, ut, :], ps[3][:, 0:16], [PS(3)], [("uTs", ut)])
                else:
                    act(dst[:, ut, :], ps[3][:, 0:16], fn_, [PS(3)], [("sgs", ut)])
            cp("dve", uTsb[:, ut, :], uTs[:, ut, :], [("uTs", ut)], [("uTsb", ut)])
        stt_ = AR.f32(2, D)
        h0 = [AR.f32(8, 16), AR.f32(8, 16)]
        hn_ = [AR.f32(8, 16), AR.f32(8, 16)]
        hnb_ = [AR.bf16(8, 16), AR.bf16(8, 16)]
        ta_, tb_ = AR.f32(8, 16), AR.f32(8, 16)
        for t_, nm in ((0, "s5re"), (1, "s5im")):
            dma_sb(stt_[R16, t_, :], st_in[nm][l].rearrange("n g p -> n (g p)"), [], [("stt", t_)])
            for j in range(8):
                tr(ps[4][:, t_ * 128 + j * 16:t_ * 128 + (j + 1) * 16], stt_[R16, t_, j * 128:(j + 1) * 128],
                   C["ident_f"][R16, R16], [("stt", t_), ("C", "ident_f")], [PS(4)])
            cp("dve", h0[t_], ps[4][:, t_ * 128:(t_ + 1) * 128].rearrange("p (a b) -> p a b", a=8), [PS(4)], [("h0", t_)])
        LBh = [AR.bf16(4, 128), AR.bf16(4, 128)]
        CXh = [AR.bf16(4, 128), AR.bf16(4, 128)]
        BXP = [AR.bf16(4, 128), AR.bf16(4, 128)]
        fl = lambda a_: a_.rearrange("p a b -> p (a b)")
        bb16 = lambda a_: a_.unsqueeze(2).to_broadcast([128, 8, 16])
        abre_, abim_ = AR.f32(8), AR.f32(8)
        tt(abre_, S5["rho"][:, :], S5["c1"][:, :], ALU.mult, ["s5rho", "s5c1"], ["abre_"])
        tt(abim_, S5["rho"][:, :], S5["s1"][:, :], ALU.mult, ["s5rho", "s5s1"], ["abim_"])
        tt(ta_, h0[0], bb16(abre_), ALU.mult, [("h0", 0), "abre_"], ["ta_"])
        tt(tb_, h0[1], bb16(abim_), ALU.mult, [("h0", 1), "abim_"], ["tb_"])
        tt(hn_[0], ta_, tb_, ALU.subtract, ["ta_", "tb_"], [("hn", 0)])
        tt(ta_, h0[1], bb16(abre_), ALU.mult, [("h0", 1), "abre_"], ["ta_"])
        tt(tb_, h0[0], bb16(abim_), ALU.mult, [("h0", 0), "abim_"], ["tb_"])
        tt(hn_[1], ta_, tb_, ALU.add, ["ta_", "tb_"], [("hn", 1)])
        for hf in range(2):
            for i_, (dst, kk) in enumerate(((LBh[0], ("LBh", 0)), (LBh[1], ("LBh", 1)), (CXh[0], ("CXh", 0)), (CXh[1], ("CXh", 1)))):
                S.dma("sp", fl(dst), tabB[hf, i_], ["bar", "s5tab"], [kk])
            for t_ in range(2):
                for jj in range(4):
                    mm(ps[t_][:, jj * 16:(jj + 1) * 16], LBh[t_][:, jj, :], uTsb[:, hf, :], True, True,
                       [("LBh", t_), ("uTsb", hf)], [PS(t_)])
                tt(hn_[t_][:, 4 * hf:4 * hf + 4, :], hn_[t_][:, 4 * hf:4 * hf + 4, :],
                   ps[t_][:, 0:64].rearrange("p (a b) -> p a b", a=4), ALU.add, [PS(t_), ("hn", t_)], [("hn", t_)])
                cp("act", hnb_[t_][:, 4 * hf:4 * hf + 4, :], hn_[t_][:, 4 * hf:4 * hf + 4, :], [("hn", t_)], [("hnb", t_)])
            for jj in range(4):
                mm(ps[5][:, hf * 16:(hf + 1) * 16], CXh[0][:, jj, :], hnb_[0][:, 4 * hf + jj, :], jj == 0, False,
                   [("CXh", 0), ("hnb", 0)], [PS(5)])
                mm(ps[5][:, hf * 16:(hf + 1) * 16], CXh[1][:, jj, :], hnb_[1][:, 4 * hf + jj, :], False, jj == 3,
                   [("CXh", 1), ("hnb", 1)], [PS(5)])
        S.barrier()
        for t_, nm in ((0, "s_s5re"), (1, "s_s5im")):
            for j in range(8):
                tr(ps[4][R16, j * 128 - (512 if j >= 4 else 0):(j + 1) * 128 - (512 if j >= 4 else 0)] if False else
                   ps[4 if j < 4 else 6][R16, (j % 4) * 128:(j % 4 + 1) * 128], hn_[t_][:, j, :], C["ident_f"][:],
                   [("hn", t_), ("C", "ident_f")], [PS(4), PS(6)])
            cp("dve", stt_[R16, t_, 0:512], ps[4][R16, :], [PS(4)], [("stt", t_)])
            cp("dve", stt_[R16, t_, 512:1024], ps[6][R16, :], [PS(6)], [("stt", t_)])
            S.dma("sp", s_outs[nm][l].rearrange("n g p -> n (g p)"), stt_[R16, t_, :], [("stt", t_)], [P.key()])
        sy, x2, sg = AR.f32(16), AR.f32(16), AR.f32(16)
        gyFs, gybs = AR.f32(2, 16), AR.bf16(2, 16)
        GWs = AR.bf16(2, 256)
        gvs = glu_d[l].rearrange("(ct p) o -> p ct o", p=128)
        dma_sb(GWs[:, :, :], gvs[:, :, :], [], ["GWs"], q="pool")
        C1 = float(np.sqrt(2.0 / np.pi))
        for ut in range(2):
            stt(sy, uTs[:, ut, :], sp_t[:, SP_S5D + ut:SP_S5D + ut + 1], ps[5][:, ut * 16:(ut + 1) * 16], ALU.mult, ALU.add,
                [("uTs", ut), "sp_t", PS(5)], ["sy"])
            tt(x2, sy, sy, ALU.mult, ["sy"], ["x2"])
            ts(x2, x2, 2.0 * C1 * 0.044715, 2.0 * C1, ALU.mult, ALU.add, ["x2"], ["x2"])
            tt(x2, x2, sy, ALU.mult, ["x2", "sy"], ["x2"])
            act(sg, x2, AF.Sigmoid, ["x2"], ["sg"])
            tt(gyFs[:, ut, :], sy, sg, ALU.mult, ["sy", "sg"], [("gyFs", ut)])
            cp("act", gybs[:, ut, :], gyFs[:, ut, :], [("gyFs", ut)], [("gybs", ut)])
        for ot in range(2):
            for ct in range(2):
                mm(ps[3][:, 0:16], GWs[:, ct, ot * 128:(ot + 1) * 128], gybs[:, ct, :], ct == 0, ct == 1,
                   ["GWs", ("gybs", ct)], [PS(3)])
            act(sg, ps[3][:, 0:16], AF.Sigmoid, [PS(3), "sp_t"], ["sg"], bias=sp_t[:, SP_GLUB + ot:SP_GLUB + ot + 1])
            tt(sg, sg, gyFs[:, ot, :], ALU.mult, ["sg", ("gyFs", ot)], ["sg"])
            tt(s5mixS[:, ot, :], sg, sgs[:, ot, :], ALU.mult, ["sg", ("sgs", ot)], [("s5mixS", ot)])

        S.barrier()
        AR.reset()
        xs = AR.f32(D)
        mx = AR.f32(D)
        mxb = AR.bf16(D)
        mTs = AR.bf16(8, 16)
        dma_sb(xs[R16, :], scr["X"], ["scrX"], ["xs"])
        memset(mx[R16, :], 0.0, "mx")
        for pc in (0, 1, 3):
            dma_sb(mx[R16, pc * 256:(pc + 1) * 256], scr["M"][pc], [("scrM", pc * 256)], ["mx"])
        cp("dve", mxb[R16, :], mx[R16, :], ["mx"], ["mxb"])
        for kt in range(8):
            tr(psb[7][:, kt * 16:(kt + 1) * 16], mxb[R16, kt * 128:(kt + 1) * 128], C["ident_bf"][R16, R16],
               ["mxb", ("C", "ident_bf")], [PS(7)])
        cp("dve", mTs, psb[7][:, 0:128].rearrange("p (k t) -> p k t", k=8), [PS(7)], ["mTs"])
        for ot in range(2):
            cp("dve", mTs[:, 4 + ot, :], s5mixS[:, ot, :], [("s5mixS", ot)], ["mTs"])
        for n_ in range(2):
            for kt in range(8):
                mm(ps[n_][R16, :], mTs[:, kt, :], Wout[:, kt, n_ * 512:(n_ + 1) * 512], kt == 0, kt == 7,
                   ["mTs", "Wout"], [PS(n_)])
            tt(xs[R16, n_ * 512:(n_ + 1) * 512], ps[n_][R16, :], xs[R16, n_ * 512:(n_ + 1) * 512], ALU.add,
               [PS(n_), "xs"], ["xs"])
        S.dma("sp", scr["X"], xs[R16, :], ["xs"], ["scrX"])

    def sample_final():
        S.barrier()
        AR.reset()
        xs, jk, fw = AR.f32(D), AR.f32(D), AR.f32(D)
        ssx, rsx = AR.f32(1), AR.f32(1)
        dma_sb(xs[R16, :], scr["X"], ["scrX"], ["xs"])
        dma_sb(fw[R16, :], fin_w.partition_broadcast(16), [], ["fw"])
        act(jk[R16, :], xs[R16, :], AF.Square, ["xs"], ["jk", "ssx"], accum_out=ssx[R16, :])
        rsqrt_(rsx[R16, :], ssx[R16, :], 1.0 / D, ["ssx"], ["rsx"])
        stt(xs[R16, :], xs[R16, :], rsx[R16, 0:1], fw[R16, :], ALU.mult, ALU.mult, ["xs", "rsx", "fw"], ["xs"])
        S.dma("sp", y_s, xs[R16, :], ["xs"], [P.key()])

    def out_proj(b):
        for j in range(4):
            i = 4 * b + j
            for n in range(2):
                for kt in range(8):
                    mm(ps[4 + n][:, :], mixT[:, kt, j * 128:(j + 1) * 128], Wout[:, kt, n * 512:(n + 1) * 512],
                       kt == 0, kt == 7, [("mixT", q) for q in range(8)] + ["Wout"], [PS(4 + n)])
                tt(x[:, i, n * 512:(n + 1) * 512], ps[4 + n][:, :], x[:, i, n * 512:(n + 1) * 512], ALU.add,
                   [PS(4 + n), ("x", i)], [("x", i)])

    P.ENABLE = getattr(build, "ENABLE", "ABCDS")
    load_weights(0)
    for l in range(DEPTH):
        S.barrier()
        nxt = l + 1 if l + 1 < DEPTH else None
        for m in ("ret", "hgrn"):
            for pr in range(2):
                memset(hst[m][:, pr, :], 0.0, ("hst", m, pr))
                for hh in range(2):
                    memset(hstbH[m][hh][:, pr, :], 0.0, ("hstbH", m, pr))
        for h in range(4):
            memset(hst["m2"][(h // 2) * 64:(h // 2) * 64 + 64, h % 2, :], 0.0, ("hst", "m2", h))
            memset(hstD_bf[h][:, :], 0.0, ("hstDbf", h))
        memset(xhist.rearrange("p a b -> p (a b)"), 0.0, "xhist")
        for q in range(8):
            memset(mixT[:, q, :], 0.0, ("mixT", q))
        if "C" in P.ENABLE:
            s5_params(l)
        if "S" in P.ENABLE:
            if l == 0:
                S.barrier()
                AR.reset()
                x0 = AR.f32(D)
                dma_sb(x0[R16, :], xs_in, [], ["x0"])
                S.dma("sp", scr["X"], x0[R16, :], ["x0"], ["scrX"])
            sample_layer(l)
        S.barrier()
        bufs = rmsnorm_alloc()
        for j in range(4):
            rmsnorm_tile(0, j, bufs)
        for b in range(NB):
            last = (b == NB - 1) and nxt is not None
            if "A" in P.ENABLE:
                phase_ret(l, b)
            if last:
                load_wgroup(nxt, 0)
            if "B" in P.ENABLE:
                phase_hgrn(l, b)
            if last:
                load_wgroup(nxt, 1)
            if "C" in P.ENABLE:
                phase_s5(l, b)
            if last:
                load_wgroup(nxt, 2)
            if "D" in P.ENABLE:
                phase_m2(l, b)
            if last:
                load_wgroup(nxt, 3)
            S.barrier()
            bufs = rmsnorm_alloc()
            for j in range(4):
                out_proj_tile(b, j)
                if b + 1 < NB:
                    rmsnorm_tile(b + 1, j, bufs)
        for pr in range(2):
            S.dma("sp", o_ret[l, 2 * pr:2 * pr + 2].rearrange("h k v -> (h k) v"), hst["ret"][:, pr, :],
                  [("hst", "ret", pr)], [P.key()])
            S.dma("sp", o_hgrn[l, 2 * pr:2 * pr + 2].rearrange("h k v -> (h k) v"), hst["hgrn"][:, pr, :],
                  [("hst", "hgrn", pr)], [P.key()])
        for g in range(2):
            S.dma("sp", o_m2[l, 2 * g:2 * g + 2].rearrange("hh n v -> n hh v"), hst["m2"][g * 64:(g + 1) * 64, :, :],
                  [("hst", "m2", 2 * g), ("hst", "m2", 2 * g + 1)], [P.key()])
        if "C" in P.ENABLE:
            S.barrier()
            tr(ps[1][0:8, 0:128], S5["hr"][:, :], C["ident_f"][:], ["s5hr", ("C", "ident_f")], [PS(1)])
            tr(ps[1][0:8, 128:256], S5["hi"][:, :], C["ident_f"][:], ["s5hi", ("C", "ident_f")], [PS(1)])
            AR.reset()
            s5T = AR.f32(256)
            cp("dve", s5T[0:8, :], ps[1][0:8, 0:256], [PS(1)], ["s5T"])
            S.dma("sp", o_s5re[l].rearrange("(j e) p -> j (e p)", e=2), s5T[0:8, 0:128], ["s5T"], [P.key()])
            S.dma("sp", o_s5im[l].rearrange("(j e) p -> j (e p)", e=2), s5T[0:8, 128:256], ["s5T"], [P.key()])
        S.barrier()
        tr(ps[0][0:12, 0:128], xhist.rearrange("p a b -> p (a b)"), C["ident_f"][:], ["xhist", ("C", "ident_f")], [PS(0)])
        AR.reset()
        AR.f32(256)
        xhT = AR.f32(128)
        cp("dve", xhT[0:12, :], ps[0][0:12, 0:128], [PS(0)], ["xhT"])
        for tl in range(4):
            S.dma("sp", o_conv[l][:, tl * 128:(tl + 1) * 128], xhT[3 * tl:3 * tl + 3, :], ["xhT"], [P.key()])
        if nxt is not None:
            load_wout(nxt)
            load_small(nxt)

    S.barrier()
    AR.reset()
    junk = AR.bf16(D)
    finw = AR.f32(D)
    S.dma("sp", finw, fin_w.partition_broadcast(128), ["bar"], ["finw"])
    for i in range(16):
        act(junk, x[:, i, :], AF.Square, [("x", i)], ["junk", ("ss", i)], accum_out=ss16[:, i:i + 1])
        rsqrt_(rstd[:, i:i + 1], ss16[:, i:i + 1], 1.0 / D, [("ss", i)], [("rstd", i)])
        stt(x[:, i, :], x[:, i, :], rstd[:, i:i + 1], finw, ALU.mult, ALU.mult,
            [("x", i), ("rstd", i), "finw"], [("x", i)])
        S.dma("sp", y_p[i * 128:(i + 1) * 128, :], x[:, i, :], [("x", i)], [P.key()])
    if "S" in P.ENABLE:
        sample_final()
    allout = [op["writes"][0] for op in S.ops if op["dma"] and op["writes"] and op["writes"][0][0] == "k"]
    S.add("sp", lambda: nc.sync.nop(), allout, ())
    n = S.emit()
    P.es.close()
    return P, n


def host_inputs(inputs, consts):
    f = lambda a: np.ascontiguousarray(np.asarray(a), dtype=np.float32)
    w_in = f(inputs["w_in"])
    sw = np.empty((DEPTH, D, 512), np.float32)
    for base, o in ((0, 0), (256, 256)):
        blk = w_in[:, :, base:base + 256].reshape(DEPTH, D, 4, 2, 32)
        sw[:, :, o:o + 256] = blk[:, :, :, ::-1, :].reshape(DEPTH, D, 256)
    pair = lambda v: np.ascontiguousarray(v.reshape(2, 128).T)
    smallp = np.zeros((DEPTH, 128, 96), np.float32)
    s5b = np.zeros((DEPTH, 128, 2, 8, 16), np.float32)
    s5c = np.zeros((DEPTH, 128, 2, 8, 16), np.float32)
    rowp = np.zeros((DEPTH, 1, 528), np.float32)
    for l in range(DEPTH):
        smallp[l, :, 0:8] = f(inputs["norm_w"])[l].reshape(8, 128).T
        smallp[l, :, 8:10] = pair(f(inputs["ret_norm_w"])[l])
        smallp[l, :, 10:12] = pair(f(inputs["hgrn_norm_w"])[l])
        smallp[l, :, 12:14] = pair(f(inputs["m2_norm_w"])[l])
        smallp[l, :, 14:16] = pair(np.repeat(f(inputs["m2_D"])[l], 64))
        smallp[l, :, 16:18] = pair(f(inputs["hgrn_lb_logits"])[0])
        smallp[l, :, 18:20] = pair(f(inputs["hgrn_lb_logits"])[1])
        smallp[l, :, 20:22] = pair(f(inputs["s5_D"])[l])
        smallp[l, :, 22:24] = pair(f(inputs["s5_glu_b"])[l])
        cw = f(inputs["m2_conv_w"])[l]
        smallp[l, :, 24:40] = cw.reshape(4, 4, 128).transpose(2, 1, 0).reshape(128, 16)
        smallp[l, :, 40:44] = f(inputs["m2_conv_b"])[l].reshape(4, 128).T
        pl = lambda a: np.ascontiguousarray(a.reshape(8, 2, 64).transpose(1, 2, 0).reshape(128, 8))
        smallp[l, :, 44:52] = pl(f(inputs["s5_A_re"])[l])
        smallp[l, :, 52:60] = pl(f(inputs["s5_A_im"])[l])
        smallp[l, :, 60:68] = pl(np.repeat(f(inputs["s5_log_dt"])[l][:, None], 64, axis=1))
        pb = lambda a: np.ascontiguousarray(a.reshape(8, 2, 64, 16).transpose(1, 2, 0, 3).reshape(128, 8, 16))
        s5b[l, :, 0] = pb(f(inputs["s5_B_re"])[l])
        s5b[l, :, 1] = pb(f(inputs["s5_B_im"])[l])
        pc = lambda a: np.ascontiguousarray(a.reshape(8, 2, 16, 64).transpose(1, 3, 0, 2).reshape(128, 8, 16))
        s5c[l, :, 0] = pc(f(inputs["s5_C_re"])[l])
        s5c[l, :, 1] = pc(f(inputs["s5_C_im"])[l])
        rowp[l, 0, 0:256] = f(inputs["hgrn_lb_logits"])[0]
        rowp[l, 0, 256:512] = f(inputs["hgrn_lb_logits"])[1]
        rowp[l, 0, 512:516] = f(inputs["m2_dt_bias"])[l]
        rowp[l, 0, 516:520] = f(inputs["m2_A_log"])[l]
    rowsS = np.zeros((DEPTH, 64, 4, 64), np.float32)
    rowsM = np.zeros((DEPTH, 32, 2), np.float32)
    rowsT = np.zeros((DEPTH, 1, 3840), np.float32)
    for l in range(DEPTH):
        rep = lambda v: np.tile(v.reshape(1, 4, 64), (16, 1, 1)).reshape(64, 64)
        rowsS[l, :, 0] = rep(f(inputs["ret_norm_w"])[l])
        rowsS[l, :, 1] = rep(f(inputs["hgrn_norm_w"])[l])
        rowsS[l, :, 2] = rep(f(inputs["hgrn_lb_logits"])[0])
        rowsS[l, :, 3] = rep(f(inputs["hgrn_lb_logits"])[1])
        rowsM[l] = np.tile(f(inputs["m2_D"])[l].reshape(1, 2, 2), (16, 1, 1)).reshape(32, 2)
        rowsT[l, 0, 0:2048] = f(inputs["m2_conv_w"])[l].reshape(-1)
        rowsT[l, 0, 2048:2560] = f(inputs["m2_conv_b"])[l]
        rowsT[l, 0, 2560:2816] = f(inputs["m2_norm_w"])[l]
        rowsT[l, 0, 2816:3072] = f(inputs["s5_glu_b"])[l]
    shared = dict(w_in=w_in, w_sw=sw, w_out=f(inputs["w_out"]), smallp=smallp, rowp=rowp, s5b=s5b, s5c=s5c,
                  rowsS=rowsS, rowsM=rowsM, rowsT=rowsT,
                  glu_w=f(inputs["s5_glu_w"]),
                  fin_w=f(inputs["final_norm_w"]).reshape(1, D))
    for k, v in consts.items():
        shared["c_" + k] = v
    maps = []
    xp = f(inputs["x_prompt"])
    xsm = f(inputs["x_sample"]).reshape(128, D)
    sts = {"st_ret": f(inputs["state_ret"]), "st_hgrn": f(inputs["state_hgrn"]), "st_m2": f(inputs["state_m2_ssm"]),
           "st_conv": f(inputs["state_m2_conv"]), "st_s5re": f(inputs["state_s5_re"]), "st_s5im": f(inputs["state_s5_im"])}
    for c in range(8):
        m = dict(shared)
        m["x_prompt"] = xp[c]
        m["x_sample"] = xsm[c * NS:(c + 1) * NS]
        for k_, v_ in sts.items():
            m[k_] = np.ascontiguousarray(v_[:, c * NS:(c + 1) * NS])
        maps.append(m)
    return maps


_CACHE = {}


def kernel(**inputs):
    consts = make_consts()
    if "prog" not in _CACHE:
        _CACHE["prog"] = build(consts)
    P, n = _CACHE["prog"]
    maps = host_inputs(inputs, consts)
    maps = [{k: v for k, v in m.items() if k in P.din} for m in maps]
    res = run_bass_kernel_spmd(P.nc, maps, core_ids=list(range(8)))
    R = res.results
    g = lambda name: np.stack([np.asarray(R[c][name]) for c in range(8)], 0)
    y_prompt = g("y_prompt")
    pst = lambda n: np.ascontiguousarray(np.moveaxis(g(n), 0, 1))
    sst = lambda n: np.ascontiguousarray(np.concatenate([np.asarray(R[c][n]) for c in range(8)], axis=1))
    y_sample = np.concatenate([np.asarray(R[c]["y_sample"]) for c in range(8)], 0).reshape(128, 1, D)
    return (y_prompt, y_sample,
            pst("p_ret"), pst("p_hgrn"), pst("p_s5re"), pst("p_s5im"), pst("p_m2"), pst("p_conv"),
            sst("s_ret"), sst("s_hgrn"), sst("s_s5re"), sst("s_s5im"), sst("s_m2"), sst("s_conv"))
```

```python
import numpy as np
import ml_dtypes
from contextlib import ExitStack
import concourse.bass as bass
import concourse.mybir as mybir
from concourse.bass_utils import run_bass_kernel_spmd

F32 = mybir.dt.float32
BF16 = mybir.dt.bfloat16
AF = mybir.ActivationFunctionType
ALU = mybir.AluOpType
AX = mybir.AxisListType

D = 1024
L = 2048
NS = 16
DEPTH = 2
PT = 3332
PAST = 16384
EPS = 1e-6
NB = 4
BL = 512
GAM = [1.0 - 2.0 ** (-5.0 - h) for h in range(4)]


class Sched:
    def __init__(self, nc, es):
        self.nc = nc
        self.ops = []
        self.eng = {"pe": nc.tensor, "act": nc.scalar, "dve": nc.vector, "sp": nc.sync, "pool": nc.gpsimd}
        self.es = es
        self.epoch = 0
        self.sem = {(e, 0): es.enter_context(nc.semaphore("sem_%s_0" % e)) for e in ("pe", "act", "dve")}
        self.nslots = 40
        self.dsem = [es.enter_context(nc.semaphore("dsem%d" % i)) for i in range(self.nslots)]

    def add(self, eng, fn, reads=(), writes=(), dma=False):
        self.ops.append(dict(eng=eng, fn=fn, reads=tuple(reads), writes=tuple(writes), dma=dma))

    def barrier(self):
        for e in ("act", "dve"):
            eng = self.eng[e]
            self.ops.append(dict(eng=e, fn=(lambda eng=eng: eng.nop()), reads=(),
                                 writes=(("bar",) if e == "dve" else ()), dma=False, barrier=True))

    def dma(self, q, out, in_, reads=(), writes=()):
        e = self.eng[q]
        self.add(q, lambda: e.dma_start(out=out, in_=in_), reads, writes, dma=True)

    def emit(self):
        ops = self.ops
        last_w, readers = {}, {}
        deps = []
        needed = [bool(op['dma']) for op in ops]
        last_on = {}
        dma_since = set()
        for i, op in enumerate(ops):
            d = set()
            if op.get("barrier"):
                d |= {j for e2, j in last_on.items() if not (e2 == "pe" and op["eng"] == "pe")}
                d |= dma_since
            for k in op["reads"]:
                if k in last_w:
                    d.add(last_w[k])
            for k in op["writes"]:
                if k in last_w:
                    d.add(last_w[k])
                d |= readers.get(k, set())
            d.discard(i)
            if op["eng"] == "pe":
                d = {j for j in d if not (ops[j]["eng"] == "pe" and not ops[j]["dma"])}
            deps.append(d)
            if op["dma"]:
                dma_since.add(i)
            elif op.get("barrier"):
                if op["eng"] == "dve":
                    dma_since = set()
            else:
                last_on[op["eng"]] = i
            for j in d:
                needed[j] = True
            for k in op["writes"]:
                last_w[k] = i
                readers[k] = set()
            for k in op["reads"]:
                readers.setdefault(k, set()).add(i)
        cnt = {e: 0 for e in ("pe", "act", "dve")}
        token = {}
        waited = {}
        slot_use = [0] * self.nslots
        slot_rr = 0
        for i, op in enumerate(ops):
            e = op["eng"]
            eng = self.eng[e]
            w = waited.setdefault(e, {})
            need = {}
            for j in deps[i]:
                s, v = token[j]
                need[s] = max(need.get(s, 0), v)
            if op["dma"] and needed[i]:
                if e == "pool":
                    slot = len(self.dsem)
                    self.dsem.append(self.es.enter_context(self.nc.semaphore("psem%d" % slot)))
                    slot_use.append(0)
                else:
                    slot = slot_rr
                    slot_rr = (slot_rr + 1) % self.nslots
                    if slot_use[slot] > 0:
                        s = ("d", slot)
                        need[s] = max(need.get(s, 0), 16 * slot_use[slot])
            for s, v in need.items():
                if w.get(s, 0) < v:
                    semh = self.dsem[s[1]] if s[0] == "d" else self.sem[s]
                    eng.wait_ge(semh, v)
                    w[s] = v
            ins = op["fn"]()
            if needed[i]:
                if op["dma"]:
                    slot_use[slot] += 1
                    ins.then_inc(self.dsem[slot], 16)
                    token[i] = (("d", slot), 16 * slot_use[slot])
                else:
                    cnt[e] += 1
                    ins.then_inc(self.sem[(e, self.epoch)], 1)
                    token[i] = ((e, self.epoch), cnt[e])
            if op.get("barrier") and e == "dve" and max(cnt.values()) > 600:
                self.epoch += 1
                for e2 in cnt:
                    cnt[e2] = 0
                    self.sem[(e2, self.epoch)] = self.es.enter_context(
                        self.nc.semaphore("sem_%s_%d" % (e2, self.epoch)))
        return len(ops)


def _bf(a):
    return np.ascontiguousarray(a).astype(ml_dtypes.bfloat16)


def make_consts():
    c = {}
    c["ident_bf"] = _bf(np.eye(128))
    c["ident_f"] = np.eye(128, dtype=np.float32)
    s = np.arange(128)
    dm = np.zeros((128, 4, 128), np.float32)
    qdec = np.zeros((128, 2, 128), np.float32)
    kend = np.zeros((128, 2, 128), np.float32)
    for h in range(4):
        g = GAM[h]
        diff = s[None, :] - s[:, None]
        dm[:, h, :] = np.where(diff >= 0, 0.125 * g ** np.maximum(diff, 0).astype(np.float64), 0.0)
        qdec[(h % 2) * 64:(h % 2) * 64 + 64, h // 2, :] = (g ** (s + 1.0))[None, :]
        kend[:, h // 2, (h % 2) * 64:(h % 2) * 64 + 64] = (0.125 * g ** (127.0 - s))[:, None]
    c["dmT"] = dm
    c["qdec"] = qdec
    c["kend"] = kend
    half = 32
    inv = 1.0 / (10000.0 ** (np.arange(half, dtype=np.float32) / half))
    pos = np.arange(L, dtype=np.float32)
    ang = (pos[None, :] * inv[:, None]).astype(np.float32)
    cs, sn = np.cos(ang).astype(np.float32), np.sin(ang).astype(np.float32)
    c["cosT"] = np.concatenate([cs, cs, cs, cs], 0)
    c["sinT"] = np.concatenate([-sn, sn, -sn, sn], 0)
    angs = (np.float32(PAST) * inv).astype(np.float32)
    c["cos_s"] = np.tile(np.cos(angs).astype(np.float32)[None, :], (64, 1))
    c["sin_s"] = np.tile(np.sin(angs).astype(np.float32)[None, :], (64, 1))
    c["gam_s"] = np.tile(np.array(GAM, np.float32)[None, :], (16, 1)).reshape(64, 1)
    caus = (s[:, None] <= s[None, :])
    same64 = (s[:, None] // 64) == (s[None, :] // 64)
    c["maskBD"] = (caus & same64).astype(np.float32)
    c["mstrBD"] = ((s[:, None] > s[None, :]) & same64).astype(np.float32)
    c["U128"] = caus.astype(np.float32)
    c["M128"] = (s[:, None] > s[None, :]).astype(np.float32)
    c["negm"] = np.where(caus, 0.0, -60000.0).astype(np.float32)
    blk = np.zeros((128, 128), np.float32)
    blk[:64, :64] = 1
    blk[64:, 64:] = 1
    c["blkones"] = _bf(blk)
    c["allones"] = _bf(np.ones((128, 128)))
    c["ones_f"] = np.ones((128, 128), np.float32)
    me = np.zeros((128, 2, 16), np.float32)
    me[:64, 0, :] = 1
    me[64:, 1, :] = 1
    c["maskE"] = me
    c["iotaR"] = np.tile(np.arange(128, dtype=np.float32)[None, :], (128, 1))
    used = ("cos_s", "sin_s", "gam_s", "maskE", "iotaR", "ones_f", "ident_f", "ident_bf", "dmT", "qdec", "kend", "cosT", "sinT", "blkones", "allones", "maskBD", "mstrBD", "U128", "M128", "negm")
    return {k: v for k, v in c.items() if k in used}


CONST_SPECS = None


class Prog:
    def __init__(self, consts):
        self.consts = consts
        self.nc = bass.Bass("TRN2", target_bir_lowering=False)
        self.es = ExitStack()
        self.S = Sched(self.nc, self.es)
        self.din = {}
        self.dout = {}
        self.uid = 0

    def inp(self, name, shape, dt=F32):
        t = self.nc.dram_tensor(name, list(shape), dt, kind="ExternalInput").ap()
        self.din[name] = t
        return t

    def outp(self, name, shape):
        t = self.nc.dram_tensor(name, list(shape), F32, kind="ExternalOutput").ap()
        self.dout[name] = t
        return t

    def sb(self, name, shape, dt=F32):
        return self.es.enter_context(self.nc.sbuf_tensor(name, list(shape), dt))

    def key(self):
        self.uid += 1
        return ("k", self.uid)


def build(consts):
    P = Prog(consts)
    nc, S = P.nc, P.S
    V, A, T = nc.vector, nc.scalar, nc.tensor

    x_in = P.inp("x_prompt", [L, D])
    w_in = P.inp("w_in", [DEPTH, D, PT])
    w_sw = P.inp("w_sw", [DEPTH, D, 512])
    w_out = P.inp("w_out", [DEPTH, D, D])
    smallp = P.inp("smallp", [DEPTH, 128, 96])
    s5b_d = P.inp("s5b", [DEPTH, 128, 2, 8, 16])
    s5c_d = P.inp("s5c", [DEPTH, 128, 2, 8, 16])
    glu_d = P.inp("glu_w", [DEPTH, 256, 256])
    rowp = P.inp("rowp", [DEPTH, 1, 528])
    fin_w = P.inp("fin_w", [1, D])
    xs_in = P.inp("x_sample", [NS, D])
    st_in = {"ret": P.inp("st_ret", [DEPTH, NS, 4, 64, 64]), "hgrn": P.inp("st_hgrn", [DEPTH, NS, 4, 64, 64]),
             "m2": P.inp("st_m2", [DEPTH, NS, 4, 64, 64]), "conv": P.inp("st_conv", [DEPTH, NS, 3, 512]),
             "s5re": P.inp("st_s5re", [DEPTH, NS, 16, 64]), "s5im": P.inp("st_s5im", [DEPTH, NS, 16, 64])}
    rowsS = P.inp("rowsS", [DEPTH, 64, 4, 64])
    rowsM = P.inp("rowsM", [DEPTH, 32, 2])
    rowsT = P.inp("rowsT", [DEPTH, 1, 3840])
    tabF = nc.dram_tensor("s5tabF", [2, 3, 128, 512], F32, kind="Internal").ap()
    tabB = nc.dram_tensor("s5tabB", [2, 4, 128, 512], BF16, kind="Internal").ap()
    scr = {k_: nc.dram_tensor("scr_" + k_, sh, F32, kind="Internal").ap() for k_, sh in
           (("X", [NS, D]), ("P", [13, NS, 256]), ("XBC", [NS, 512]), ("DT", [NS, 4]), ("M", [4, NS, 256]),
            ("XM", [NS, 256]), ("B", [NS, 128]), ("C", [NS, 128]), ("DT2", [NS, 4]), ("DE", [NS, 4]),
            ("3", [NS, 256]), ("4", [NS, 256]))}
    cdram = {}
    for k, v in consts.items():
        cdram[k] = P.inp("c_" + k, v.shape, BF16 if v.dtype == ml_dtypes.bfloat16 else F32)
    y_p = P.outp("y_prompt", [L, D])
    o_ret = P.outp("p_ret", [DEPTH, 4, 64, 64])
    o_hgrn = P.outp("p_hgrn", [DEPTH, 4, 64, 64])
    o_m2 = P.outp("p_m2", [DEPTH, 4, 64, 64])
    o_conv = P.outp("p_conv", [DEPTH, 3, 512])
    o_s5re = P.outp("p_s5re", [DEPTH, 16, 64])
    o_s5im = P.outp("p_s5im", [DEPTH, 16, 64])
    y_s = P.outp("y_sample", [NS, D])
    s_outs = {"s_ret": P.outp("s_ret", [DEPTH, NS, 4, 64, 64]), "s_hgrn": P.outp("s_hgrn", [DEPTH, NS, 4, 64, 64]),
              "s_s5re": P.outp("s_s5re", [DEPTH, NS, 16, 64]), "s_s5im": P.outp("s_s5im", [DEPTH, NS, 16, 64]),
              "s_m2": P.outp("s_m2", [DEPTH, NS, 4, 64, 64]), "s_conv": P.outp("s_conv", [DEPTH, NS, 3, 512])}

    x = P.sb("x", [128, 16, D])
    Win = P.sb("Win", [128, 8, PT], BF16)
    Wsw = P.sb("Wsw", [128, 8, 512], BF16)
    Wout = P.sb("Wout", [128, 8, D], BF16)
    hnT = P.sb("hnT", [128, 8, BL], BF16)
    mixT = P.sb("mixT", [128, 8, BL], BF16)
    sp_t = P.sb("sp_t", [128, 96])
    rp_t = P.sb("rp_t", [128, 528])
    C = {}
    for k, v in consts.items():
        if k in ("cosT", "sinT"):
            continue
        C[k] = P.sb("C_" + k, v.shape, BF16 if v.dtype == ml_dtypes.bfloat16 else F32)
    ss16 = P.sb("ss16", [128, 16])
    rstd = P.sb("rstd", [128, 16])
    hst = {m: P.sb("hst_" + m, [128, 2, 64]) for m in ("ret", "hgrn", "m2")}
    hstbH = {m: [P.sb("hstbH_%s%d" % (m, i), [128, 2, 64], BF16) for i in range(2)] for m in ("ret", "hgrn")}
    hstD_bf = [P.sb("hstDbf%d" % h, [128, 64], BF16) for h in range(4)]
    xhist = P.sb("xhist", [128, 4, 3])
    S5 = {n_: P.sb("s5_" + n_, [128, 8]) for n_ in ("th", "rho", "c1", "s1", "hr", "hi")}
    s5mixS = P.sb("s5mixS", [128, 2, 16], BF16)
    BXc = [P.sb("BXc%d" % i, [128, 8, 32], BF16) for i in range(2)]
    CXc = [P.sb("CXc%d" % i, [128, 8, 32], BF16) for i in range(2)]
    AW = 8500
    arena_t = P.sb("arena", [128, AW])
    arena_f = arena_t[:, :]
    arena_b = arena_t[:, :].bitcast(BF16)
    arena_i = arena_t[:, :].bitcast(mybir.dt.int32)

    class Arena:
        off = 0

        def reset(self):
            self.off = 0

        def _shape(self, ap, shape):
            if len(shape) == 1:
                return ap
            if len(shape) == 2:
                return ap.rearrange("p (a b) -> p a b", a=shape[0])
            return ap.rearrange("p (a b c) -> p a b c", a=shape[0], b=shape[1])

        def f32(self, *shape):
            n = int(np.prod(shape))
            ap = arena_f[:, self.off:self.off + n]
            self.off += n
            assert self.off <= AW, self.off
            return self._shape(ap, shape)

        def i32_like(self, f32_ap_off, n):
            return arena_i[:, f32_ap_off:f32_ap_off + n]

        def bf16(self, *shape):
            n = int(np.prod(shape))
            ap = arena_b[:, 2 * self.off:2 * self.off + n]
            self.off += (n + 1) // 2
            assert self.off <= AW, self.off
            return self._shape(ap, shape)

    AR = Arena()
    ps = [P.es.enter_context(nc.psum_tensor("ps%d" % i, [128, 512], F32)) for i in range(8)]
    psb = [p_[:, 0:512].bitcast(BF16) for p_ in ps]

    def PS(i):
        return ("ps", i)

    def act(out, in_, func, reads, writes, **kw):
        S.add("act", lambda: A.activation(out=out, in_=in_, func=func, **kw), reads, writes)

    def tt(out, in0, in1, op, reads, writes):
        S.add("dve", lambda: V.tensor_tensor(out=out, in0=in0, in1=in1, op=op), reads, writes)

    def ts(out, in0, s1, s2, op0, op1, reads, writes):
        if op1 is None:
            S.add("dve", lambda: V.tensor_scalar(out=out, in0=in0, scalar1=s1, scalar2=None, op0=op0), reads, writes)
        else:
            S.add("dve", lambda: V.tensor_scalar(out=out, in0=in0, scalar1=s1, scalar2=s2, op0=op0, op1=op1),
                  reads, writes)

    def stt(out, in0, scalar, in1, op0, op1, reads, writes):
        S.add("dve", lambda: V.scalar_tensor_tensor(out=out, in0=in0, scalar=scalar, in1=in1, op0=op0, op1=op1),
              reads, writes)

    def cp(eng, out, in_, reads, writes):
        if eng == "act":
            S.add("act", lambda: A.copy(out=out, in_=in_), reads, writes)
        else:
            S.add("dve", lambda: V.tensor_copy(out=out, in_=in_), reads, writes)

    def memset(ap, val, key):
        S.add("dve", lambda: V.memset(ap, val), [key], [key])

    def mm(out, lhsT, rhs, start, stop, reads, writes):
        S.add("pe", lambda: T.matmul(out, lhsT=lhsT, rhs=rhs, start=start, stop=stop), reads, writes)

    def tr(out, in_, ident, reads, writes):
        S.add("pe", lambda: T.transpose(out, in_, ident), reads, writes)

    def rsqrt_(out, in_, scale, reads, writes):
        ts(out, in_, scale, EPS, ALU.mult, ALU.add, reads, writes)
        act(out, out, AF.Ln, writes, writes)
        act(out, out, AF.Exp, writes, writes, scale=-0.5)

    for k in consts:
        if k in ("cosT", "sinT"):
            continue
        S.dma("sp", C[k][:], cdram[k], (), [("C", k)])
    for i in range(16):
        S.dma("sp", x[:, i, :], x_in[i * 128:(i + 1) * 128, :], (), [("x", i)])

    def load_wgroup(l, gi):
        wv = w_in[l].rearrange("(kt p) c -> p kt c", p=128)
        a_, b_ = WGRP[gi]
        S.dma("pool", Win[:, :, a_:b_], wv[:, :, a_:b_], (), [("Win", gi)])
        if gi == 0:
            wsv = w_sw[l].rearrange("(kt p) c -> p kt c", p=128)
            S.dma("pool", Wsw[:, :, :], wsv[:, :, :], (), ["Wsw"])

    def load_wout(l):
        wov = w_out[l].rearrange("(kt p) c -> p kt c", p=128)
        S.dma("pool", Wout[:, :, :], wov[:, :, :], (), ["Wout"])

    def load_small(l):
        S.dma("sp", sp_t[:], smallp[l], (), ["sp_t"])
        S.dma("sp", rp_t[:], rowp[l].partition_broadcast(128), (), ["rp_t"])

    def load_weights(l):
        for gi in range(4):
            load_wgroup(l, gi)
        load_wout(l)
        load_small(l)

    SP_NORMW, SP_RETNW, SP_HGNW, SP_M2NW, SP_M2D, SP_LBL = 0, 8, 10, 12, 14, 16
    SP_S5D, SP_GLUB, SP_CW, SP_CB, SP_ARE, SP_AIM = 20, 22, 24, 40, 44, 52

    def head_norm(o_bank, nwcol, gate_ap, gate_key, out_ap, out_key, blk_lhsT, scale, sqb, rr, onb,
                  keys=("sqb", "rr", "onb")):
        ks, kr, ko = keys
        act(sqb, ps[o_bank][:, :], AF.Square, [PS(o_bank)], [ks])
        mm(ps[3][:, :], blk_lhsT, sqb, True, True, [ks, ("C", "blkones"), ("C", "allones")], [PS(3)])
        rsqrt_(rr, ps[3][:, :], scale, [PS(3)], [kr])
        stt(onb, ps[o_bank][:, :], sp_t[:, nwcol:nwcol + 1], rr, ALU.mult, ALU.mult,
            [PS(o_bank), kr, "sp_t"], [ko])
        tt(out_ap, onb, gate_ap, ALU.mult, [ko, gate_key], [out_key])

    def head_norm2(o_banks, nwcol0, gates, gate_keys, outs, out_keys, bufs, ss_banks):
        for pr in range(2):
            sqb_, rr_, onb_, (ks, kr, ko) = bufs[pr]
            act(sqb_, ps[o_banks[pr]][:, :], AF.Square, [PS(o_banks[pr])], [ks])
        for pr in range(2):
            sqb_, rr_, onb_, (ks, kr, ko) = bufs[pr]
            mm(ps[ss_banks[pr]][:, :], C["blkones"][:], sqb_, True, True, [ks, ("C", "blkones")], [PS(ss_banks[pr])])
        for pr in range(2):
            sqb_, rr_, onb_, (ks, kr, ko) = bufs[pr]
            ts(rr_, ps[ss_banks[pr]][:, :], 1.0 / 64, EPS, ALU.mult, ALU.add, [PS(ss_banks[pr])], [kr])
        for pr in range(2):
            sqb_, rr_, onb_, (ks, kr, ko) = bufs[pr]
            act(rr_, rr_, AF.Ln, [kr], [kr])
        for pr in range(2):
            sqb_, rr_, onb_, (ks, kr, ko) = bufs[pr]
            act(rr_, rr_, AF.Exp, [kr], [kr], scale=-0.5)
        for pr in range(2):
            sqb_, rr_, onb_, (ks, kr, ko) = bufs[pr]
            stt(onb_, ps[o_banks[pr]][:, :], sp_t[:, nwcol0 + pr:nwcol0 + pr + 1], rr_, ALU.mult, ALU.mult,
                [PS(o_banks[pr]), kr, "sp_t"], [ko])
        for pr in range(2):
            sqb_, rr_, onb_, (ks, kr, ko) = bufs[pr]
            tt(outs[pr], onb_, gates[pr], ALU.mult, [ko, gate_keys[pr]], [out_keys[pr]])

    def state_update(m, pr, kend_ap, kend_key, v_ap, v_key, dec, dec_keys=(), bank=1):
        mm(ps[bank][:, 0:128], kend_ap, v_ap, True, True, [kend_key, v_key], [PS(bank)])
        for hh in range(2):
            r = slice(hh * 64, hh * 64 + 64)
            stt(hst[m][r, pr, :], hst[m][r, pr, :], dec[hh], ps[bank][r, hh * 64:hh * 64 + 64], ALU.mult, ALU.add,
                [PS(bank), ("hst", m, pr)] + list(dec_keys), [("hst", m, pr)])
            cp("act", hstbH[m][hh][r, pr, :], hst[m][r, pr, :], [("hst", m, pr)], [("hstbH", m, pr)])

    def rmsnorm_alloc():
        AR.reset()
        return [AR.bf16(D), AR.bf16(D)], [AR.bf16(D), AR.bf16(D)]

    def rmsnorm_tile(b, j, bufs):
        junks, hnbs = bufs
        i = 4 * b + j
        junk, hnb, pb = junks[j % 2], hnbs[j % 2], (7 if j % 2 == 0 else 3)
        act(junk, x[:, i, :], AF.Square, [("x", i)], [("junk", j % 2), ("ss", i)], accum_out=ss16[:, i:i + 1])
        rsqrt_(rstd[:, i:i + 1], ss16[:, i:i + 1], 1.0 / D, [("ss", i)], [("rstd", i)])
        ts(hnb, x[:, i, :], rstd[:, i:i + 1], None, ALU.mult, None, [("x", i), ("rstd", i)], [("hnb", j % 2)])
        pt = psb[pb]
        for kt in range(8):
            tr(pt[:, kt * 128:(kt + 1) * 128], hnb[:, kt * 128:(kt + 1) * 128], C["ident_bf"][:],
               [("hnb", j % 2), ("C", "ident_bf")], [PS(pb)])
        tt(hnT[:, :, j * 128:(j + 1) * 128], pt.rearrange("p (k t) -> p k t", k=8),
           sp_t[:, SP_NORMW:SP_NORMW + 8].unsqueeze(2).to_broadcast([128, 8, 128]), ALU.mult,
           [PS(pb), "sp_t"], ["hnT"])

    def out_proj_tile(b, j):
        i = 4 * b + j
        for n in range(2):
            for kt in range(8):
                mm(ps[4 + n][:, :], mixT[:, kt, j * 128:(j + 1) * 128], Wout[:, kt, n * 512:(n + 1) * 512],
                   kt == 0, kt == 7, [("mixT", q) for q in range(8)] + ["Wout"], [PS(4 + n)])
            tt(x[:, i, n * 512:(n + 1) * 512], ps[4 + n][:, :], x[:, i, n * 512:(n + 1) * 512], ALU.add,
               [PS(4 + n), ("x", i)], [("x", i)])

    WGRP = ((0, 1024), (1024, 2048), (2048, 2560), (2560, PT))

    def wkey(c0):
        for gi, (a_, b_) in enumerate(WGRP):
            if a_ <= c0 < b_:
                return ("Win", gi)

    def proj_fm(bank, wt, c0, reads=None):
        if reads is None:
            reads = (wkey(c0),)
        for kt in range(8):
            mm(ps[bank][:, :], wt[:, kt, c0:c0 + 128], hnT[:, kt, :], kt == 0, kt == 7,
               ["hnT"] + list(reads), [PS(bank)])

    def proj_tm(bank, c0, n, j, ncol0=0):
        for kt in range(8):
            mm(ps[bank][:, ncol0:ncol0 + n], hnT[:, kt, j * 128:(j + 1) * 128], Win[:, kt, c0:c0 + n], kt == 0, kt == 7,
               ["hnT", wkey(c0)], [PS(bank)])

    def phase_ret(l, b):
        S.barrier()
        AR.reset()
        tA, tB, tC, tD = AR.f32(BL), AR.f32(BL), AR.f32(BL), AR.f32(BL)
        qrot, krot, qdd, gs = AR.bf16(2, BL), AR.bf16(2, BL), AR.bf16(2, BL), AR.bf16(2, BL)
        krotH = [AR.bf16(2, BL), AR.bf16(2, BL)]
        v_tm = AR.bf16(4, 256)
        PT2 = [AR.bf16(2, 128), AR.bf16(2, 128)]
        kend2 = [AR.bf16(128), AR.bf16(128)]
        sqb, rr, onb = AR.bf16(BL), AR.f32(BL), AR.f32(BL)
        C["cosT"], C["sinT"] = AR.f32(BL), AR.f32(BL)
        for hh in range(2):
            for pr in range(2):
                memset(krotH[hh][:, pr, :], 0.0, ("krotH", hh, pr))
        S.dma("sp", C["cosT"], cdram["cosT"][:, b * BL:(b + 1) * BL], ["bar"], [("C", "cosT")])
        S.dma("sp", C["sinT"], cdram["sinT"][:, b * BL:(b + 1) * BL], ["bar"], [("C", "sinT")])
        for pr in range(2):
            proj_fm(0, Win, pr * 128)
            proj_fm(1, Wsw, pr * 128, reads=("Wsw",))
            proj_fm(2, Win, 256 + pr * 128)
            proj_fm(4, Wsw, 256 + pr * 128, reads=("Wsw",))
            proj_fm(5, Win, 768 + pr * 128)
            tt(tA, ps[0][:, :], C["cosT"], ALU.mult, [PS(0), ("C", "cosT")], ["tA"])
            tt(tB, ps[1][:, :], C["sinT"], ALU.mult, [PS(1), ("C", "sinT")], ["tB"])
            tt(tA, tA, tB, ALU.add, ["tA", "tB"], ["tA"])
            cp("act", qrot[:, pr, :], tA, ["tA"], [("qrot", pr)])
            tt(qdd[:, pr, :].rearrange("p (c t) -> p c t", c=4), tA.rearrange("p (c t) -> p c t", c=4),
               C["qdec"][:, pr, :].unsqueeze(1).to_broadcast([128, 4, 128]), ALU.mult,
               ["tA", ("C", "qdec")], [("qdd", pr)])
            tt(tC, ps[2][:, :], C["cosT"], ALU.mult, [PS(2), ("C", "cosT")], ["tC"])
            tt(tD, ps[4][:, :], C["sinT"], ALU.mult, [PS(4), ("C", "sinT")], ["tD"])
            tt(krot[:, pr, :], tC, tD, ALU.add, ["tC", "tD"], [("krot", pr)])
            for hh in range(2):
                r = slice(hh * 64, hh * 64 + 64)
                tt(krotH[hh][r, pr, :], tC[r, :], tD[r, :], ALU.add, ["tC", "tD"], [("krotH", hh, pr)])
            act(gs[:, pr, :], ps[5][:, :], AF.Silu, [PS(5)], [("gs", pr)])
        for j in range(4):
            proj_tm(6, 512, 256, j)
            cp("act", v_tm[:, j, :], ps[6][:, 0:256], [PS(6)], [("v_tm", j)])
        SBk, HBk, OBk, TBk = [0, 4], [1, 5], [2, 6], [7, 3]

        def scores(j_):
            cs_ = slice(j_ * 128, (j_ + 1) * 128)
            for pr in range(2):
                for hh in range(2):
                    mm(ps[SBk[pr]][:, hh * 128:(hh + 1) * 128], krotH[hh][:, pr, cs_], qrot[:, pr, cs_], True, True,
                       [("krotH", hh, pr), ("qrot", pr)], [PS(SBk[pr])])
                tr(psb[TBk[pr]][:, 0:128], krot[:, pr, cs_], C["ident_bf"][:], [("krot", pr), ("C", "ident_bf")], [PS(TBk[pr])])

        scores(0)
        for j in range(4):
            cs = slice(j * 128, (j + 1) * 128)
            for pr in range(2):
                tt(PT2[pr], ps[SBk[pr]][:, 0:256].rearrange("p (h t) -> p h t", h=2), C["dmT"][:, 2 * pr:2 * pr + 2, :],
                   ALU.mult, [PS(SBk[pr]), ("C", "dmT")], [("PTt", pr)])
                tt(kend2[pr], psb[TBk[pr]][:, 0:128], C["kend"][:, pr, :], ALU.mult, [PS(TBk[pr]), ("C", "kend")],
                   [("kendb", pr)])
            if j + 1 < 4:
                scores(j + 1)
            for pr in range(2):
                for hh in range(2):
                    r = slice(hh * 64, hh * 64 + 64)
                    h = 2 * pr + hh
                    mm(ps[OBk[pr]][r, cs], v_tm[:, j, h * 64:(h + 1) * 64], PT2[pr][:, hh, :], True, False,
                       [("v_tm", j), ("PTt", pr)], [PS(OBk[pr])])
                    mm(ps[OBk[pr]][r, cs], hstbH["ret"][hh][:, pr, :], qdd[:, pr, cs], False, True,
                       [("hstbH", "ret", pr), ("qdd", pr)], [PS(OBk[pr])])
                state_update("ret", pr, kend2[pr], ("kendb", pr), v_tm[:, j, pr * 128:(pr + 1) * 128], ("v_tm", j),
                             [GAM[2 * pr] ** 128, GAM[2 * pr + 1] ** 128], bank=HBk[pr])
        head_norm2(OBk, SP_RETNW, [gs[:, 0, :], gs[:, 1, :]], [("gs", 0), ("gs", 1)],
                   [mixT[:, 0, :], mixT[:, 1, :]], [("mixT", 0), ("mixT", 1)],
                   [(sqb, tA, tB, ("sqb", "tA", "tB")), (qdd[:, 0, :], tC, tD, (("qdd", 0), "tC", "tD"))], [3, 7])

    def phase_hgrn(l, b):
        S.barrier()
        AR.reset()
        qs, kTf, E2, E1t = AR.f32(BL), AR.f32(BL), AR.f32(BL), AR.f32(BL)
        t256a, t256b = qs[:, 0:256], qs[:, 256:512]
        rr, onb = E2, kTf
        sv = AR.off
        AR.off = sv - BL
        sqb = AR.bf16(BL)
        AR.off = sv
        decs = AR.f32(2, 8)
        qT, gs = AR.bf16(2, BL), AR.bf16(2, BL)
        kTH = [AR.bf16(2, BL), AR.bf16(2, BL)]
        logf = AR.f32(4, 256)
        k_tm, v_tm = AR.bf16(4, 256), AR.bf16(4, 256)
        kendH = [AR.bf16(4, 256), AR.bf16(4, 256)]
        PT2 = [AR.bf16(2, 128), AR.bf16(2, 128)]
        lbF, omF, nomF = AR.f32(2), AR.f32(2), AR.f32(2)
        lbR, omR = AR.f32(256), AR.f32(256)
        if l == 0:
            memset(lbF, 0.0, "lbF")
            memset(lbR, 0.0, "lbR")
        else:
            tt(lbF, sp_t[:, SP_LBL + 2:SP_LBL + 4], sp_t[:, SP_LBL:SP_LBL + 2], ALU.subtract, ["sp_t"], ["lbF"])
            act(lbF, lbF, AF.Sigmoid, ["lbF"], ["lbF"])
            tt(lbR, rp_t[:, 256:512], rp_t[:, 0:256], ALU.subtract, ["rp_t"], ["lbR"])
            act(lbR, lbR, AF.Sigmoid, ["lbR"], ["lbR"])
        ts(omF, lbF, -1.0, 1.0, ALU.mult, ALU.add, ["lbF"], ["omF"])
        ts(nomF, lbF, 1.0, -1.0, ALU.mult, ALU.add, ["lbF"], ["nomF"])
        ts(omR, lbR, -1.0, 1.0, ALU.mult, ALU.add, ["lbR"], ["omR"])
        for hh in range(2):
            for pr in range(2):
                memset(kTH[hh][:, pr, :], 0.0, ("kTH", hh, pr))
            memset(kendH[hh].rearrange("p a b -> p (a b)"), 0.0, ("kendH", hh))
        for j in range(4):
            proj_tm(4 + (j % 2), 1280, 256, j)
            act(logf[:, j, :], ps[4 + (j % 2)][:, 0:256], AF.Sigmoid, [PS(4 + (j % 2))], [("logf", j)])
        for j in range(4):
            proj_tm(j % 2, 1536, 256, j)
            cp("act", v_tm[:, j, :], ps[j % 2][:, 0:256], [PS(j % 2)], [("v_tm", j)])
        for j in range(4):
            tt(logf[:, j, :], logf[:, j, :], omR, ALU.mult, [("logf", j), "omR"], [("logf", j)])
            tt(logf[:, j, :], logf[:, j, :], lbR, ALU.add, [("logf", j), "lbR"], [("logf", j)])
            ts(k_tm[:, j, :], logf[:, j, :], -1.0, 1.0, ALU.mult, ALU.add, [("logf", j)], [("k_tm", j)])
        for j in range(4):
            act(logf[:, j, :], logf[:, j, :], AF.Ln, [("logf", j)], [("logf", j)])
        for j in range(4):
            tb_ = t256a if j % 2 == 0 else t256b
            mm(ps[7][:, (j % 2) * 256:(j % 2) * 256 + 256], C["mstrBD"][:], logf[:, j, :], True, True,
               [("C", "mstrBD"), ("logf", j)], [PS(7)])
            act(tb_, ps[7][:, (j % 2) * 256:(j % 2) * 256 + 256], AF.Exp, [PS(7)], [("qs", j % 2)])
            for u in range(2):
                r = slice(u * 64, u * 64 + 64)
                tt(kendH[u][r, j, :], k_tm[r, j, :], tb_[r, :], ALU.mult, [("k_tm", j), ("qs", j % 2)], [("kendH", u)])
        for pr in range(2):
            proj_fm(0, Win, 1024 + pr * 128)
            proj_fm(1, Win, 1280 + pr * 128)
            proj_fm(2, Win, 1792 + pr * 128)
            act(qs, ps[0][:, :], AF.Silu, [PS(0), ("qs", 0), ("qs", 1)], ["qs", ("qs", 0), ("qs", 1)])
            act(kTf, ps[1][:, :], AF.Sigmoid, [PS(1)], ["kTf"])
            ts(kTf, kTf, nomF[:, pr:pr + 1], omF[:, pr:pr + 1], ALU.mult, ALU.add, ["kTf", "nomF", "omF"], ["kTf"])
            act(gs[:, pr, :], ps[2][:, :], AF.Silu, [PS(2)], [("gs", pr)])
            for j in range(4):
                mm(ps[6][:, j * 128:(j + 1) * 128], logf[:, j, pr * 128:(pr + 1) * 128], C["maskBD"][:], True, True,
                   [("logf", j), ("C", "maskBD")], [PS(6)])
            act(E1t, ps[6][:, :], AF.Exp, [PS(6)], ["E1t"])
            act(E2, ps[6][:, :], AF.Exp, [PS(6)], ["E2"], scale=-1.0)
            tt(qT[:, pr, :], qs, E1t, ALU.mult, ["qs", "E1t"], [("qT", pr)])
            cp("dve", decs[:, pr, :], E1t.rearrange("p (c t) -> p c t", t=64)[:, :, 63], ["E1t"], [("decs", pr)])
            for hh in range(2):
                r = slice(hh * 64, hh * 64 + 64)
                tt(kTH[hh][r, pr, :], kTf[r, :], E2[r, :], ALU.mult, ["kTf", "E2"], [("kTH", hh, pr)])
        SBk, HBk, OBk = [0, 4], [1, 5], [2, 6]

        def scores(j_):
            cs_ = slice(j_ * 128, (j_ + 1) * 128)
            for pr in range(2):
                for hh in range(2):
                    mm(ps[SBk[pr]][:, hh * 128:(hh + 1) * 128], kTH[hh][:, pr, cs_], qT[:, pr, cs_], True, True,
                       [("kTH", hh, pr), ("qT", pr)], [PS(SBk[pr])])

        scores(0)
        for j in range(4):
            cs = slice(j * 128, (j + 1) * 128)
            for pr in range(2):
                tt(PT2[pr], ps[SBk[pr]][:, 0:256].rearrange("p (h t) -> p h t", h=2),
                   C["maskBD"][:, :].unsqueeze(1).to_broadcast([128, 2, 128]), ALU.mult,
                   [PS(SBk[pr]), ("C", "maskBD")], [("PTt", pr)])
            if j + 1 < 4:
                scores(j + 1)
            for u in range(2):
                cu = slice(j * 128 + u * 64, j * 128 + u * 64 + 64)
                for pr in range(2):
                    for hh in range(2):
                        r = slice(hh * 64, hh * 64 + 64)
                        h = 2 * pr + hh
                        mm(ps[OBk[pr]][r, cu], v_tm[:, j, h * 64:(h + 1) * 64], PT2[pr][:, hh, u * 64:u * 64 + 64], True, False,
                           [("v_tm", j), ("PTt", pr)], [PS(OBk[pr])])
                        mm(ps[OBk[pr]][r, cu], hstbH["hgrn"][hh][:, pr, :], qT[:, pr, cu], False, True,
                           [("hstbH", "hgrn", pr), ("qT", pr)], [PS(OBk[pr])])
                    ci = 2 * j + u
                    dec = [decs[hh * 64:hh * 64 + 64, pr, ci:ci + 1] for hh in range(2)]
                    state_update("hgrn", pr, kendH[u][:, j, pr * 128:(pr + 1) * 128], ("kendH", u),
                                 v_tm[:, j, pr * 128:(pr + 1) * 128], ("v_tm", j), dec, [("decs", pr)], bank=HBk[pr])
        head_norm2(OBk, SP_HGNW, [gs[:, 0, :], gs[:, 1, :]], [("gs", 0), ("gs", 1)],
                   [mixT[:, 2, :], mixT[:, 3, :]], [("mixT", 2), ("mixT", 3)],
                   [(sqb, rr, onb, ("E1t", "E2", "kTf")), (kTH[0][:, 0, :], qs, logf.rearrange("p a b -> p (a b)")[:, 0:BL],
                                                          (("kTH", 0, 0), "qs", ("logf", 0)))], [3, 7])

    def phase_m2(l, b):
        S.barrier()
        AR.reset()
        xraw = AR.f32(4, 3 + BL)
        alias0 = AR.off - 4 * (3 + BL)
        xc = AR.f32(4, BL)
        xcb = AR.bf16(4, BL)
        BTH = [AR.bf16(BL), AR.bf16(BL)]
        gz = AR.bf16(2, BL)
        dt_t, la_t = AR.f32(4, 4), AR.f32(4, 32)
        x4, arow = AR.f32(4), AR.f32(4)
        laB = AR.f32(4, 128)
        dm4 = laB
        cum_s, vsc, etc = AR.f32(4), AR.f32(4), AR.f32(4)
        ecb = AR.f32(4, 128)
        PTm, cdec = AR.bf16(4, 128), AR.bf16(4, 128)
        v_tm, v_end = AR.bf16(4, 64), AR.bf16(4, 64)
        B_tm = AR.bf16(128)
        save = AR.off
        AR.off = alias0
        my2 = AR.f32(2, BL)
        sqb = AR.bf16(2, BL)
        rr = AR.f32(BL)
        assert AR.off <= alias0 + 4 * (3 + BL)
        AR.off = save
        act(arow, rp_t[:, 516:520], AF.Exp, ["rp_t"], ["arow"])
        ts(arow, arow, -1.0, None, ALU.mult, None, ["arow"], ["arow"])
        for g in range(2):
            memset(BTH[g], 0.0, ("BTH", g))
        for pr in range(2):
            proj_fm(0, Win, 2560 + pr * 128)
            act(gz[:, pr, :], ps[0][:, :], AF.Silu, [PS(0)], [("gz", pr)])
        cp("dve", xraw[:, :, 0:3], xhist[:, :, :], ["xhist"], ["xraw_h"])
        for tl in range(4):
            bk = 1 if tl % 2 == 0 else 5
            proj_fm(bk, Win, 2816 + tl * 128)
            cp("act", xraw[:, tl, 3:3 + BL], ps[bk][:, :], [PS(bk)], [("xraw", tl)])
        for tl in range(4):
            ts(xc[:, tl, :], xraw[:, tl, 0:BL], sp_t[:, SP_CW + 4 * tl:SP_CW + 4 * tl + 1],
               sp_t[:, SP_CB + tl:SP_CB + tl + 1], ALU.mult, ALU.add, [("xraw", tl), "xraw_h", "sp_t"], [("xc", tl)])
            for i in range(1, 4):
                stt(xc[:, tl, :], xraw[:, tl, i:i + BL], sp_t[:, SP_CW + 4 * tl + i:SP_CW + 4 * tl + i + 1],
                    xc[:, tl, :], ALU.mult, ALU.add, [("xraw", tl), "xraw_h", ("xc", tl), "sp_t"], [("xc", tl)])
        for tl in range(4):
            if tl < 2:
                act(xc[:, tl, :], xc[:, tl, :], AF.Silu, [("xc", tl)], [("xc", tl)])
            else:
                act(xcb[:, tl, :], xc[:, tl, :], AF.Silu, [("xc", tl)], [("xcb", tl)])
        for tl in range(2):
            cp("act", xcb[:, tl, :], xc[:, tl, :], [("xc", tl)], [("xcb", tl)])
        cp("dve", xhist[:, :, :], xraw[:, :, BL:BL + 3], [("xraw", t_) for t_ in range(4)] + ["xraw_h"], ["xhist"])
        for g in range(2):
            r = slice(g * 64, g * 64 + 64)
            cp("dve", BTH[g][r, :], xcb[r, 2, :], [("xcb", 2)], [("BTH", g)])
        dkeys = [("dt", j_) for j_ in range(4)]
        for j in range(4):
            proj_tm(4, 3300, 32, j)
            tt(dt_t[:, j, :], ps[4][:, 28:32], rp_t[:, 512:516], ALU.add, [PS(4), "rp_t"], [("dt", j)])
        dflat = dt_t.rearrange("p a b -> p (a b)")
        act(dflat, dflat, AF.Exp, dkeys, dkeys)
        ts(dflat, dflat, 1.0, None, ALU.add, None, dkeys, dkeys)
        act(dflat, dflat, AF.Ln, dkeys, dkeys)
        for j in range(4):
            memset(la_t[:, j, :], 0.0, ("la", j))
            tt(la_t[:, j, 0:4], dt_t[:, j, :], arow, ALU.mult, [("dt", j), "arow"], [("la", j)])
        obs = [2, 6]
        STG = getattr(build, "STAGE", 9)
        if STG < 2:
            return
        for j in range(4):
            cs = slice(j * 128, (j + 1) * 128)
            mm(ps[4][:, 64:96], C["U128"][:], la_t[:, j, :], True, True, [("C", "U128"), ("la", j)], [PS(4)])
            mm(ps[4][:, 128:160], C["M128"][:], la_t[:, j, :], True, True, [("C", "M128"), ("la", j)], [PS(4)])
            ts(cum_s, ps[4][:, 64:68], -1.0, None, ALU.mult, None, [PS(4)], ["cum_s"])
            act(etc, ps[4][:, 128:132], AF.Exp, [PS(4)], ["etc"])
            tt(vsc, etc, dt_t[:, j, :], ALU.mult, ["etc", ("dt", j)], ["vsc"])
            for h in range(4):
                ts(laB[:, h, :], C["U128"][:], la_t[:, j, h:h + 1], None, ALU.mult, None,
                   [("la", j), ("C", "U128")], [("laB", h), "dm4"])
                mm(ps[0][:, h * 128:(h + 1) * 128], C["ones_f"][:], laB[:, h, :], True, True,
                   [("laB", h), ("C", "ones_f")], [PS(0)])
            act(ecb.rearrange("p a b -> p (a b)"), ps[0][:, :], AF.Exp, [PS(0)], ["ecb"])
            if STG < 2.5:
                continue
            for g in range(2):
                mm(ps[1][:, g * 128:(g + 1) * 128], BTH[g][:, cs], xcb[:, 3, cs], True, True,
                   [("BTH", g), ("xcb", 3)], [PS(1)])
            tt(cdec, xcb[:, 3, cs].unsqueeze(1).to_broadcast([128, 4, 128]), ecb, ALU.mult, [("xcb", 3), "ecb"], ["cdec"])
            tt(dm4, ps[0][:, :].rearrange("p (h t) -> p h t", h=4), cum_s.unsqueeze(2).to_broadcast([128, 4, 128]),
               ALU.add, [PS(0), "cum_s"], ["dm4"] + [("laB", h_) for h_ in range(4)])
            tt(dm4, dm4, C["negm"][:, :].unsqueeze(1).to_broadcast([128, 4, 128]), ALU.add, ["dm4", ("C", "negm")], ["dm4"])
            act(dm4.rearrange("p h t -> p (h t)"), dm4.rearrange("p h t -> p (h t)"), AF.Exp, ["dm4"], ["dm4"])
            for g in range(2):
                tt(PTm[:, 2 * g:2 * g + 2, :], dm4[:, 2 * g:2 * g + 2, :],
                   ps[1][:, g * 128:(g + 1) * 128].unsqueeze(1).to_broadcast([128, 2, 128]), ALU.mult,
                   [PS(1), "dm4"], ["PTm"])
            if STG < 3:
                continue
            for tl in range(2):
                tr(psb[7][:, tl * 128:(tl + 1) * 128], xcb[:, tl, cs], C["ident_bf"][:],
                   [("xcb", tl), ("C", "ident_bf")], [PS(7)])
            tr(psb[7][:, 256:384], xcb[:, 2, cs], C["ident_bf"][:], [("xcb", 2), ("C", "ident_bf")], [PS(7)])
            xview = psb[7][:, 0:256].rearrange("p (h v) -> p h v", h=4)
            tt(v_tm, xview, dt_t[:, j, :].unsqueeze(2).to_broadcast([128, 4, 64]), ALU.mult,
               [PS(7), ("dt", j)], ["v_tm"])
            tt(v_end, xview, vsc.unsqueeze(2).to_broadcast([128, 4, 64]), ALU.mult, [PS(7), "vsc"], ["v_end"])
            cp("act", B_tm, psb[7][:, 256:384], [PS(7)], ["B_tm"])
            for h in range(4):
                g, hh = h // 2, h % 2
                r = slice(hh * 64, hh * 64 + 64)
                mm(ps[obs[g]][r, cs], v_tm[:, h, :], PTm[:, h, :], True, False, ["v_tm", "PTm"], [PS(obs[g])])
                mm(ps[obs[g]][r, cs], hstD_bf[h][:, :], cdec[:, h, :], False, True,
                   [("hstDbf", h), "cdec"], [PS(obs[g])])
            mm(ps[5][:, 0:256], B_tm, v_end.rearrange("p h v -> p (h v)"), True, True, ["B_tm", "v_end"], [PS(5)])
            for h in range(4):
                g, hh = h // 2, h % 2
                r = slice(g * 64, g * 64 + 64)
                stt(hst["m2"][r, hh, :], hst["m2"][r, hh, :], ecb[r, h, 127:128], ps[5][r, h * 64:(h + 1) * 64],
                    ALU.mult, ALU.add, [PS(5), ("hst", "m2", h), "ecb"], [("hst", "m2", h)])
                cp("act", hstD_bf[h][r, :], hst["m2"][r, hh, :], [("hst", "m2", h)], [("hstDbf", h)])
        if STG < 4:
            return
        for pr in range(2):
            stt(my2[:, pr, :], xc[:, pr, :], sp_t[:, SP_M2D + pr:SP_M2D + pr + 1], ps[obs[pr]][:, :],
                ALU.mult, ALU.add, [("xc", pr), "sp_t", PS(obs[pr]), "xhist"], [("my2", pr)])
            tt(my2[:, pr, :], my2[:, pr, :], gz[:, pr, :], ALU.mult, [("my2", pr), ("gz", pr)], [("my2", pr)])
            act(sqb[:, pr, :], my2[:, pr, :], AF.Square, [("my2", pr)], [("sqb", pr)])
        for pr in range(2):
            mm(ps[3][:, :], C["allones"][:], sqb[:, pr, :], pr == 0, pr == 1, [("sqb", pr), ("C", "allones")], [PS(3)])
        rsqrt_(rr, ps[3][:, :], 1.0 / 256, [PS(3)], ["rr"])
        for pr in range(2):
            stt(mixT[:, 6 + pr, :], my2[:, pr, :], sp_t[:, SP_M2NW + pr:SP_M2NW + pr + 1], rr, ALU.mult, ALU.mult,
                [("my2", pr), "rr", "sp_t"], [("mixT", 6 + pr)])


    TWO_PI = float(2 * np.pi)

    def sincos(out_s, out_c, ang, n, key_in, key_s, key_c, rows=slice(0, 128)):
        o0 = AR.off
        kf, r_ = AR.f32(n)[rows, :], AR.f32(n)[rows, :]
        ki = arena_i[rows, AR.off:AR.off + n]
        AR.off += n
        assert AR.off <= AW
        for shift, out, kout in ((0.0, out_s, key_s), (float(np.pi / 2), out_c, key_c)):
            ts(r_, ang, shift, None, ALU.add, None, [key_in], ["sc_r"])
            ts(ki, r_, 1.0 / TWO_PI, None, ALU.mult, None, ["sc_r"], ["sc_ki"])
            cp("dve", kf, ki, ["sc_ki"], ["sc_kf"])
            stt(r_, kf, -TWO_PI, r_, ALU.mult, ALU.add, ["sc_kf", "sc_r"], ["sc_r"])
            ts(r_, r_, float(np.pi), float(-np.pi), ALU.min, ALU.max, ["sc_r"], ["sc_r"])
            act(out, r_, AF.Sin, ["sc_r"], [kout])
        AR.off = o0

    def s5_params(l):
        S.barrier()
        AR.reset()
        Bt, Ct = AR.f32(2, 8, 16), AR.f32(2, 8, 16)
        S.dma("sp", Bt, s5b_d[l], ["bar"], ["s5Bt"])
        S.dma("sp", Ct, s5c_d[l], ["bar"], ["s5Ct"])
        dtv, lr, abre, abim, nr, den, t3, fre, fim = [AR.f32(8) for _ in range(9)]
        Are, Aim = sp_t[:, SP_ARE:SP_ARE + 8], sp_t[:, SP_AIM:SP_AIM + 8]
        act(dtv, sp_t[:, 60:68], AF.Exp, ["sp_t"], ["dtv"])
        tt(S5["th"][:, :], Aim, dtv, ALU.mult, ["sp_t", "dtv"], ["s5th"])
        tt(lr, Are, dtv, ALU.mult, ["sp_t", "dtv"], ["lr"])
        act(S5["rho"][:, :], lr, AF.Exp, ["lr"], ["s5rho"])
        sincos(S5["s1"][:, :], S5["c1"][:, :], S5["th"][:, :], 8, "s5th", "s5s1", "s5c1")
        tt(abre, S5["rho"][:, :], S5["c1"][:, :], ALU.mult, ["s5rho", "s5c1"], ["abre"])
        tt(abim, S5["rho"][:, :], S5["s1"][:, :], ALU.mult, ["s5rho", "s5s1"], ["abim"])
        ts(nr, abre, -1.0, None, ALU.add, None, ["abre"], ["nr"])
        tt(den, Are, Are, ALU.mult, ["sp_t"], ["den"])
        tt(t3, Aim, Aim, ALU.mult, ["sp_t"], ["t3"])
        tt(den, den, t3, ALU.add, ["den", "t3"], ["den"])
        S.add("dve", lambda: V.reciprocal(out=den, in_=den), ["den"], ["den"])
        tt(fre, nr, Are, ALU.mult, ["nr", "sp_t"], ["fre"])
        tt(t3, abim, Aim, ALU.mult, ["abim", "sp_t"], ["t3"])
        tt(fre, fre, t3, ALU.add, ["fre", "t3"], ["fre"])
        tt(fre, fre, den, ALU.mult, ["fre", "den"], ["fre"])
        tt(fim, abim, Are, ALU.mult, ["abim", "sp_t"], ["fim"])
        tt(t3, nr, Aim, ALU.mult, ["nr", "sp_t"], ["t3"])
        tt(fim, fim, t3, ALU.subtract, ["fim", "t3"], ["fim"])
        tt(fim, fim, den, ALU.mult, ["fim", "den"], ["fim"])
        bbre, bbim, u1 = AR.f32(8, 16), AR.f32(8, 16), AR.f32(8, 16)
        fb = lambda f_: f_.unsqueeze(2).to_broadcast([128, 8, 16])
        tt(bbre, Bt[:, 0, :, :], fb(fre), ALU.mult, ["s5Bt", "fre"], ["bbre"])
        tt(u1, Bt[:, 1, :, :], fb(fim), ALU.mult, ["s5Bt", "fim"], ["u1"])
        tt(bbre, bbre, u1, ALU.subtract, ["bbre", "u1"], ["bbre"])
        tt(bbim, Bt[:, 1, :, :], fb(fre), ALU.mult, ["s5Bt", "fre"], ["bbim"])
        tt(u1, Bt[:, 0, :, :], fb(fim), ALU.mult, ["s5Bt", "fim"], ["u1"])
        tt(bbim, bbim, u1, ALU.add, ["bbim", "u1"], ["bbim"])
        nCim = AR.f32(8, 16)
        ts(nCim, Ct[:, 1, :, :], -1.0, None, ALU.mult, None, ["s5Ct"], ["nCim"])
        for t_ in range(2):
            memset(BXc[t_].rearrange("p a b -> p (a b)"), 0.0, ("BXc", t_))
            memset(CXc[t_].rearrange("p a b -> p (a b)"), 0.0, ("CXc", t_))
        for e in range(2):
            r = slice(e * 64, e * 64 + 64)
            cc = slice(16 * e, 16 * e + 16)
            cp("dve", BXc[0][r, :, cc], bbre[r, :, :], ["bbre"], [("BXc", 0)])
            cp("dve", BXc[1][r, :, cc], bbim[r, :, :], ["bbim"], [("BXc", 1)])
            cp("dve", CXc[0][r, :, cc], Ct[r, 0, :, :], ["s5Ct"], [("CXc", 0)])
            cp("dve", CXc[1][r, :, cc], nCim[r, :, :], ["nCim"], [("CXc", 1)])
        memset(S5["hr"][:, :], 0.0, "s5hr")
        memset(S5["hi"][:, :], 0.0, "s5hi")
        fl = lambda a_: a_.rearrange("p a b -> p (a b)")
        for hf in range(2):
            S.barrier()
            AR.reset()
            cosR, sinR, rhoB, angR = AR.f32(4, 128), AR.f32(4, 128), AR.f32(4, 128), AR.f32(4, 128)
            LBh = [AR.bf16(4, 128), AR.bf16(4, 128)]
            CXh = [AR.bf16(4, 128), AR.bf16(4, 128)]
            BXP = [AR.bf16(4, 128), AR.bf16(4, 128)]
            for jj in range(4):
                j = 4 * hf + jj
                ts(angR[:, jj, :], C["iotaR"][:], S5["th"][:, j:j + 1], None, ALU.mult, None,
                   [("C", "iotaR"), "s5th"], ["angR"])
                ts(rhoB[:, jj, :], C["ones_f"][:], S5["rho"][:, j:j + 1], None, ALU.mult, None,
                   [("C", "ones_f"), "s5rho"], ["rhoB"])
            sincos(fl(sinR), fl(cosR), fl(angR), 512, "angR", "sinR", "cosR")
            for t_ in range(2):
                memset(fl(BXP[t_]), 0.0, ("BXP", t_))
                memset(fl(CXh[t_]), 0.0, ("CXh", t_))
                for jj in range(4):
                    cc = slice(32 * jj, 32 * jj + 32)
                    cp("dve", BXP[t_][:, jj, cc], BXc[t_][:, 4 * hf + jj, :], [("BXc", t_)], [("BXP", t_)])
                    cp("dve", CXh[t_][:, jj, cc], CXc[t_][:, 4 * hf + jj, :], [("CXc", t_)], [("CXh", t_)])
                for jj in range(4):
                    tr(psb[7][:, jj * 128:(jj + 1) * 128], BXP[t_][:, jj, :], C["ident_bf"][:],
                       [("BXP", t_), ("C", "ident_bf")], [PS(7)])
                cp("dve", LBh[t_], psb[7][:, 0:512].rearrange("p (a b) -> p a b", a=4), [PS(7)], [("LBh", t_)])
            for i_, (src, kk) in enumerate(((cosR, "cosR"), (sinR, "sinR"), (rhoB, "rhoB"))):
                S.dma("sp", tabF[hf, i_], fl(src), [kk], ["s5tab"])
            for i_, (src, kk) in enumerate(((LBh[0], ("LBh", 0)), (LBh[1], ("LBh", 1)), (CXh[0], ("CXh", 0)), (CXh[1], ("CXh", 1)))):
                S.dma("sp", tabB[hf, i_], fl(src), [kk], ["s5tab"])

    def phase_s5(l, b):
        S.barrier()
        AR.reset()
        uT, uTb, gsg = AR.f32(2, BL), AR.bf16(2, BL), AR.bf16(2, BL)
        GW = AR.bf16(2, 256)
        cosR, sinR, rhoB = AR.f32(4, 128), AR.f32(4, 128), AR.f32(4, 128)
        LBh = [AR.bf16(4, 128), AR.bf16(4, 128)]
        CXh = [AR.bf16(4, 128), AR.bf16(4, 128)]
        mark = AR.off
        t1, t2 = AR.f32(4, 128), AR.f32(4, 128)
        hb0 = AR.off
        hreb, himb = AR.bf16(4, 128), AR.bf16(4, 128)
        gyb = arena_b[:, 2 * hb0:2 * hb0 + 2 * BL].rearrange("p (a b) -> p a b", a=2)
        btr, bti = AR.f32(4, 128), AR.f32(4, 128)
        g_re, g_im = AR.f32(4, 128), AR.f32(4, 128)
        gir, gii, gt = AR.f32(4), AR.f32(4), AR.f32(4)
        fl = lambda a_: a_.rearrange("p a b -> p (a b)")
        gv = glu_d[l].rearrange("(ct p) o -> p ct o", p=128)
        S.dma("pool", GW[:, :, :], gv[:, :, :], ["bar"], ["GW"])
        for ut in range(2):
            proj_fm(3, Win, 2048 + ut * 128)
            cp("act", uT[:, ut, :], ps[3][:, :], [PS(3)], [("uT", ut)])
            cp("act", uTb[:, ut, :], uT[:, ut, :], [("uT", ut)], [("uTb", ut)])
            proj_fm(7, Win, 2304 + ut * 128)
            act(gsg[:, ut, :], ps[7][:, :], AF.Silu, [PS(7)], [("gsg", ut)])
        yb = [5, 6]
        for hf in range(2):
            hs = slice(4 * hf, 4 * hf + 4)
            for i_, (dst, kk) in enumerate(((cosR, "cosR"), (sinR, "sinR"), (rhoB, "rhoB"))):
                S.dma("sp", fl(dst), tabF[hf, i_], ["bar", "s5tab"], [kk])
            for i_, (dst, kk) in enumerate(((LBh[0], ("LBh", 0)), (LBh[1], ("LBh", 1)), (CXh[0], ("CXh", 0)), (CXh[1], ("CXh", 1)))):
                S.dma("sp", fl(dst), tabB[hf, i_], ["bar", "s5tab"], [kk])
            def bu_mm(cj_):
                cs_ = slice(cj_ * 128, (cj_ + 1) * 128)
                br_, bi_ = (0, 1) if cj_ % 2 == 0 else (2, 4)
                for jj in range(4):
                    mm(ps[br_][:, jj * 128:(jj + 1) * 128], LBh[0][:, jj, :], uTb[:, hf, cs_], True, True,
                       [("LBh", 0), ("uTb", hf)], [PS(br_)])
                    mm(ps[bi_][:, jj * 128:(jj + 1) * 128], LBh[1][:, jj, :], uTb[:, hf, cs_], True, True,
                       [("LBh", 1), ("uTb", hf)], [PS(bi_)])

            bu_mm(0)
            for cj in range(4):
                cs = slice(cj * 128, (cj + 1) * 128)
                br, bi = (0, 1) if cj % 2 == 0 else (2, 4)
                if cj + 1 < 4:
                    bu_mm(cj + 1)
                tt(fl(t1), ps[br][:, :], fl(cosR), ALU.mult, [PS(br), "cosR"], ["t1"])
                tt(fl(t2), ps[bi][:, :], fl(sinR), ALU.mult, [PS(bi), "sinR"], ["t2"])
                tt(fl(btr), fl(t1), fl(t2), ALU.add, ["t1", "t2"], ["btr"])
                tt(fl(t1), ps[bi][:, :], fl(cosR), ALU.mult, [PS(bi), "cosR"], ["t1"])
                tt(fl(t2), ps[br][:, :], fl(sinR), ALU.mult, [PS(br), "sinR"], ["t2"])
                tt(fl(bti), fl(t1), fl(t2), ALU.subtract, ["t1", "t2"], ["bti"])
                tt(gir, S5["hr"][:, hs], S5["c1"][:, hs], ALU.mult, ["s5hr", "s5c1"], ["gir"])
                tt(gt, S5["hi"][:, hs], S5["s1"][:, hs], ALU.mult, ["s5hi", "s5s1"], ["gt"])
                tt(gir, gir, gt, ALU.subtract, ["gir", "gt"], ["gir"])
                tt(gii, S5["hr"][:, hs], S5["s1"][:, hs], ALU.mult, ["s5hr", "s5s1"], ["gii"])
                tt(gt, S5["hi"][:, hs], S5["c1"][:, hs], ALU.mult, ["s5hi", "s5c1"], ["gt"])
                tt(gii, gii, gt, ALU.add, ["gii", "gt"], ["gii"])
                for jj in range(4):
                    S.add("dve", lambda jj=jj: V.tensor_tensor_scan(out=g_re[:, jj, :], data0=rhoB[:, jj, :],
                          data1=btr[:, jj, :], initial=gir[:, jj:jj + 1], op0=ALU.mult, op1=ALU.add),
                          ["rhoB", "btr", "gir"], ["g_re"])
                    S.add("dve", lambda jj=jj: V.tensor_tensor_scan(out=g_im[:, jj, :], data0=rhoB[:, jj, :],
                          data1=bti[:, jj, :], initial=gii[:, jj:jj + 1], op0=ALU.mult, op1=ALU.add),
                          ["rhoB", "bti", "gii"], ["g_im"])
                tt(fl(t1), fl(g_re), fl(cosR), ALU.mult, ["g_re", "cosR"], ["t1"])
                tt(fl(t2), fl(g_im), fl(sinR), ALU.mult, ["g_im", "sinR"], ["t2"])
                tt(fl(btr), fl(t1), fl(t2), ALU.subtract, ["t1", "t2"], ["btr"])
                tt(fl(t1), fl(g_re), fl(sinR), ALU.mult, ["g_re", "sinR"], ["t1"])
                tt(fl(t2), fl(g_im), fl(cosR), ALU.mult, ["g_im", "cosR"], ["t2"])
                tt(fl(bti), fl(t1), fl(t2), ALU.add, ["t1", "t2"], ["bti"])
                cp("dve", S5["hr"][:, hs], btr[:, :, 127], ["btr"], ["s5hr"])
                cp("dve", S5["hi"][:, hs], bti[:, :, 127], ["bti"], ["s5hi"])
                cp("act", fl(hreb), fl(btr), ["btr"], ["hreb"])
                cp("act", fl(himb), fl(bti), ["bti"], ["himb"])
                for jj in range(4):
                    mm(ps[yb[hf]][:, cs], CXh[0][:, jj, :], hreb[:, jj, :], jj == 0, False, [("CXh", 0), "hreb"], [PS(yb[hf])])
                    mm(ps[yb[hf]][:, cs], CXh[1][:, jj, :], himb[:, jj, :], False, jj == 3, [("CXh", 1), "himb"], [PS(yb[hf])])
        S.barrier()
        gyF = [fl(g_re), fl(g_im)]
        C1 = float(np.sqrt(2.0 / np.pi))
        E = [dict(sy=fl(t1), ksy="t1", x2=fl(t2), kx2="t2", sg=fl(btr), ksg="btr"),
             dict(sy=fl(bti), ksy="bti", x2=fl(cosR), kx2="cosR", sg=fl(sinR), ksg="sinR")]
        for ut in range(2):
            e_ = E[ut]
            stt(e_["sy"], uT[:, ut, :], sp_t[:, SP_S5D + ut:SP_S5D + ut + 1], ps[yb[ut]][:, :], ALU.mult, ALU.add,
                [("uT", ut), "sp_t", PS(yb[ut])], [e_["ksy"]])
        for ut in range(2):
            e_ = E[ut]
            tt(e_["x2"], e_["sy"], e_["sy"], ALU.mult, [e_["ksy"]], [e_["kx2"]])
        for ut in range(2):
            e_ = E[ut]
            ts(e_["x2"], e_["x2"], 2.0 * C1 * 0.044715, 2.0 * C1, ALU.mult, ALU.add, [e_["kx2"]], [e_["kx2"]])
        for ut in range(2):
            e_ = E[ut]
            tt(e_["x2"], e_["x2"], e_["sy"], ALU.mult, [e_["kx2"], e_["ksy"]], [e_["kx2"]])
        for ut in range(2):
            e_ = E[ut]
            act(e_["sg"], e_["x2"], AF.Sigmoid, [e_["kx2"]], [e_["ksg"]])
        for ut in range(2):
            e_ = E[ut]
            tt(gyF[ut], e_["sy"], e_["sg"], ALU.mult, [e_["ksy"], e_["ksg"]], [("gyF", ut)])
        for ut in range(2):
            cp("act", gyb[:, ut, :], gyF[ut], [("gyF", ut)], [("gyb", ut)])
        gb = [3, 7]
        for ot in range(2):
            for ct in range(2):
                mm(ps[gb[ot]][:, :], GW[:, ct, ot * 128:(ot + 1) * 128], gyb[:, ct, :], ct == 0, ct == 1,
                   ["GW", ("gyb", ct)], [PS(gb[ot])])
        for ot in range(2):
            e_ = E[ot]
            act(e_["sg"], ps[gb[ot]][:, :], AF.Sigmoid, [PS(gb[ot]), "sp_t"], [e_["ksg"]],
                bias=sp_t[:, SP_GLUB + ot:SP_GLUB + ot + 1])
        for ot in range(2):
            e_ = E[ot]
            tt(e_["sg"], e_["sg"], gyF[ot], ALU.mult, [e_["ksg"], ("gyF", ot)], [e_["ksg"]])
        for ot in range(2):
            e_ = E[ot]
            tt(mixT[:, 4 + ot, :], e_["sg"], gsg[:, ot, :], ALU.mult, [e_["ksg"], ("gsg", ot)], [("mixT", 4 + ot)])


    R16, R32, R64 = slice(0, 16), slice(0, 32), slice(0, 64)

    def dma_sb(out, in_, reads, writes, q="sp"):
        S.dma(q, out, in_, list(reads) + ["bar"], writes)

    def sample_layer(l):
        S.barrier()
        AR.reset()
        xs = AR.f32(D)
        junk, hnb = AR.bf16(D), AR.bf16(D)
        hnTs = AR.bf16(8, 16)
        projS = AR.f32(PT)
        ssx, rsx = AR.f32(1), AR.f32(1)
        dma_sb(xs[R16, :], scr["X"], ["scrX"], ["xs"])
        act(junk[R16, :], xs[R16, :], AF.Square, ["xs"], ["junk_s", "ssx"], accum_out=ssx[R16, :])
        rsqrt_(rsx[R16, :], ssx[R16, :], 1.0 / D, ["ssx"], ["rsx"])
        ts(hnb[R16, :], xs[R16, :], rsx[R16, 0:1], None, ALU.mult, None, ["xs", "rsx"], ["hnb_s"])
        for kt in range(8):
            tr(psb[7][:, kt * 16:(kt + 1) * 16], hnb[R16, kt * 128:(kt + 1) * 128], C["ident_bf"][R16, R16],
               ["hnb_s", ("C", "ident_bf")], [PS(7)])
        tt(hnTs, psb[7][:, 0:128].rearrange("p (k t) -> p k t", k=8),
           sp_t[:, SP_NORMW:SP_NORMW + 8].unsqueeze(2).to_broadcast([128, 8, 16]), ALU.mult, [PS(7), "sp_t"], ["hnTs"])
        c0 = 0
        ib = 0
        while c0 < PT:
            n_ = min(512, PT - c0)
            bank = ib % 2
            for kt in range(8):
                mm(ps[bank][R16, 0:n_], hnTs[:, kt, :], Win[:, kt, c0:c0 + n_], kt == 0, kt == 7,
                   ["hnTs"] + [("Win", g_) for g_ in range(4)], [PS(bank)])
            cp("act" if ib % 2 else "dve", projS[R16, c0:c0 + n_], ps[bank][R16, 0:n_], [PS(bank)], ["projS"])
            c0 += n_
            ib += 1
        S.dma("sp", scr["P"].rearrange("c n k -> n c k"), projS[R16, 0:3328].rearrange("n (c k) -> n c k", c=13),
              ["projS"], ["scrP"])
        S.dma("sp", scr["XBC"], projS[R16, 2816:3328], ["projS"], ["scrP2"])
        S.dma("sp", scr["DT"], projS[R16, 3328:3332], ["projS"], ["scrP3"])

        def linrec(m, col0, st_key, out_st, mix_col, nw_idx):
            S.barrier()
            AR.reset()
            S0 = AR.f32(64, 64)
            tmp = AR.f32(32, 64)
            q_, k_, v_, g_ = AR.f32(64), AR.f32(64), AR.f32(64), AR.f32(64)
            ta, tb, o_, oh = AR.f32(64), AR.f32(64), AR.f32(64), AR.f32(64)
            nw, l0r, l1r, fr_ = AR.f32(64), AR.f32(64), AR.f32(64), AR.f32(64)
            cs_, sn_, gm_ = AR.f32(32), AR.f32(32), AR.f32(1)
            ss_, rr_ = AR.f32(1), AR.f32(1)
            view = lambda c_: scr["P"][c_ // 256].rearrange("n (h k) -> (n h) k", h=4)
            dma_sb(q_[R64, :], view(col0), ["scrP"], ["q_"])
            dma_sb(k_[R64, :], view(col0 + 256), ["scrP"], ["k_"])
            dma_sb(v_[R64, :], view(col0 + 512), ["scrP"], ["v_"])
            dma_sb(g_[R64, :], view(col0 + 768), ["scrP"], ["g_"])
            dma_sb(S0[R64, :, :].rearrange("p a b -> p (a b)"), st_in[st_key][l].rearrange("n h k v -> (n h) (k v)"), [], ["S0"])
            dma_sb(nw[R64, :], rowsS[l, :, nw_idx, :], [], ["nw"])
            if m == "ret":
                dma_sb(cs_[R64, :], cdram["cos_s"], [], ["cs_"])
                dma_sb(sn_[R64, :], cdram["sin_s"], [], ["sn_"])
                dma_sb(gm_[R64, :], cdram["gam_s"], [], ["gm_"])
                for src, key in ((q_, "q_"), (k_, "k_")):
                    x1, x2 = src[R64, 0:32], src[R64, 32:64]
                    tt(ta[R64, 0:32], x1, cs_[R64, :], ALU.mult, [key, "cs_"], ["ta"])
                    tt(tb[R64, 0:32], x2, sn_[R64, :], ALU.mult, [key, "sn_"], ["tb"])
                    tt(ta[R64, 32:64], x1, sn_[R64, :], ALU.mult, [key, "sn_"], ["ta"])
                    tt(tb[R64, 32:64], x2, cs_[R64, :], ALU.mult, [key, "cs_"], ["tb"])
                    tt(src[R64, 0:32], ta[R64, 0:32], tb[R64, 0:32], ALU.subtract, ["ta", "tb"], [key])
                    tt(src[R64, 32:64], ta[R64, 32:64], tb[R64, 32:64], ALU.add, ["ta", "tb"], [key])
                ts(k_[R64, :], k_[R64, :], 0.125, None, ALU.mult, None, ["k_"], ["k_"])
            else:
                dma_sb(l0r[R64, :], rowsS[l, :, 2, :], [], ["l0r"])
                dma_sb(l1r[R64, :], rowsS[l, :, 3, :], [], ["l1r"])
                act(q_[R64, :], q_[R64, :], AF.Silu, ["q_"], ["q_"])
                act(fr_[R64, :], k_[R64, :], AF.Sigmoid, ["k_"], ["fr_"])
                if l == 0:
                    memset(l0r[R64, :], 0.0, "l0r")
                else:
                    tt(l0r[R64, :], l1r[R64, :], l0r[R64, :], ALU.subtract, ["l0r", "l1r"], ["l0r"])
                    act(l0r[R64, :], l0r[R64, :], AF.Sigmoid, ["l0r"], ["l0r"])
                ts(l1r[R64, :], l0r[R64, :], -1.0, 1.0, ALU.mult, ALU.add, ["l0r"], ["l1r"])
                tt(fr_[R64, :], fr_[R64, :], l1r[R64, :], ALU.mult, ["fr_", "l1r"], ["fr_"])
                tt(fr_[R64, :], fr_[R64, :], l0r[R64, :], ALU.add, ["fr_", "l0r"], ["fr_"])
                ts(k_[R64, :], fr_[R64, :], -1.0, 1.0, ALU.mult, ALU.add, ["fr_"], ["k_"])
            memset(o_[R64, :], 0.0, "o_")
            for kh in range(2):
                ks = slice(kh * 32, kh * 32 + 32)
                tt(tmp[R64, :, :], k_[R64, ks].unsqueeze(2).to_broadcast([64, 32, 64]),
                   v_[R64, :].unsqueeze(1).to_broadcast([64, 32, 64]), ALU.mult, ["k_", "v_"], ["tmp"])
                if m == "ret":
                    stt(S0[R64, ks, :], S0[R64, ks, :], gm_[R64, 0:1], tmp[R64, :, :], ALU.mult, ALU.add,
                        ["S0", "gm_", "tmp"], ["S0"])
                else:
                    tt(S0[R64, ks, :], S0[R64, ks, :], fr_[R64, ks].unsqueeze(2).to_broadcast([64, 32, 64]), ALU.mult,
                       ["S0", "fr_"], ["S0"])
                    tt(S0[R64, ks, :], S0[R64, ks, :], tmp[R64, :, :], ALU.add, ["S0", "tmp"], ["S0"])
                tt(tmp[R64, :, :], S0[R64, ks, :], q_[R64, ks].unsqueeze(2).to_broadcast([64, 32, 64]), ALU.mult,
                   ["S0", "q_"], ["tmp"])
                S.add("dve", lambda: V.tensor_reduce(out=oh[R64, :], in_=tmp[R64, :, :].rearrange("p k v -> p v k"),
                                                     axis=AX.X, op=ALU.add), ["tmp"], ["oh"])
                tt(o_[R64, :], o_[R64, :], oh[R64, :], ALU.add, ["o_", "oh"], ["o_"])
            S.dma("sp", out_st[l].rearrange("n h k v -> (n h) (k v)"), S0[R64, :, :].rearrange("p a b -> p (a b)"),
                  ["S0"], [P.key()])
            act(ta[R64, :], o_[R64, :], AF.Square, ["o_"], ["ta", "ss_"], accum_out=ss_[R64, :])
            rsqrt_(rr_[R64, :], ss_[R64, :], 1.0 / 64, ["ss_"], ["rr_"])
            stt(o_[R64, :], o_[R64, :], rr_[R64, 0:1], nw[R64, :], ALU.mult, ALU.mult, ["o_", "rr_", "nw"], ["o_"])
            act(g_[R64, :], g_[R64, :], AF.Silu, ["g_"], ["g_"])
            tt(o_[R64, :], o_[R64, :], g_[R64, :], ALU.mult, ["o_", "g_"], ["o_"])
            S.dma("sp", scr["M"][mix_col // 256].rearrange("n (h k) -> (n h) k", h=4), o_[R64, :],
                  ["o_"], [("scrM", mix_col)])

        linrec("ret", 0, "ret", s_outs["s_ret"], 0, 0)
        linrec("hgrn", 1024, "hgrn", s_outs["s_hgrn"], 256, 1)

        S.barrier()
        AR.reset()
        buf = AR.f32(3, 512)
        xn, acc, t5 = AR.f32(512), AR.f32(512), AR.f32(512)
        cw, cb = AR.f32(4, 512), AR.f32(512)
        dts, des = AR.f32(4), AR.f32(4)
        ar4 = AR.f32(4)
        dma_sb(buf[R16, :, :], st_in["conv"][l], [], ["buf"])
        dma_sb(xn[R16, :], scr["XBC"], ["scrP2"], ["xn"])
        dma_sb(dts[R16, :], scr["DT"], ["scrP3"], ["dts"])
        dma_sb(cw[R16, :, :].rearrange("p a b -> p (a b)"), rowsT[l][:, 0:2048].partition_broadcast(16), [], ["cw"])
        dma_sb(cb[R16, :], rowsT[l][:, 2048:2560].partition_broadcast(16), [], ["cb"])
        tt(acc[R16, :], xn[R16, :], cw[R16, 3, :], ALU.mult, ["xn", "cw"], ["acc"])
        tt(acc[R16, :], acc[R16, :], cb[R16, :], ALU.add, ["acc", "cb"], ["acc"])
        for i in range(3):
            tt(t5[R16, :], buf[R16, i, :], cw[R16, i, :], ALU.mult, ["buf", "cw"], ["t5"])
            tt(acc[R16, :], acc[R16, :], t5[R16, :], ALU.add, ["acc", "t5"], ["acc"])
        act(acc[R16, :], acc[R16, :], AF.Silu, ["acc"], ["acc"])
        S.dma("sp", s_outs["s_conv"][l][:, 0:2, :], buf[R16, 1:3, :], ["buf"], [P.key()])
        S.dma("sp", s_outs["s_conv"][l][:, 2, :], xn[R16, :], ["xn"], [P.key()])
        tt(dts[R16, :], dts[R16, :], rp_t[R16, 512:516], ALU.add, ["dts", "rp_t"], ["dts"])
        act(dts[R16, :], dts[R16, :], AF.Exp, ["dts"], ["dts"])
        ts(dts[R16, :], dts[R16, :], 1.0, None, ALU.add, None, ["dts"], ["dts"])
        act(dts[R16, :], dts[R16, :], AF.Ln, ["dts"], ["dts"])
        act(ar4[R16, :], rp_t[R16, 516:520], AF.Exp, ["rp_t"], ["ar4"])
        tt(des[R16, :], dts[R16, :], ar4[R16, :], ALU.mult, ["dts", "ar4"], ["des"])
        act(des[R16, :], des[R16, :], AF.Exp, ["des"], ["des"], scale=-1.0)
        S.dma("sp", scr["XM"], acc[R16, 0:256], ["acc"], [("scr2", 0)])
        S.dma("sp", scr["B"], acc[R16, 256:384], ["acc"], [("scr2", 3)])
        S.dma("sp", scr["C"], acc[R16, 384:512], ["acc"], [("scr2", 4)])
        S.dma("sp", scr["DT2"], dts[R16, :], ["dts"], [("scr2", 1)])
        S.dma("sp", scr["DE"], des[R16, :], ["des"], [("scr2", 2)])
        S.barrier()
        AR.reset()
        S0 = AR.f32(64, 64)
        tmp = AR.f32(32, 64)
        xm, zz = AR.f32(2, 64), AR.f32(2, 64)
        Bm, Cm = AR.f32(64), AR.f32(64)
        dtg, deg, dro = AR.f32(2), AR.f32(2), AR.f32(2)
        vdt, o_, oh, my = AR.f32(64), AR.f32(64), AR.f32(64), AR.f32(2, 64)
        dma_sb(xm[R32, :, :].rearrange("p a b -> p (a b)"), scr["XM"].rearrange("n (g x) -> (n g) x", g=2),
               [("scr2", 0)], ["xm"])
        dma_sb(Bm[R32, :], scr["B"].rearrange("n (g x) -> (n g) x", g=2), [("scr2", 3)], ["Bm"])
        dma_sb(Cm[R32, :], scr["C"].rearrange("n (g x) -> (n g) x", g=2), [("scr2", 4)], ["Cm"])
        dma_sb(dtg[R32, :], scr["DT2"].rearrange("n (g x) -> (n g) x", g=2), [("scr2", 1)], ["dtg"])
        dma_sb(deg[R32, :], scr["DE"].rearrange("n (g x) -> (n g) x", g=2), [("scr2", 2)], ["deg"])
        dma_sb(zz[R32, :, :].rearrange("p a b -> p (a b)"), scr["P"][10].rearrange("n (g x) -> (n g) x", g=2),
               ["scrP"], ["zz"])
        dma_sb(dro[R32, :], rowsM[l], [], ["dro"])
        act(zz[R32, :, :], zz[R32, :, :], AF.Silu, ["zz"], ["zz"])
        stv = st_in["m2"][l].rearrange("n (g e) k v -> (n g) e (k v)", e=2)
        sov = s_outs["s_m2"][l].rearrange("n (g e) k v -> (n g) e (k v)", e=2)
        for e in range(2):
            dma_sb(S0[R32, :, :].rearrange("p a b -> p (a b)"), stv[:, e, :], [], ["S0"])
            ts(vdt[R32, :], xm[R32, e, :], dtg[R32, e:e + 1], None, ALU.mult, None, ["xm", "dtg"], ["vdt"])
            memset(o_[R32, :], 0.0, "o_")
            for kh in range(2):
                ks = slice(kh * 32, kh * 32 + 32)
                tt(tmp[R32, :, :], Bm[R32, ks].unsqueeze(2).to_broadcast([32, 32, 64]),
                   vdt[R32, :].unsqueeze(1).to_broadcast([32, 32, 64]), ALU.mult, ["Bm", "vdt"], ["tmp"])
                stt(S0[R32, ks, :], S0[R32, ks, :], deg[R32, e:e + 1], tmp[R32, :, :], ALU.mult, ALU.add,
                    ["S0", "deg", "tmp"], ["S0"])
                tt(tmp[R32, :, :], S0[R32, ks, :], Cm[R32, ks].unsqueeze(2).to_broadcast([32, 32, 64]), ALU.mult,
                   ["S0", "Cm"], ["tmp"])
                S.add("dve", lambda: V.tensor_reduce(out=oh[R32, :], in_=tmp[R32, :, :].rearrange("p k v -> p v k"),
                                                     axis=AX.X, op=ALU.add), ["tmp"], ["oh"])
                tt(o_[R32, :], o_[R32, :], oh[R32, :], ALU.add, ["o_", "oh"], ["o_"])
            S.dma("sp", sov[:, e, :], S0[R32, :, :].rearrange("p a b -> p (a b)"), ["S0"], [P.key()])
            stt(my[R32, e, :], xm[R32, e, :], dro[R32, e:e + 1], o_[R32, :], ALU.mult, ALU.add, ["xm", "dro", "o_"], ["my"])
        tt(my[R32, :, :], my[R32, :, :], zz[R32, :, :], ALU.mult, ["my", "zz"], ["my"])
        S.dma("sp", scr["3"].rearrange("n (g x) -> (n g) x", g=2), my[R32, :, :].rearrange("p a b -> p (a b)"),
              ["my"], ["scr3"])
        mt, nwr = AR.f32(256), AR.f32(256)
        jk = AR.f32(256)
        ss_, rr_ = AR.f32(1), AR.f32(1)
        dma_sb(mt[R16, :], scr["3"], ["scr3"], ["mt"])
        dma_sb(nwr[R16, :], rowsT[l][:, 2560:2816].partition_broadcast(16), [], ["nwr"])
        act(jk[R16, :], mt[R16, :], AF.Square, ["mt"], ["jk", "ss_"], accum_out=ss_[R16, :])
        rsqrt_(rr_[R16, :], ss_[R16, :], 1.0 / 256, ["ss_"], ["rr_"])
        stt(mt[R16, :], mt[R16, :], rr_[R16, 0:1], nwr[R16, :], ALU.mult, ALU.mult, ["mt", "rr_", "nwr"], ["mt"])
        S.dma("sp", scr["M"][3], mt[R16, :], ["mt"], [("scrM", 768)])

        S.barrier()
        AR.reset()
        hnTs2 = AR.bf16(8, 16)
        xs2, junk2, hnb2 = AR.f32(D), AR.bf16(D), AR.bf16(D)
        ssx2, rsx2 = AR.f32(1), AR.f32(1)
        dma_sb(xs2[R16, :], scr["X"], ["scrX"], ["xs2"])
        act(junk2[R16, :], xs2[R16, :], AF.Square, ["xs2"], ["junk2", "ssx2"], accum_out=ssx2[R16, :])
        rsqrt_(rsx2[R16, :], ssx2[R16, :], 1.0 / D, ["ssx2"], ["rsx2"])
        ts(hnb2[R16, :], xs2[R16, :], rsx2[R16, 0:1], None, ALU.mult, None, ["xs2", "rsx2"], ["hnb2"])
        for kt in range(8):
            tr(psb[7][:, kt * 16:(kt + 1) * 16], hnb2[R16, kt * 128:(kt + 1) * 128], C["ident_bf"][R16, R16],
               ["hnb2", ("C", "ident_bf")], [PS(7)])
        tt(hnTs2, psb[7][:, 0:128].rearrange("p (k t) -> p k t", k=8),
           sp_t[:, SP_NORMW:SP_NORMW + 8].unsqueeze(2).to_broadcast([128, 8, 16]), ALU.mult, [PS(7), "sp_t"], ["hnTs2"])
        uTs, uTsb, sgs = AR.f32(2, 16), AR.bf16(2, 16), AR.f32(2, 16)
        for ut in range(2):
            for (c0_, dst, fn_) in ((2048, uTs, None), (2304, sgs, AF.Silu)):
                for kt in range(8):
                    mm(ps[3][:, 0:16], Win[:, kt, c0_ + ut * 128:c0_ + (ut + 1) * 128], hnTs2[:, kt, :], kt == 0, kt == 7,
                       ["hnTs2", ("Win", 2)], [PS(3)])
                if fn_ is None:
                    cp("act", dst[:, ut, :], ps[3][:, 0:16], [PS(3)], [("uTs", ut)])
                else:
                    act(dst[:, ut, :], ps[3][:, 0:16], fn_, [PS(3)], [("sgs", ut)])
            cp("dve", uTsb[:, ut, :], uTs[:, ut, :], [("uTs", ut)], [("uTsb", ut)])
        stt_ = AR.f32(2, D)
        h0 = [AR.f32(8, 16), AR.f32(8, 16)]
        hn_ = [AR.f32(8, 16), AR.f32(8, 16)]
        hnb_ = [AR.bf16(8, 16), AR.bf16(8, 16)]
        ta_, tb_ = AR.f32(8, 16), AR.f32(8, 16)
        for t_, nm in ((0, "s5re"), (1, "s5im")):
            dma_sb(stt_[R16, t_, :], st_in[nm][l].rearrange("n g p -> n (g p)"), [], [("stt", t_)])
            for j in range(8):
                tr(ps[4][:, t_ * 128 + j * 16:t_ * 128 + (j + 1) * 16], stt_[R16, t_, j * 128:(j + 1) * 128],
                   C["ident_f"][R16, R16], [("stt", t_), ("C", "ident_f")], [PS(4)])
            cp("dve", h0[t_], ps[4][:, t_ * 128:(t_ + 1) * 128].rearrange("p (a b) -> p a b", a=8), [PS(4)], [("h0", t_)])
        LBh = [AR.bf16(4, 128), AR.bf16(4, 128)]
        CXh = [AR.bf16(4, 128), AR.bf16(4, 128)]
        BXP = [AR.bf16(4, 128), AR.bf16(4, 128)]
        fl = lambda a_: a_.rearrange("p a b -> p (a b)")
        bb16 = lambda a_: a_.unsqueeze(2).to_broadcast([128, 8, 16])
        abre_, abim_ = AR.f32(8), AR.f32(8)
        tt(abre_, S5["rho"][:, :], S5["c1"][:, :], ALU.mult, ["s5rho", "s5c1"], ["abre_"])
        tt(abim_, S5["rho"][:, :], S5["s1"][:, :], ALU.mult, ["s5rho", "s5s1"], ["abim_"])
        tt(ta_, h0[0], bb16(abre_), ALU.mult, [("h0", 0), "abre_"], ["ta_"])
        tt(tb_, h0[1], bb16(abim_), ALU.mult, [("h0", 1), "abim_"], ["tb_"])
        tt(hn_[0], ta_, tb_, ALU.subtract, ["ta_", "tb_"], [("hn", 0)])
        tt(ta_, h0[1], bb16(abre_), ALU.mult, [("h0", 1), "abre_"], ["ta_"])
        tt(tb_, h0[0], bb16(abim_), ALU.mult, [("h0", 0), "abim_"], ["tb_"])
        tt(hn_[1], ta_, tb_, ALU.add, ["ta_", "tb_"], [("hn", 1)])
        for hf in range(2):
            for i_, (dst, kk) in enumerate(((LBh[0], ("LBh", 0)), (LBh[1], ("LBh", 1)), (CXh[0], ("CXh", 0)), (CXh[1], ("CXh", 1)))):
                S.dma("sp", fl(dst), tabB[hf, i_], ["bar", "s5tab"], [kk])
            for t_ in range(2):
                for jj in range(4):
                    mm(ps[t_][:, jj * 16:(jj + 1) * 16], LBh[t_][:, jj, :], uTsb[:, hf, :], True, True,
                       [("LBh", t_), ("uTsb", hf)], [PS(t_)])
                tt(hn_[t_][:, 4 * hf:4 * hf + 4, :], hn_[t_][:, 4 * hf:4 * hf + 4, :],
                   ps[t_][:, 0:64].rearrange("p (a b) -> p a b", a=4), ALU.add, [PS(t_), ("hn", t_)], [("hn", t_)])
                cp("act", hnb_[t_][:, 4 * hf:4 * hf + 4, :], hn_[t_][:, 4 * hf:4 * hf + 4, :], [("hn", t_)], [("hnb", t_)])
            for jj in range(4):
                mm(ps[5][:, hf * 16:(hf + 1) * 16], CXh[0][:, jj, :], hnb_[0][:, 4 * hf + jj, :], jj == 0, False,
                   [("CXh", 0), ("hnb", 0)], [PS(5)])
                mm(ps[5][:, hf * 16:(hf + 1) * 16], CXh[1][:, jj, :], hnb_[1][:, 4 * hf + jj, :], False, jj == 3,
                   [("CXh", 1), ("hnb", 1)], [PS(5)])
        S.barrier()
        for t_, nm in ((0, "s_s5re"), (1, "s_s5im")):
            for j in range(8):
                tr(ps[4][R16, j * 128 - (512 if j >= 4 else 0):(j + 1) * 128 - (512 if j >= 4 else 0)] if False else
                   ps[4 if j < 4 else 6][R16, (j % 4) * 128:(j % 4 + 1) * 128], hn_[t_][:, j, :], C["ident_f"][:],
                   [("hn", t_), ("C", "ident_f")], [PS(4), PS(6)])
            cp("dve", stt_[R16, t_, 0:512], ps[4][R16, :], [PS(4)], [("stt", t_)])
            cp("dve", stt_[R16, t_, 512:1024], ps[6][R16, :], [PS(6)], [("stt", t_)])
            S.dma("sp", s_outs[nm][l].rearrange("n g p -> n (g p)"), stt_[R16, t_, :], [("stt", t_)], [P.key()])
        sy, x2, sg = AR.f32(16), AR.f32(16), AR.f32(16)
        gyFs, gybs = AR.f32(2, 16), AR.bf16(2, 16)
        GWs = AR.bf16(2, 256)
        gvs = glu_d[l].rearrange("(ct p) o -> p ct o", p=128)
        dma_sb(GWs[:, :, :], gvs[:, :, :], [], ["GWs"], q="pool")
        C1 = float(np.sqrt(2.0 / np.pi))
        for ut in range(2):
            stt(sy, uTs[:, ut, :], sp_t[:, SP_S5D + ut:SP_S5D + ut + 1], ps[5][:, ut * 16:(ut + 1) * 16], ALU.mult, ALU.add,
                [("uTs", ut), "sp_t", PS(5)], ["sy"])
            tt(x2, sy, sy, ALU.mult, ["sy"], ["x2"])
            ts(x2, x2, 2.0 * C1 * 0.044715, 2.0 * C1, ALU.mult, ALU.add, ["x2"], ["x2"])
            tt(x2, x2, sy, ALU.mult, ["x2", "sy"], ["x2"])
            act(sg, x2, AF.Sigmoid, ["x2"], ["sg"])
            tt(gyFs[:, ut, :], sy, sg, ALU.mult, ["sy", "sg"], [("gyFs", ut)])
            cp("act", gybs[:, ut, :], gyFs[:, ut, :], [("gyFs", ut)], [("gybs", ut)])
        for ot in range(2):
            for ct in range(2):
                mm(ps[3][:, 0:16], GWs[:, ct, ot * 128:(ot + 1) * 128], gybs[:, ct, :], ct == 0, ct == 1,
                   ["GWs", ("gybs", ct)], [PS(3)])
            act(sg, ps[3][:, 0:16], AF.Sigmoid, [PS(3), "sp_t"], ["sg"], bias=sp_t[:, SP_GLUB + ot:SP_GLUB + ot + 1])
            tt(sg, sg, gyFs[:, ot, :], ALU.mult, ["sg", ("gyFs", ot)], ["sg"])
            tt(s5mixS[:, ot, :], sg, sgs[:, ot, :], ALU.mult, ["sg", ("sgs", ot)], [("s5mixS", ot)])

        S.barrier()
        AR.reset()
        xs = AR.f32(D)
        mx = AR.f32(D)
        mxb = AR.bf16(D)
        mTs = AR.bf16(8, 16)
        dma_sb(xs[R16, :], scr["X"], ["scrX"], ["xs"])
        memset(mx[R16, :], 0.0, "mx")
        for pc in (0, 1, 3):
            dma_sb(mx[R16, pc * 256:(pc + 1) * 256], scr["M"][pc], [("scrM", pc * 256)], ["mx"])
        cp("dve", mxb[R16, :], mx[R16, :], ["mx"], ["mxb"])
        for kt in range(8):
            tr(psb[7][:, kt * 16:(kt + 1) * 16], mxb[R16, kt * 128:(kt + 1) * 128], C["ident_bf"][R16, R16],
               ["mxb", ("C", "ident_bf")], [PS(7)])
        cp("dve", mTs, psb[7][:, 0:128].rearrange("p (k t) -> p k t", k=8), [PS(7)], ["mTs"])
        for ot in range(2):
            cp("dve", mTs[:, 4 + ot, :], s5mixS[:, ot, :], [("s5mixS", ot)], ["mTs"])
        for n_ in range(2):
            for kt in range(8):
                mm(ps[n_][R16, :], mTs[:, kt, :], Wout[:, kt, n_ * 512:(n_ + 1) * 512], kt == 0, kt == 7,
                   ["mTs", "Wout"], [PS(n_)])
            tt(xs[R16, n_ * 512:(n_ + 1) * 512], ps[n_][R16, :], xs[R16, n_ * 512:(n_ + 1) * 512], ALU.add,
               [PS(n_), "xs"], ["xs"])
        S.dma("sp", scr["X"], xs[R16, :], ["xs"], ["scrX"])

    def sample_final():
        S.barrier()
        AR.reset()
        xs, jk, fw = AR.f32(D), AR.f32(D), AR.f32(D)
        ssx, rsx = AR.f32(1), AR.f32(1)
        dma_sb(xs[R16, :], scr["X"], ["scrX"], ["xs"])
        dma_sb(fw[R16, :], fin_w.partition_broadcast(16), [], ["fw"])
        act(jk[R16, :], xs[R16, :], AF.Square, ["xs"], ["jk", "ssx"], accum_out=ssx[R16, :])
        rsqrt_(rsx[R16, :], ssx[R16, :], 1.0 / D, ["ssx"], ["rsx"])
        stt(xs[R16, :], xs[R16, :], rsx[R16, 0:1], fw[R16, :], ALU.mult, ALU.mult, ["xs", "rsx", "fw"], ["xs"])
        S.dma("sp", y_s, xs[R16, :], ["xs"], [P.key()])

    def out_proj(b):
        for j in range(4):
            i = 4 * b + j
            for n in range(2):
                for kt in range(8):
                    mm(ps[4 + n][:, :], mixT[:, kt, j * 128:(j + 1) * 128], Wout[:, kt, n * 512:(n + 1) * 512],
                       kt == 0, kt == 7, [("mixT", q) for q in range(8)] + ["Wout"], [PS(4 + n)])
                tt(x[:, i, n * 512:(n + 1) * 512], ps[4 + n][:, :], x[:, i, n * 512:(n + 1) * 512], ALU.add,
                   [PS(4 + n), ("x", i)], [("x", i)])

    P.ENABLE = getattr(build, "ENABLE", "ABCDS")
    load_weights(0)
    for l in range(DEPTH):
        S.barrier()
        nxt = l + 1 if l + 1 < DEPTH else None
        for m in ("ret", "hgrn"):
            for pr in range(2):
                memset(hst[m][:, pr, :], 0.0, ("hst", m, pr))
                for hh in range(2):
                    memset(hstbH[m][hh][:, pr, :], 0.0, ("hstbH", m, pr))
        for h in range(4):
            memset(hst["m2"][(h // 2) * 64:(h // 2) * 64 + 64, h % 2, :], 0.0, ("hst", "m2", h))
            memset(hstD_bf[h][:, :], 0.0, ("hstDbf", h))
        memset(xhist.rearrange("p a b -> p (a b)"), 0.0, "xhist")
        for q in range(8):
            memset(mixT[:, q, :], 0.0, ("mixT", q))
        if "C" in P.ENABLE:
            s5_params(l)
        if "S" in P.ENABLE:
            if l == 0:
                S.barrier()
                AR.reset()
                x0 = AR.f32(D)
                dma_sb(x0[R16, :], xs_in, [], ["x0"])
                S.dma("sp", scr["X"], x0[R16, :], ["x0"], ["scrX"])
            sample_layer(l)
        S.barrier()
        bufs = rmsnorm_alloc()
        for j in range(4):
            rmsnorm_tile(0, j, bufs)
        for b in range(NB):
            last = (b == NB - 1) and nxt is not None
            if "A" in P.ENABLE:
                phase_ret(l, b)
            if last:
                load_wgroup(nxt, 0)
            if "B" in P.ENABLE:
                phase_hgrn(l, b)
            if last:
                load_wgroup(nxt, 1)
            if "C" in P.ENABLE:
                phase_s5(l, b)
            if last:
                load_wgroup(nxt, 2)
            if "D" in P.ENABLE:
                phase_m2(l, b)
            if last:
                load_wgroup(nxt, 3)
            S.barrier()
            bufs = rmsnorm_alloc()
            for j in range(4):
                out_proj_tile(b, j)
                if b + 1 < NB:
                    rmsnorm_tile(b + 1, j, bufs)
        for pr in range(2):
            S.dma("sp", o_ret[l, 2 * pr:2 * pr + 2].rearrange("h k v -> (h k) v"), hst["ret"][:, pr, :],
                  [("hst", "ret", pr)], [P.key()])
            S.dma("sp", o_hgrn[l, 2 * pr:2 * pr + 2].rearrange("h k v -> (h k) v"), hst["hgrn"][:, pr, :],
                  [("hst", "hgrn", pr)], [P.key()])
        for g in range(2):
            S.dma("sp", o_m2[l, 2 * g:2 * g + 2].rearrange("hh n v -> n hh v"), hst["m2"][g * 64:(g + 1) * 64, :, :],
                  [("hst", "m2", 2 * g), ("hst", "m2", 2 * g + 1)], [P.key()])
        if "C" in P.ENABLE:
            S.barrier()
            tr(ps[1][0:8, 0:128], S5["hr"][:, :], C["ident_f"][:], ["s5hr", ("C", "ident_f")], [PS(1)])
            tr(ps[1][0:8, 128:256], S5["hi"][:, :], C["ident_f"][:], ["s5hi", ("C", "ident_f")], [PS(1)])
            AR.reset()
            s5T = AR.f32(256)
            cp("dve", s5T[0:8, :], ps[1][0:8, 0:256], [PS(1)], ["s5T"])
            S.dma("sp", o_s5re[l].rearrange("(j e) p -> j (e p)", e=2), s5T[0:8, 0:128], ["s5T"], [P.key()])
            S.dma("sp", o_s5im[l].rearrange("(j e) p -> j (e p)", e=2), s5T[0:8, 128:256], ["s5T"], [P.key()])
        S.barrier()
        tr(ps[0][0:12, 0:128], xhist.rearrange("p a b -> p (a b)"), C["ident_f"][:], ["xhist", ("C", "ident_f")], [PS(0)])
        AR.reset()
        AR.f32(256)
        xhT = AR.f32(128)
        cp("dve", xhT[0:12, :], ps[0][0:12, 0:128], [PS(0)], ["xhT"])
        for tl in range(4):
            S.dma("sp", o_conv[l][:, tl * 128:(tl + 1) * 128], xhT[3 * tl:3 * tl + 3, :], ["xhT"], [P.key()])
        if nxt is not None:
            load_wout(nxt)
            load_small(nxt)

    S.barrier()
    AR.reset()
    junk = AR.bf16(D)
    finw = AR.f32(D)
    S.dma("sp", finw, fin_w.partition_broadcast(128), ["bar"], ["finw"])
    for i in range(16):
        act(junk, x[:, i, :], AF.Square, [("x", i)], ["junk", ("ss", i)], accum_out=ss16[:, i:i + 1])
        rsqrt_(rstd[:, i:i + 1], ss16[:, i:i + 1], 1.0 / D, [("ss", i)], [("rstd", i)])
        stt(x[:, i, :], x[:, i, :], rstd[:, i:i + 1], finw, ALU.mult, ALU.mult,
            [("x", i), ("rstd", i), "finw"], [("x", i)])
        S.dma("sp", y_p[i * 128:(i + 1) * 128, :], x[:, i, :], [("x", i)], [P.key()])
    if "S" in P.ENABLE:
        sample_final()
    allout = [op["writes"][0] for op in S.ops if op["dma"] and op["writes"] and op["writes"][0][0] == "k"]
    S.add("sp", lambda: nc.sync.nop(), allout, ())
    n = S.emit()
    P.es.close()
    return P, n


def host_inputs(inputs, consts):
    f = lambda a: np.ascontiguousarray(np.asarray(a), dtype=np.float32)
    w_in = f(inputs["w_in"])
    sw = np.empty((DEPTH, D, 512), np.float32)
    for base, o in ((0, 0), (256, 256)):
        blk = w_in[:, :, base:base + 256].reshape(DEPTH, D, 4, 2, 32)
        sw[:, :, o:o + 256] = blk[:, :, :, ::-1, :].reshape(DEPTH, D, 256)
    pair = lambda v: np.ascontiguousarray(v.reshape(2, 128).T)
    smallp = np.zeros((DEPTH, 128, 96), np.float32)
    s5b = np.zeros((DEPTH, 128, 2, 8, 16), np.float32)
    s5c = np.zeros((DEPTH, 128, 2, 8, 16), np.float32)
    rowp = np.zeros((DEPTH, 1, 528), np.float32)
    for l in range(DEPTH):
        smallp[l, :, 0:8] = f(inputs["norm_w"])[l].reshape(8, 128).T
        smallp[l, :, 8:10] = pair(f(inputs["ret_norm_w"])[l])
        smallp[l, :, 10:12] = pair(f(inputs["hgrn_norm_w"])[l])
        smallp[l, :, 12:14] = pair(f(inputs["m2_norm_w"])[l])
        smallp[l, :, 14:16] = pair(np.repeat(f(inputs["m2_D"])[l], 64))
        smallp[l, :, 16:18] = pair(f(inputs["hgrn_lb_logits"])[0])
        smallp[l, :, 18:20] = pair(f(inputs["hgrn_lb_logits"])[1])
        smallp[l, :, 20:22] = pair(f(inputs["s5_D"])[l])
        smallp[l, :, 22:24] = pair(f(inputs["s5_glu_b"])[l])
        cw = f(inputs["m2_conv_w"])[l]
        smallp[l, :, 24:40] = cw.reshape(4, 4, 128).transpose(2, 1, 0).reshape(128, 16)
        smallp[l, :, 40:44] = f(inputs["m2_conv_b"])[l].reshape(4, 128).T
        pl = lambda a: np.ascontiguousarray(a.reshape(8, 2, 64).transpose(1, 2, 0).reshape(128, 8))
        smallp[l, :, 44:52] = pl(f(inputs["s5_A_re"])[l])
        smallp[l, :, 52:60] = pl(f(inputs["s5_A_im"])[l])
        smallp[l, :, 60:68] = pl(np.repeat(f(inputs["s5_log_dt"])[l][:, None], 64, axis=1))
        pb = lambda a: np.ascontiguousarray(a.reshape(8, 2, 64, 16).transpose(1, 2, 0, 3).reshape(128, 8, 16))
        s5b[l, :, 0] = pb(f(inputs["s5_B_re"])[l])
        s5b[l, :, 1] = pb(f(inputs["s5_B_im"])[l])
        pc = lambda a: np.ascontiguousarray(a.reshape(8, 2, 16, 64).transpose(1, 3, 0, 2).reshape(128, 8, 16))
        s5c[l, :, 0] = pc(f(inputs["s5_C_re"])[l])
        s5c[l, :, 1] = pc(f(inputs["s5_C_im"])[l])
        rowp[l, 0, 0:256] = f(inputs["hgrn_lb_logits"])[0]
        rowp[l, 0, 256:512] = f(inputs["hgrn_lb_logits"])[1]
        rowp[l, 0, 512:516] = f(inputs["m2_dt_bias"])[l]
        rowp[l, 0, 516:520] = f(inputs["m2_A_log"])[l]
    rowsS = np.zeros((DEPTH, 64, 4, 64), np.float32)
    rowsM = np.zeros((DEPTH, 32, 2), np.float32)
    rowsT = np.zeros((DEPTH, 1, 3840), np.float32)
    for l in range(DEPTH):
        rep = lambda v: np.tile(v.reshape(1, 4, 64), (16, 1, 1)).reshape(64, 64)
        rowsS[l, :, 0] = rep(f(inputs["ret_norm_w"])[l])
        rowsS[l, :, 1] = rep(f(inputs["hgrn_norm_w"])[l])
        rowsS[l, :, 2] = rep(f(inputs["hgrn_lb_logits"])[0])
        rowsS[l, :, 3] = rep(f(inputs["hgrn_lb_logits"])[1])
        rowsM[l] = np.tile(f(inputs["m2_D"])[l].reshape(1, 2, 2), (16, 1, 1)).reshape(32, 2)
        rowsT[l, 0, 0:2048] = f(inputs["m2_conv_w"])[l].reshape(-1)
        rowsT[l, 0, 2048:2560] = f(inputs["m2_conv_b"])[l]
        rowsT[l, 0, 2560:2816] = f(inputs["m2_norm_w"])[l]
        rowsT[l, 0, 2816:3072] = f(inputs["s5_glu_b"])[l]
    shared = dict(w_in=w_in, w_sw=sw, w_out=f(inputs["w_out"]), smallp=smallp, rowp=rowp, s5b=s5b, s5c=s5c,
                  rowsS=rowsS, rowsM=rowsM, rowsT=rowsT,
                  glu_w=f(inputs["s5_glu_w"]),
                  fin_w=f(inputs["final_norm_w"]).reshape(1, D))
    for k, v in consts.items():
        shared["c_" + k] = v
    maps = []
    xp = f(inputs["x_prompt"])
    xsm = f(inputs["x_sample"]).reshape(128, D)
    sts = {"st_ret": f(inputs["state_ret"]), "st_hgrn": f(inputs["state_hgrn"]), "st_m2": f(inputs["state_m2_ssm"]),
           "st_conv": f(inputs["state_m2_conv"]), "st_s5re": f(inputs["state_s5_re"]), "st_s5im": f(inputs["state_s5_im"])}
    for c in range(8):
        m = dict(shared)
        m["x_prompt"] = xp[c]
        m["x_sample"] = xsm[c * NS:(c + 1) * NS]
        for k_, v_ in sts.items():
            m[k_] = np.ascontiguousarray(v_[:, c * NS:(c + 1) * NS])
        maps.append(m)
    return maps


_CACHE = {}


def kernel(**inputs):
    consts = make_consts()
    if "prog" not in _CACHE:
        _CACHE["prog"] = build(consts)
    P, n = _CACHE["prog"]
    maps = host_inputs(inputs, consts)
    maps = [{k: v for k, v in m.items() if k in P.din} for m in maps]
    res = run_bass_kernel_spmd(P.nc, maps, core_ids=list(range(8)))
    R = res.results
    g = lambda name: np.stack([np.asarray(R[c][name]) for c in range(8)], 0)
    y_prompt = g("y_prompt")
    pst = lambda n: np.ascontiguousarray(np.moveaxis(g(n), 0, 1))
    sst = lambda n: np.ascontiguousarray(np.concatenate([np.asarray(R[c][n]) for c in range(8)], axis=1))
    y_sample = np.concatenate([np.asarray(R[c]["y_sample"]) for c in range(8)], 0).reshape(128, 1, D)
    return (y_prompt, y_sample,
            pst("p_ret"), pst("p_hgrn"), pst("p_s5re"), pst("p_s5im"), pst("p_m2"), pst("p_conv"),
            sst("s_ret"), sst("s_hgrn"), sst("s_s5re"), sst("s_s5im"), sst("s_m2"), sst("s_conv"))
```
